# Optimizing a Trainium2 kernel written in Bass

```python
import jax, jax.numpy as jnp
from jax import lax
import numpy as np

D_MODEL = 1024
BATCH = 4
SEQ = 4096
DEPTH = 1

HEAD_DIM = 64
ROPE_THETA = 10000.0
NORM_EPS = 1e-6
NEG = -1e30
NSA_HEADS = 12
NSA_KV_GROUPS = 3
NSA_HPG = NSA_HEADS // NSA_KV_GROUPS
CMP_BLOCK = 32
CMP_STRIDE = 16
CMP_HIDDEN = 256
SLC_BLOCK = 64
SLC_TOPK = 16
WIN_SIZE = 512
SLC_Q_BLOCK = 64
WIN_Q_BLOCK = 128
DIL_PAIRS = ((128, 1), (512, 4), (2048, 16))
DIL_GROUPS = len(DIL_PAIRS)
DIL_HPG = 4
DIL_HEADS = DIL_GROUPS * DIL_HPG
DIL_NKEYS = max(w // d + 1 for w, d in DIL_PAIRS)
DIL_Q_BLOCK = 128
D_FF = 4 * D_MODEL
Q_A_COLS = NSA_HEADS * HEAD_DIM
KV_A_COLS = NSA_KV_GROUPS * HEAD_DIM
GATE_A_COLS = 3 * NSA_HEADS
QKV_B_COLS = DIL_HEADS * HEAD_DIM
IN_SIZES = (Q_A_COLS, KV_A_COLS, KV_A_COLS, KV_A_COLS, KV_A_COLS, KV_A_COLS, KV_A_COLS,
            GATE_A_COLS, QKV_B_COLS, QKV_B_COLS, QKV_B_COLS, D_MODEL, D_MODEL)
IN_COLS = sum(IN_SIZES)
SPLIT_POINTS = tuple(int(c) for c in np.cumsum(IN_SIZES)[:-1])

kernel_name = "hybrid_nsa_dilated_gated_block"


def _rmsnorm(x, g):
    xf = x.astype(jnp.float32)
    y = xf * lax.rsqrt(jnp.mean(xf * xf, axis=-1, keepdims=True) + NORM_EPS)
    return (y * g.astype(jnp.float32)).astype(x.dtype)


def _rope(x, pos):
    half = HEAD_DIM // 2
    inv_freq = jnp.power(ROPE_THETA, -jnp.arange(half, dtype=jnp.float32) / half)
    ang = jnp.asarray(pos, jnp.float32)[:, None] * inv_freq[None, :]
    shape = (1, ang.shape[0]) + (1,) * (x.ndim - 3) + (half,)
    cos = jnp.cos(ang).reshape(shape)
    sin = jnp.sin(ang).reshape(shape)
    xf = x.astype(jnp.float32)
    x1, x2 = xf[..., :half], xf[..., half:]
    return jnp.concatenate([x1 * cos - x2 * sin, x2 * cos + x1 * sin], axis=-1).astype(x.dtype)


def _masked_softmax(s, mask):
    s = jnp.where(mask, s.astype(jnp.float32), NEG)
    return jax.nn.softmax(s, axis=-1) * mask


def _compress(kv, pos_emb, w1, w2):
    B, S, G, dh = kv.shape
    n_cmp = (S - CMP_BLOCK) // CMP_STRIDE + 1
    idx = np.arange(n_cmp)[:, None] * CMP_STRIDE + np.arange(CMP_BLOCK)[None, :]
    blocks = kv[:, idx] + pos_emb[None, None, :, None, :]
    blocks = jnp.transpose(blocks, (0, 1, 3, 2, 4)).reshape(B, n_cmp, G, CMP_BLOCK * dh)
    return jax.nn.gelu(blocks @ w1) @ w2


def _cmp_to_slc_overlap(n_cmp, n_slc):
    cs = np.arange(n_cmp)[:, None] * CMP_STRIDE
    ss = np.arange(n_slc)[None, :] * SLC_BLOCK
    ov = np.minimum(cs + CMP_BLOCK, ss + SLC_BLOCK) - np.maximum(cs, ss)
    return (np.clip(ov, 0, None) / CMP_BLOCK).astype(np.float32)


def _nsa_mixer(q, k_cmp, v_cmp, k_slc, v_slc, k_win, v_win, gate,
               k_norm_cmp, cmp_k_pos, cmp_k_w1, cmp_k_w2, cmp_v_pos, cmp_v_w1, cmp_v_w2):
    B, S = q.shape[0], q.shape[1]
    G, R, dh = NSA_KV_GROUPS, NSA_HPG, HEAD_DIM
    scale = dh ** -0.5
    t = np.arange(S)

    kc = _compress(k_cmp, cmp_k_pos, cmp_k_w1, cmp_k_w2)
    vc = _compress(v_cmp, cmp_v_pos, cmp_v_w1, cmp_v_w2)
    n_cmp = kc.shape[1]
    c_end = np.arange(n_cmp) * CMP_STRIDE + CMP_BLOCK - 1
    kc = _rope(_rmsnorm(kc, k_norm_cmp), c_end)
    s_cmp = jnp.einsum("bsgrd,bcgd->bgrsc", q, kc) * scale
    p_cmp = _masked_softmax(s_cmp, c_end[None, :] <= t[:, None])
    o_cmp = jnp.einsum("bgrsc,bcgd->bsgrd", p_cmp.astype(vc.dtype), vc)

    n_slc = S // SLC_BLOCK
    top_n = min(SLC_TOPK, n_slc)
    overlap = jnp.asarray(_cmp_to_slc_overlap(n_cmp, n_slc))
    imp = jnp.einsum("bgrsc,cn->bgsn", p_cmp, overlap)
    blk = np.arange(n_slc)[None, :]
    cur = (t // SLC_BLOCK)[:, None]
    forced = (blk == 0) | (blk == cur) | (blk == cur - 1)
    score = jnp.where(forced, jnp.inf, jnp.where(blk <= cur, imp, -jnp.inf))
    top_score, top_idx = lax.top_k(score, top_n)
    top_ok = top_score > -jnp.inf

    kb = jnp.transpose(k_slc.reshape(B, n_slc, SLC_BLOCK, G, dh), (0, 3, 1, 2, 4))
    vb = jnp.transpose(v_slc.reshape(B, n_slc, SLC_BLOCK, G, dh), (0, 3, 1, 2, 4))
    gather_blocks = jax.vmap(jax.vmap(lambda kk, ii: kk[ii]))
    nqs = S // SLC_Q_BLOCK

    def slc_block(xs):
        qc, idx, ok, tq = xs
        ks = gather_blocks(kb, idx)
        vs = gather_blocks(vb, idx)
        s = jnp.einsum("bqgrd,bgqnkd->bgrqnk", qc, ks) * scale
        kpos = idx[..., None] * SLC_BLOCK + np.arange(SLC_BLOCK)
        mask = ok[..., None] & (kpos <= tq[:, None, None])
        s = s.reshape(B, G, R, SLC_Q_BLOCK, top_n * SLC_BLOCK)
        mask = mask.reshape(B, G, 1, SLC_Q_BLOCK, top_n * SLC_BLOCK)
        p = _masked_softmax(s, mask)
        vs = vs.reshape(B, G, SLC_Q_BLOCK, top_n * SLC_BLOCK, dh)
        return jnp.einsum("bgrqj,bgqjd->bqgrd", p.astype(vs.dtype), vs)

    qs = jnp.moveaxis(q.reshape(B, nqs, SLC_Q_BLOCK, G, R, dh), 1, 0)
    idx_s = jnp.moveaxis(top_idx.reshape(B, G, nqs, SLC_Q_BLOCK, top_n), 2, 0)
    ok_s = jnp.moveaxis(top_ok.reshape(B, G, nqs, SLC_Q_BLOCK, top_n), 2, 0)
    tq_s = jnp.arange(S, dtype=jnp.int32).reshape(nqs, SLC_Q_BLOCK)
    o_slc = lax.map(slc_block, (qs, idx_s, ok_s, tq_s))
    o_slc = jnp.moveaxis(o_slc, 0, 1).reshape(B, S, G, R, dh)

    span = WIN_SIZE + WIN_Q_BLOCK
    kw_pad = jnp.pad(k_win, ((0, 0), (WIN_SIZE, 0), (0, 0), (0, 0)))
    vw_pad = jnp.pad(v_win, ((0, 0), (WIN_SIZE, 0), (0, 0), (0, 0)))
    nqw = S // WIN_Q_BLOCK

    def win_block(xs):
        qc, start = xs
        ks = lax.dynamic_slice_in_dim(kw_pad, start, span, axis=1)
        vs = lax.dynamic_slice_in_dim(vw_pad, start, span, axis=1)
        s = jnp.einsum("bqgrd,bkgd->bgrqk", qc, ks) * scale
        tq = start + jnp.arange(WIN_Q_BLOCK)
        kp = start - WIN_SIZE + jnp.arange(span)
        diff = tq[:, None] - kp[None, :]
        mask = (diff >= 0) & (diff < WIN_SIZE) & (kp[None, :] >= 0)
        p = _masked_softmax(s, mask)
        return jnp.einsum("bgrqk,bkgd->bqgrd", p.astype(vs.dtype), vs)

    qw = jnp.moveaxis(q.reshape(B, nqw, WIN_Q_BLOCK, G, R, dh), 1, 0)
    starts = jnp.arange(nqw, dtype=jnp.int32) * WIN_Q_BLOCK
    o_win = lax.map(win_block, (qw, starts))
    o_win = jnp.moveaxis(o_win, 0, 1).reshape(B, S, G, R, dh)

    y = gate[..., 0:1] * o_cmp + gate[..., 1:2] * o_slc + gate[..., 2:3] * o_win
    return y.reshape(B, S, G * R * dh)


def _dilated_mixer(q, k, v):
    B, S = q.shape[0], q.shape[1]
    dh = HEAD_DIM
    scale = dh ** -0.5
    dil = np.array([d for _, d in DIL_PAIRS])
    win = np.array([w for w, _ in DIL_PAIRS])
    j = np.arange(DIL_NKEYS)
    in_window = (j[None, :] * dil[:, None]) <= win[:, None]
    kg = jnp.transpose(k, (2, 0, 1, 3, 4))
    vg = jnp.transpose(v, (2, 0, 1, 3, 4))
    gather_stride = jax.vmap(lambda kk, ii: kk[:, ii])
    nq = S // DIL_Q_BLOCK

    def dil_block(xs):
        qc, tq = xs
        kpos = tq[None, :, None] - j[None, None, :] * dil[:, None, None]
        mask = (kpos >= 0) & in_window[:, None, :]
        idx = jnp.maximum(kpos, 0)
        ks = gather_stride(kg, idx)
        vs = gather_stride(vg, idx)
        s = jnp.einsum("gbqhd,gbqjhd->gbhqj", qc, ks).astype(jnp.float32) * scale
        s = jnp.where(mask[:, None, None], s, NEG)
        m = jnp.max(s, axis=-1, keepdims=True)
        e = jnp.exp(s - m)
        den = jnp.sum(e, axis=-1, keepdims=True)
        o = jnp.einsum("gbhqj,gbqjhd->gbqhd", (e / den).astype(vs.dtype), vs)
        lse = (m + jnp.log(den))[..., 0]
        w = jax.nn.softmax(lse, axis=0)
        return jnp.einsum("gbhq,gbqhd->bqhd", w.astype(o.dtype), o)

    qd = jnp.moveaxis(jnp.transpose(q, (2, 0, 1, 3, 4)).reshape(DIL_GROUPS, B, nq, DIL_Q_BLOCK, DIL_HPG, dh), 2, 0)
    tq_d = jnp.arange(S, dtype=jnp.int32).reshape(nq, DIL_Q_BLOCK)
    o = lax.map(dil_block, (qd, tq_d))
    return jnp.moveaxis(o, 0, 1).reshape(B, S, DIL_HPG * dh)


def setup_inputs(seed: int = 0) -> dict:
    key = jax.random.key(seed)
    ks = jax.random.split(key, 21)
    L = DEPTH

    def nrm(k, shape, scale):
        return scale * jax.random.normal(k, shape, jnp.float32)

    def gain(k, n):
        return 1.0 + 0.02 * jax.random.normal(k, (L, n), jnp.float32)

    return {
        "x": jax.random.normal(ks[0], (BATCH, SEQ, D_MODEL), jnp.float32),
        "norm1_g": gain(ks[1], D_MODEL),
        "w_in": nrm(ks[2], (L, D_MODEL, IN_COLS), D_MODEL ** -0.5),
        "q_norm_a": gain(ks[3], HEAD_DIM),
        "k_norm_cmp": gain(ks[4], HEAD_DIM),
        "k_norm_slc": gain(ks[5], HEAD_DIM),
        "k_norm_win": gain(ks[6], HEAD_DIM),
        "cmp_k_pos": nrm(ks[7], (L, CMP_BLOCK, HEAD_DIM), 0.2),
        "cmp_k_w1": nrm(ks[8], (L, CMP_BLOCK * HEAD_DIM, CMP_HIDDEN), (CMP_BLOCK * HEAD_DIM) ** -0.5),
        "cmp_k_w2": nrm(ks[9], (L, CMP_HIDDEN, HEAD_DIM), CMP_HIDDEN ** -0.5),
        "cmp_v_pos": nrm(ks[10], (L, CMP_BLOCK, HEAD_DIM), 0.2),
        "cmp_v_w1": nrm(ks[11], (L, CMP_BLOCK * HEAD_DIM, CMP_HIDDEN), (CMP_BLOCK * HEAD_DIM) ** -0.5),
        "cmp_v_w2": nrm(ks[12], (L, CMP_HIDDEN, HEAD_DIM), CMP_HIDDEN ** -0.5),
        "q_norm_b": gain(ks[13], HEAD_DIM),
        "k_norm_b": gain(ks[14], HEAD_DIM),
        "w_o_a": nrm(ks[15], (L, Q_A_COLS, D_MODEL), Q_A_COLS ** -0.5),
        "w_o_b": nrm(ks[16], (L, DIL_HPG * HEAD_DIM, D_MODEL), (DIL_HPG * HEAD_DIM) ** -0.5),
        "w_out": nrm(ks[17], (L, D_MODEL, D_MODEL), D_MODEL ** -0.5),
        "norm2_g": gain(ks[18], D_MODEL),
        "w_up": nrm(ks[19], (L, D_MODEL, D_FF), D_MODEL ** -0.5),
        "w_down": nrm(ks[20], (L, D_FF, D_MODEL), D_FF ** -0.5),
    }


def reference(x, norm1_g, w_in, q_norm_a, k_norm_cmp, k_norm_slc, k_norm_win,
              cmp_k_pos, cmp_k_w1, cmp_k_w2, cmp_v_pos, cmp_v_w1, cmp_v_w2,
              q_norm_b, k_norm_b, w_o_a, w_o_b, w_out, norm2_g, w_up, w_down):
    B, S = x.shape[0], x.shape[1]
    G, R, dh = NSA_KV_GROUPS, NSA_HPG, HEAD_DIM
    pos = np.arange(S)
    for i in range(DEPTH):
        h = _rmsnorm(x, norm1_g[i])
        proj = h @ w_in[i]
        (q_a, k_c, v_c, k_s, v_s, k_w, v_w, g_nsa,
         q_b, k_b, v_b, g_ma, g_mb) = jnp.split(proj, SPLIT_POINTS, axis=-1)

        q_a = _rope(_rmsnorm(q_a.reshape(B, S, NSA_HEADS, dh), q_norm_a[i]), pos).reshape(B, S, G, R, dh)
        k_c = k_c.reshape(B, S, G, dh)
        v_c = v_c.reshape(B, S, G, dh)
        k_s = _rope(_rmsnorm(k_s.reshape(B, S, G, dh), k_norm_slc[i]), pos)
        v_s = v_s.reshape(B, S, G, dh)
        k_w = _rope(_rmsnorm(k_w.reshape(B, S, G, dh), k_norm_win[i]), pos)
        v_w = v_w.reshape(B, S, G, dh)
        g_nsa = jax.nn.sigmoid(g_nsa).reshape(B, S, G, R, 3)
        y_a = _nsa_mixer(q_a, k_c, v_c, k_s, v_s, k_w, v_w, g_nsa,
                         k_norm_cmp[i], cmp_k_pos[i], cmp_k_w1[i], cmp_k_w2[i],
                         cmp_v_pos[i], cmp_v_w1[i], cmp_v_w2[i])

        q_b = _rope(_rmsnorm(q_b.reshape(B, S, DIL_HEADS, dh), q_norm_b[i]), pos)
        k_b = _rope(_rmsnorm(k_b.reshape(B, S, DIL_HEADS, dh), k_norm_b[i]), pos)
        y_b = _dilated_mixer(q_b.reshape(B, S, DIL_GROUPS, DIL_HPG, dh),
                             k_b.reshape(B, S, DIL_GROUPS, DIL_HPG, dh),
                             v_b.reshape(B, S, DIL_GROUPS, DIL_HPG, dh))

        mixed = jax.nn.sigmoid(g_ma) * (y_a @ w_o_a[i]) + jax.nn.sigmoid(g_mb) * (y_b @ w_o_b[i])
        x = x + mixed @ w_out[i]

        h2 = _rmsnorm(x, norm2_g[i])
        x = x + jnp.square(jax.nn.relu(h2 @ w_up[i])) @ w_down[i]
    return x
```

```python
import contextlib
import math
import numpy as np
import ml_dtypes
import concourse.bass as bass
import concourse.mybir as mybir
from concourse.bass_utils import run_bass_kernel_spmd

F32 = mybir.dt.float32
BF16 = mybir.dt.bfloat16
AF = mybir.ActivationFunctionType
ALU = mybir.AluOpType
AX = mybir.AxisListType

S = 4096
D = 1024
NT = 32
NOWN = 16
EPS = 1e-6
SCALE = 0.125
QT = ([0, 3, 4, 7], [1, 2, 5, 6])
KV_COLS = 2688
Q_COLS = 3620
NEGSEL = -2048.0
L_C, L_W, L_D0, L_D1, L_D2 = 1408, 1920, 1536, 1920, 3456
SBUF_BYTES = 206 * 1024


class Res:
    __slots__ = ("name", "lw", "rd")

    def __init__(self, name=""):
        self.name = name
        self.lw = None
        self.rd = []


class Tile:
    def __init__(self, ap, name):
        self.ap = ap
        self.res = Res(name)

    def __getitem__(self, k):
        return self.ap[k]


class FW:
    ENG = ["pe", "act", "dve", "pool", "sp"]

    def __init__(self, nc, es):
        self.nc = nc
        self.es = es
        self.q = {e: [] for e in self.ENG}
        self.cnt = {e: 0 for e in self.ENG}
        self.sem = {e: es.enter_context(nc.semaphore("s_" + e)) for e in ["pe", "act", "dve", "pool"]}
        self.waited = {e: {} for e in self.ENG}
        self.dsem = {}

    @staticmethod
    def _r(x):
        return x.res if isinstance(x, Tile) else x

    def _deps(self, reads, writes):
        deps = []
        for r in reads:
            if r.lw is not None:
                deps.append(r.lw)
        for w in writes:
            deps.extend(w.rd)
            if w.lw is not None:
                deps.append(w.lw)
        return deps

    def _emit_waits(self, eng, deps):
        need = {}
        for (k, v) in deps:
            if k == eng and eng == "pe":
                continue
            if v > need.get(k, 0):
                need[k] = v
        for k, v in need.items():
            if self.waited[eng].get(k, 0) >= v:
                continue
            self.waited[eng][k] = v
            semh = self.sem[k] if k in self.sem else self.dsem[k][0]
            self.q[eng].append(("wait", semh, v))

    def _mark(self, key, seq, reads, writes):
        for r in reads:
            r.rd.append((key, seq))
            if len(r.rd) > 48:
                mx = {}
                for (k, v) in r.rd:
                    if v > mx.get(k, 0):
                        mx[k] = v
                r.rd = list(mx.items())
        for w in writes:
            w.lw = (key, seq)
            w.rd = []

    def op(self, eng, fn, reads=(), writes=()):
        return self.group(eng, [fn], reads, writes)

    def group(self, eng, fns, reads=(), writes=()):
        reads = [self._r(x) for x in reads]
        writes = [self._r(x) for x in writes]
        self._emit_waits(eng, self._deps(reads, writes))
        self.cnt[eng] += 1
        seq = self.cnt[eng]
        for f in fns[:-1]:
            self.q[eng].append(("op", f, None))
        self.q[eng].append(("op", fns[-1], self.sem[eng]))
        self._mark(eng, seq, reads, writes)
        return seq

    def dma(self, queue, fn, key, reads=(), writes=()):
        reads = [self._r(x) for x in reads]
        writes = [self._r(x) for x in writes]
        if key not in self.dsem:
            self.dsem[key] = [self.es.enter_context(self.nc.semaphore("d_" + key)), 0]
        self._emit_waits(queue, self._deps(reads, writes))
        ent = self.dsem[key]
        ent[1] += 16
        self.q[queue].append(("dma", fn, ent[0]))
        self._mark(key, ent[1], reads, writes)

    def barrier(self):
        allv = [(e, self.cnt[e]) for e in self.sem if self.cnt[e] > 0]
        allv += [(k, v[1]) for k, v in self.dsem.items() if v[1] > 0]
        for e in self.ENG:
            self._emit_waits(e, [(k, v) for (k, v) in allv if k != e])
            if e in self.sem and e != "pe" and self.cnt[e] > 0:
                self._emit_waits(e, [(e, self.cnt[e])])

    def final_wait(self, eng, ress):
        deps = []
        for r in ress:
            r = self._r(r)
            if r.lw is not None:
                deps.append(r.lw)
        self._emit_waits(eng, deps)

    def replay(self):
        nc = self.nc
        with nc.Block() as block:
            def run(engname):
                def f(e):
                    for it in self.q[engname]:
                        if it[0] == "wait":
                            e.wait_ge(it[1], it[2])
                        elif it[0] == "op":
                            ins = it[1](e)
                            if it[2] is not None:
                                ins.then_inc(it[2], 1)
                        else:
                            it[1](e).then_inc(it[2], 16)
                return f
            block.tensor(run("pe"))
            block.scalar(run("act"))
            block.vector(run("dve"))
            block.gpsimd(run("pool"))
            block.sync(run("sp"))


class Arena:
    def __init__(self, big, nbytes):
        self.big = big
        self.cap = nbytes
        self.top = 0
        self.hi = 0

    def t(self, name, shape, dt):
        esz = 2 if dt == BF16 else 4
        n = int(np.prod(shape))
        nb = (n * esz + 31) // 32 * 32
        off = self.top
        assert off + nb <= self.cap, (name, off, nb, self.cap)
        self.top = off + nb
        self.hi = max(self.hi, self.top)
        ap = self.big[:, off // 4:(off + nb) // 4]
        if dt == BF16:
            ap = ap.bitcast(BF16)
        ap = ap[:, 0:n]
        if len(shape) == 2:
            ap = ap.rearrange("p (a b) -> p a b", a=shape[0])
        elif len(shape) == 3:
            ap = ap.rearrange("p (a b c) -> p a b c", a=shape[0], b=shape[1])
        elif len(shape) == 4:
            ap = ap.rearrange("p (a b c d) -> p a b c d", a=shape[0], b=shape[1], c=shape[2])
        return Tile(ap, name)


def bc(ap, shape):
    return ap.to_broadcast(list(shape))


def MM(out, lhsT, rhs, start=True, stop=True):
    return lambda e: e.matmul(out, lhsT=lhsT, rhs=rhs, start=start, stop=stop)


def TT(out, in0, in1, op):
    return lambda e: e.tensor_tensor(out=out, in0=in0, in1=in1, op=op)


def TS(out, in0, s1, op0, s2=None, op1=None):
    if op1 is None:
        return lambda e: e.tensor_scalar(out=out, in0=in0, scalar1=s1, scalar2=None, op0=op0)
    return lambda e: e.tensor_scalar(out=out, in0=in0, scalar1=s1, scalar2=s2, op0=op0, op1=op1)


def ACTF(out, in_, func, scale=1.0, bias=None, accum=None):
    kw = {}
    if bias is not None:
        kw["bias"] = bias
    if accum is not None:
        kw["accum_out"] = accum
    return lambda e: e.activation(out=out, in_=in_, func=func, scale=scale, **kw)


def ACP(out, in_):
    return lambda e: e.copy(out=out, in_=in_)


def TC(out, in_):
    return lambda e: e.tensor_copy(out=out, in_=in_)


def TRN(out, in_, ident):
    return lambda e: e.transpose(out=out, in_=in_, identity=ident)


def DMA(out, in_):
    return lambda e: e.dma_start(out=out, in_=in_)


def MSET(ap, v):
    return lambda e: e.memset(ap, v)


def RECIP(out, in_):
    return lambda e: e.reciprocal(out=out, in_=in_)


def build(upto="all", debug=False):
    nc = bass.Bass("TRN2", target_bir_lowering=False)
    dbg_kind = "ExternalOutput" if debug else "Internal"

    def din(name, shape, dt=F32):
        return nc.dram_tensor(name, list(shape), dt, kind="ExternalInput").ap()

    def dscr(name, shape, dt=BF16):
        return nc.dram_tensor(name, list(shape), dt, kind=dbg_kind).ap()

    x_all = din("x_all", [S, D])
    x_own = din("x_own", [2048, D])
    wkv = din("wkv", [128, 8, KV_COLS])
    wq = din("wq", [128, 8, Q_COLS])
    g1 = din("g1", [128, 8])
    g2 = din("g2", [128, 8])
    gcolK = din("gcolK", [1, 1152])
    gcolQ = din("gcolQ", [1, 1536])
    gcmp = din("gcmp", [1, 64])
    cs_all = din("cs_all", [128, NT, 32])
    sn_all = din("sn_all", [128, NT, 32])
    cs_own = din("cs_own", [128, NOWN, 32])
    sn_own = din("sn_own", [128, NOWN, 32])
    cs_cmp = din("cs_cmp", [128, 2, 32])
    sn_cmp = din("sn_cmp", [128, 2, 32])
    ident = din("ident", [128, 128])
    expand = din("expand", [64, S], BF16)
    ov1 = din("ov1", [128, 2, 65], BF16)
    cmask = din("cmask", [128, 4, 2, 512], BF16)
    selA = din("selA", [128, 4, 4, 64])
    selB = din("selB", [128, 4, 4, 64])
    tbc = din("tbc", [128, 2, L_C], BF16)
    tbw = din("tbw", [128, 2, L_W], BF16)
    tbd0 = din("tbd0", [128, 2, L_D0], BF16)
    tbd1 = din("tbd1", [128, 2, L_D1], BF16)
    tbd2 = din("tbd2", [128, 2, L_D2], BF16)
    w1k = din("w1k", [128, 16, 256])
    w2k = din("w2k", [128, 2, 64])
    posk = din("posk", [128, 16])
    w1v = din("w1v", [128, 16, 256])
    w2v = din("w2v", [128, 2, 64])
    posv = din("posv", [128, 16])
    woa = din("woa", [128, 6, 1024])
    wob = din("wob", [128, 2, 1024])
    wout = din("wout", [128, 8, 1024])
    wup = din("wup", [128, 8, 4096])
    wdown = din("wdown", [128, 32, 1024])
    out = nc.dram_tensor("out", [2048, D], F32, kind="ExternalOutput").ap()

    KsT_scr = dscr("KsT_scr", [64, 3, S])
    KwT_scr = dscr("KwT_scr", [64, 3, S])
    KbT_scr = dscr("KbT_scr", [128, 6, S])
    V_scr = dscr("V_scr", [129, NT, 1554])
    QaT_scr = dscr("QaT_scr", [64, 12, 2048])
    QbT_scr = dscr("QbT_scr", [128, 6, 2048])
    gT_scr = dscr("gT_scr", [128, 16, 2048])
    yT_scr = dscr("yT_scr", [128, 8, 2048])
    h2T_scr = dscr("h2T_scr", [128, 8, 2048])
    dbg_gn = dscr("dbg_gn", [128, NOWN, 36], F32) if debug else None
    dbg_kc = dscr("dbg_kc", [128, 3, 256], BF16) if debug else None
    dbg_vc = dscr("dbg_vc", [128, 2, 3, 65], BF16) if debug else None

    with contextlib.ExitStack() as es:
        fw = FW(nc, es)
        big = es.enter_context(nc.sbuf_tensor("big", [128, SBUF_BYTES // 4], F32))
        psum = es.enter_context(nc.psum_tensor("psum", [128, 4096], F32))
        A = Arena(big, SBUF_BYTES)

        def bank(b, n=512, dt=F32, name="ps"):
            ap = psum[:, 512 * b:512 * b + 512]
            if dt == BF16:
                ap = ap.bitcast(BF16)
            return Tile(ap[:, 0:n], f"{name}{b}")

        idf = A.t("idf", [128], F32)
        idb = A.t("idb", [128], BF16)
        gn = A.t("gn", [NOWN, 36], F32)
        fw.dma("sp", lambda e: e.dma_start(out=idf[:], in_=ident), "idf", writes=[idf])
        fw.op("dve", lambda e: e.tensor_copy(out=idb[:], in_=idf[:]), reads=[idf], writes=[idb])
        stage_off = A.top
        stage = [A.t(f"wstage{i}", [2048], F32) for i in range(4)]
        stage_ctr = [0]

        def load_weight(dst_tile, dst_ap_fn, src_ap_fn, nchunks, width, gain=None, eng_cycle=("dve", "pool"), as3=None):
            for c in range(nchunks):
                i = stage_ctr[0]
                stage_ctr[0] += 1
                st = stage[i % 4]
                assert width <= 2048
                sview = st[:, 0:width] if as3 is None else st[:, 0:width].rearrange("p (a b) -> p a b", a=as3)
                fw.dma("sp", DMA(sview, src_ap_fn(c)), f"wstage{i % 4}", writes=[st])
                eng = ("dve", "act")[i % 2]
                if gain is None:
                    fw.op(eng, (TC if eng == "dve" else ACP)(dst_ap_fn(c), sview), reads=[st], writes=[dst_tile])
                elif eng == "dve":
                    fw.op(eng, TS(dst_ap_fn(c), sview, gain[:, c:c + 1], ALU.mult), reads=[st, gain], writes=[dst_tile])
                else:
                    fw.op(eng, ACTF(dst_ap_fn(c), sview, AF.Copy, scale=gain[:, c:c + 1]), reads=[st, gain], writes=[dst_tile])

        base_mark = A.top

        def proj_phase(which):
            A.top = base_mark
            isA = which == "A"
            ntile = NT if isA else NOWN
            ncols = KV_COLS if isA else Q_COLS
            nrope = 18 if isA else 24
            xsrc = x_all if isA else x_own
            W = A.t("W", [8, ncols], BF16)
            g1t = A.t("g1t", [8], F32)
            fw.dma("sp", lambda e: e.dma_start(out=g1t[:], in_=g1), "g1t", writes=[g1t])
            gcol = A.t("gcol", [nrope, 64], F32)
            gsrc = gcolK if isA else gcolQ
            fw.dma("sp", lambda e: e.dma_start(out=gcol[:].rearrange("p a b -> p (a b)"),
                                               in_=gsrc.partition_broadcast(128).rearrange("p a b -> p (a b)")),
                   "gcol", writes=[gcol])
            cs = A.t("cs", [ntile, 32], F32)
            sn = A.t("sn", [ntile, 32], F32)
            fw.dma("sp", lambda e: e.dma_start(out=cs[:], in_=cs_all if isA else cs_own), "cs", writes=[cs])
            fw.dma("sp", lambda e: e.dma_start(out=sn[:], in_=sn_all if isA else sn_own), "sn", writes=[sn])
            wsrc = wkv if isA else wq
            csplit = 1152 if isA else 1572
            W_lo, W_hi = Res("W_lo"), Res("W_hi")

            xin = [A.t(f"xin{i}", [D], F32) for i in range(2)]
            junk = A.t("junk", [D], BF16)
            hb = [A.t(f"hb{i}", [D], BF16) for i in range(2)]
            stat = [A.t(f"stat{i}", [4], F32) for i in range(2)]
            sq = [A.t(f"sq{i}", [512], F32) for i in range(2)]
            ssh = [A.t(f"ssh{i}", [nrope], F32) for i in range(2)]
            kn = [A.t(f"kn{i}", [nrope, 64], F32) for i in range(2)]
            tr = [A.t(f"tr{i}", [4, nrope, 32], F32) for i in range(2)]
            kr = [A.t(f"kr{i}", [nrope, 64], BF16) for i in range(2)]
            pTh = bank(7, 1024, BF16, "pTh")
            pTk = [bank(5, 1024, BF16, "pTkA"), bank(6, 1024, BF16, "pTkB")]
            nmm = 5
            pmm = [bank(b, 512, F32, "pmm") for b in range(nmm)]
            mmctr = [0]
            if isA:
                hT = [A.t(f"hT{i}", [8, 128], BF16) for i in range(2)]
                KTst = [A.t(f"KTst{i}", [12, 512], BF16) for i in range(2)]
                Vst = [A.t(f"Vst{i}", [4, 1554], BF16) for i in range(2)]
                for v_ in Vst:
                    fw.op("pool", MSET(v_[:], 1.0), writes=[v_])
            else:
                hT = [A.t(f"hTs{i}", [8, 512], BF16) for i in range(2)]
                QTst = [A.t(f"QTst{i}", [12, 512], BF16) for i in range(2)]
                gst = [A.t(f"gst{i}", [512], BF16) for i in range(4)]

            def s_load(i):
                t = xin[i % 2]
                fw.dma("sp", lambda e: e.dma_start(out=t[:], in_=xsrc[i * 128:(i + 1) * 128, :]), f"xin{i % 2}", writes=[t])

            def s_norm(i):
                p = i % 2
                x, st, h = xin[p], stat[p], hb[p]
                fw.op("act", lambda e: e.activation(out=junk[:], in_=x[:], func=AF.Square, accum_out=st[:, 0:1]),
                      reads=[x], writes=[junk, st])
                fw.op("act", lambda e: e.activation(out=st[:, 1:2], in_=st[:, 0:1], func=AF.Sqrt, scale=1.0 / D, bias=epsb[:, 0:1]),
                      reads=[st, epsb], writes=[st])
                fw.op("dve", lambda e: e.reciprocal(out=st[:, 2:3], in_=st[:, 1:2]), reads=[st], writes=[st])
                fw.op("dve", lambda e: e.tensor_scalar(out=h[:], in0=x[:], scalar1=st[:, 2:3], scalar2=None, op0=ALU.mult),
                      reads=[x, st], writes=[h])

            def s_transp_h(i):
                p = i % 2
                h = hb[p]
                fw.group("pe", [lambda e, kc=kc: e.transpose(out=pTh[:, kc * 128:(kc + 1) * 128], in_=h[:, kc * 128:(kc + 1) * 128],
                                                            identity=idb[:]) for kc in range(8)],
                         reads=[h, idb], writes=[pTh])
                if isA:
                    dst = hT[p]
                    fw.op("act", lambda e: e.copy(out=dst[:].rearrange("p a b -> p (a b)"), in_=pTh[:]), reads=[pTh], writes=[dst])
                else:
                    dst = hT[(i // 4) % 2]
                    sub = i % 4
                    fw.op("act", lambda e: e.copy(out=dst[:, :, sub * 128:(sub + 1) * 128],
                                                  in_=pTh[:].rearrange("p (a b) -> p a b", a=8)), reads=[pTh], writes=[dst])

            def lhs_of(i, kc):
                if isA:
                    return hT[i % 2][:, kc, :]
                return hT[(i // 4) % 2][:, kc, (i % 4) * 128:(i % 4 + 1) * 128]

            def hT_of(i):
                return hT[i % 2] if isA else hT[(i // 4) % 2]

            def mm_block(i, c0, w):
                pm = pmm[mmctr[0] % nmm]
                mmctr[0] += 1
                fw.group("pe", [lambda e, kc=kc: e.matmul(pm[:, 0:w], lhsT=lhs_of(i, kc), rhs=W[:, kc, c0:c0 + w],
                                                          start=(kc == 0), stop=(kc == 7)) for kc in range(8)],
                         reads=[hT_of(i), W_lo if c0 + w <= csplit else W_hi], writes=[pm])
                return pm

            def rope_epilogue(i, blocks):
                p = i % 2
                k_n, s_h, t_r, k_r, s_q = kn[p], ssh[p], tr[p], kr[p], sq
                for bi, (pm, h0, nh) in enumerate(blocks):
                    sqt = s_q[bi % 2]
                    w = nh * 64
                    fw.op("act", lambda e, pm=pm, sqt=sqt, w=w: e.activation(out=sqt[:, 0:w], in_=pm[:, 0:w], func=AF.Square),
                          reads=[pm], writes=[sqt])
                    fw.op("dve", lambda e, sqt=sqt, h0=h0, nh=nh, w=w: e.tensor_reduce(
                        out=s_h[:, h0:h0 + nh], in_=sqt[:, 0:w].rearrange("p (a b) -> p a b", a=nh), axis=AX.X, op=ALU.add),
                        reads=[sqt], writes=[s_h])
                fw.op("act", lambda e: e.activation(out=s_h[:], in_=s_h[:], func=AF.Sqrt, scale=1.0 / 64, bias=epsb[:, 0:1]),
                      reads=[s_h, epsb], writes=[s_h])
                fw.op("dve", lambda e: e.reciprocal(out=s_h[:], in_=s_h[:]), reads=[s_h], writes=[s_h])
                for (pm, h0, nh) in blocks:
                    w = nh * 64
                    fw.op("dve", lambda e, pm=pm, h0=h0, nh=nh, w=w: e.tensor_tensor(
                        out=k_n[:, h0:h0 + nh, :], in0=pm[:, 0:w].rearrange("p (a b) -> p a b", a=nh),
                        in1=bc(s_h[:, h0:h0 + nh].unsqueeze(2), [128, nh, 64]), op=ALU.mult),
                        reads=[pm, s_h], writes=[k_n])
                fw.op("dve", lambda e: e.tensor_tensor(out=k_n[:], in0=k_n[:], in1=gcol[:], op=ALU.mult),
                      reads=[k_n, gcol], writes=[k_n])
                cosb = bc(cs[:, i, :].unsqueeze(1), [128, nrope, 32])
                sinb = bc(sn[:, i, :].unsqueeze(1), [128, nrope, 32])
                x1 = k_n[:, :, 0:32]
                x2 = k_n[:, :, 32:64]
                fw.op("dve", lambda e: e.tensor_tensor(out=t_r[:, 0], in0=x1, in1=cosb, op=ALU.mult), reads=[k_n, cs], writes=[t_r])
                fw.op("dve", lambda e: e.tensor_tensor(out=t_r[:, 1], in0=x2, in1=sinb, op=ALU.mult), reads=[k_n, sn], writes=[t_r])
                fw.op("pool", lambda e: e.tensor_tensor(out=t_r[:, 2], in0=x2, in1=cosb, op=ALU.mult), reads=[k_n, cs], writes=[t_r])
                fw.op("pool", lambda e: e.tensor_tensor(out=t_r[:, 3], in0=x1, in1=sinb, op=ALU.mult), reads=[k_n, sn], writes=[t_r])
                fw.op("pool", lambda e: e.tensor_tensor(out=k_r[:, :, 0:32], in0=t_r[:, 0], in1=t_r[:, 1], op=ALU.subtract),
                      reads=[t_r], writes=[k_r])
                fw.op("pool", lambda e: e.tensor_tensor(out=k_r[:, :, 32:64], in0=t_r[:, 2], in1=t_r[:, 3], op=ALU.add),
                      reads=[t_r], writes=[k_r])

            def s_mm_A(i):
                blocks = []
                for (c0, w, h0, nh) in [(0, 512, 0, 8), (512, 512, 8, 8), (1024, 128, 16, 2)]:
                    blocks.append((mm_block(i, c0, w), h0, nh))
                rope_epilogue(i, blocks)
                vs = Vst[(i // 4) % 2]
                t4 = i % 4
                pm = mm_block(i, 1152, 512)
                fw.op("act", ACP(vs[:, t4, 0:520].rearrange("p (h c) -> p h c", c=65)[:, :, 0:64], pm[:].rearrange("p (h c) -> p h c", c=64)),
                      reads=[pm], writes=[vs])
                pm = mm_block(i, 1664, 512)
                fw.op("dve", TC(vs[:, t4, 520:1040].rearrange("p (h c) -> p h c", c=65)[:, :, 0:64], pm[:].rearrange("p (h c) -> p h c", c=64)),
                      reads=[pm], writes=[vs])
                pm = mm_block(i, 2176, 512)
                fw.op("act", ACP(vs[:, t4, 1040:1170].rearrange("p (h c) -> p h c", c=65)[:, :, 0:64], pm[:, 0:128].rearrange("p (h c) -> p h c", c=64)),
                      reads=[pm], writes=[vs])
                fw.op("act", ACP(vs[:, t4, 1170:1554], pm[:, 128:512]), reads=[pm], writes=[vs])
                if i % 4 == 3:
                    tb = i // 4
                    fw.dma("sp", DMA(V_scr[0:128, 4 * tb:4 * tb + 4, :], vs[:]), f"Vst{(i // 4) % 2}", reads=[vs], writes=[r_Vscr])

            def s_transp_k_A(i):
                p = i % 2
                k_r = kr[p]
                st = KTst[(i // 4) % 2]
                sub = i % 4
                fns = []
                for j in range(6):
                    fns.append(lambda e, j=j: e.transpose(out=pTk[0][0:64, j * 128:(j + 1) * 128], in_=k_r[:, j, :], identity=idb[:]))
                for j in range(2):
                    fns.append(lambda e, j=j: e.transpose(out=pTk[0][:, (6 + j) * 128:(7 + j) * 128],
                                                          in_=k_r[:, 6 + 2 * j:8 + 2 * j, :].rearrange("p a b -> p (a b)"), identity=idb[:]))
                fw.group("pe", fns, reads=[k_r, idb], writes=[pTk[0]])
                fns = []
                for j in range(2, 6):
                    fns.append(lambda e, j=j: e.transpose(out=pTk[1][:, (j - 2) * 128:(j - 1) * 128],
                                                          in_=k_r[:, 6 + 2 * j:8 + 2 * j, :].rearrange("p a b -> p (a b)"), identity=idb[:]))
                fw.group("pe", fns, reads=[k_r, idb], writes=[pTk[1]])
                cols = slice(sub * 128, (sub + 1) * 128)
                fw.op("act", lambda e: e.copy(out=st[0:64, 0:6, cols], in_=pTk[0][0:64, 0:768].rearrange("p (a b) -> p a b", a=6)),
                      reads=[pTk[0]], writes=[st])
                fw.op("dve", lambda e: e.tensor_copy(out=st[:, 6:8, cols], in_=pTk[0][:, 768:1024].rearrange("p (a b) -> p a b", a=2)),
                      reads=[pTk[0]], writes=[st])
                fw.op("act", lambda e: e.copy(out=st[:, 8:12, cols], in_=pTk[1][:, 0:512].rearrange("p (a b) -> p a b", a=4)),
                      reads=[pTk[1]], writes=[st])
                if sub == 3:
                    t0 = (i // 4) * 512
                    key = f"KTst{(i // 4) % 2}"
                    fw.dma("sp", lambda e: e.dma_start(out=KsT_scr[:, :, t0:t0 + 512], in_=st[0:64, 0:3, :]), key, reads=[st], writes=[r_Kscr])
                    fw.dma("sp", lambda e: e.dma_start(out=KwT_scr[:, :, t0:t0 + 512], in_=st[0:64, 3:6, :]), key, reads=[st], writes=[r_Kscr])
                    fw.dma("sp", lambda e: e.dma_start(out=KbT_scr[:, :, t0:t0 + 512], in_=st[:, 6:12, :]), key, reads=[st], writes=[r_Kscr])

            def s_mm_B(i):
                blocks = []
                for j in range(3):
                    blocks.append((mm_block(i, 512 * j, 512), 8 * j, 8))
                rope_epilogue(i, blocks)
                pmg = mm_block(i, 1536, 36)
                fw.op("act", lambda e: e.activation(out=gn[:, i, :], in_=pmg[:, 0:36], func=AF.Sigmoid), reads=[pmg], writes=[gn])
                if i % 4 == 3:
                    slot = i // 4
                    hTs = hT[slot % 2]
                    for cc in range(16):
                        pm = pmm[mmctr[0] % nmm]
                        mmctr[0] += 1
                        c0 = 1572 + cc * 128
                        fw.group("pe", [lambda e, kc=kc, pm=pm, c0=c0: e.matmul(pm[:], lhsT=W[:, kc, c0:c0 + 128], rhs=hTs[:, kc, :],
                                                                              start=(kc == 0), stop=(kc == 7)) for kc in range(8)],
                                 reads=[hTs, W_hi], writes=[pm])
                        g = gst[cc % 4]
                        fw.op("act", lambda e, pm=pm, g=g: e.activation(out=g[:], in_=pm[:], func=AF.Sigmoid), reads=[pm], writes=[g])
                        fw.dma("sp", lambda e, g=g, cc=cc: e.dma_start(out=gT_scr[:, cc, slot * 512:(slot + 1) * 512], in_=g[:]),
                               f"gst{cc % 4}", reads=[g], writes=[r_gscr])

            def s_transp_q_B(i):
                p = i % 2
                k_r = kr[p]
                st = QTst[(i // 4) % 2]
                sub = i % 4
                for half in range(2):
                    fns = [lambda e, j=j, half=half: e.transpose(out=pTk[half][:, j * 128:(j + 1) * 128],
                                                                 in_=k_r[:, 12 * half + 2 * j:12 * half + 2 * j + 2, :].rearrange("p a b -> p (a b)"),
                                                                 identity=idb[:]) for j in range(6)]
                    fw.group("pe", fns, reads=[k_r, idb], writes=[pTk[half]])
                    eng = "act" if half == 0 else "dve"
                    dst = st[:, 6 * half:6 * half + 6, sub * 128:(sub + 1) * 128]
                    src = pTk[half][:, 0:768].rearrange("p (a b) -> p a b", a=6)
                    if eng == "act":
                        fw.op("act", lambda e, dst=dst, src=src: e.copy(out=dst, in_=src), reads=[pTk[half]], writes=[st])
                    else:
                        fw.op("dve", lambda e, dst=dst, src=src: e.tensor_copy(out=dst, in_=src), reads=[pTk[half]], writes=[st])
                if sub == 3:
                    t0 = (i // 4) * 512
                    key = f"QTst{(i // 4) % 2}"
                    qa_v = QaT_scr.rearrange("p (j two) t -> p j two t", two=2)
                    fw.dma("sp", lambda e: e.dma_start(out=qa_v[:, :, 0, t0:t0 + 512], in_=st[0:64, 0:6, :]), key, reads=[st], writes=[r_Qscr])
                    fw.dma("sp", lambda e: e.dma_start(out=qa_v[:, :, 1, t0:t0 + 512], in_=st[64:128, 0:6, :]), key, reads=[st], writes=[r_Qscr])
                    fw.dma("sp", lambda e: e.dma_start(out=QbT_scr[:, :, t0:t0 + 512], in_=st[:, 6:12, :]), key, reads=[st], writes=[r_Qscr])

            s_mm = s_mm_A if isA else s_mm_B
            s_tk = s_transp_k_A if isA else s_transp_q_B
            s_load(0)
            s_load(1)
            load_weight(W_lo, lambda c: W[:, c, 0:csplit], lambda c: wsrc[:, c, 0:csplit], 8, csplit, gain=g1t)
            s_norm(0)
            s_load(2)
            s_transp_h(0)
            s_norm(1)
            load_weight(W_hi, lambda c: W[:, c, csplit:ncols], lambda c: wsrc[:, c, csplit:ncols], 8, ncols - csplit, gain=g1t)
            for i in range(ntile):
                if i + 1 < ntile:
                    s_transp_h(i + 1)
                if i + 2 < ntile:
                    s_norm(i + 2)
                if i + 3 < ntile:
                    s_load(i + 3)
                s_mm(i)
                if i >= 1:
                    s_tk(i - 1)
            s_tk(ntile - 1)
            fw.barrier()

        def cmp_phase(kcT, vc1):
            C2 = 2.0 * math.sqrt(2.0 / math.pi)
            w1b = A.t("w1b", [16, 256], BF16)
            w2b = A.t("w2b", [2, 64], BF16)
            posb = A.t("posb", [16], BF16)
            w2f = A.t("w2f", [2, 64], F32)
            posf = A.t("posf", [16], F32)
            gc = A.t("gc", [64], F32)
            csc = A.t("csc", [2, 32], F32)
            snc = A.t("snc", [2, 32], F32)
            fw.dma("sp", DMA(gc[:], gcmp.partition_broadcast(128).rearrange("p a b -> p (a b)")), "gc", writes=[gc])
            fw.dma("sp", DMA(csc[:], cs_cmp), "csc", writes=[csc])
            fw.dma("sp", DMA(snc[:], sn_cmp), "snc", writes=[snc])
            kc2 = A.t("kc2", [NT, 3, 2, 64], BF16)
            kT2 = A.t("kT2", [3, S], BF16)
            kcv = A.t("kcv", [NT, 384], BF16)
            kcv_sh = A.t("kcv_sh", [NT, 384], BF16)
            fw.dma("sp", DMA(kcv[:], V_scr[0:128, :, 1170:1554]), "kcv", reads=[r_Vscr], writes=[kcv])
            fw.dma("sp", DMA(kcv_sh[:, :, :], V_scr[1:129, :, 1170:1554]), "kcvsh", reads=[r_Vscr], writes=[kcv_sh])
            fw.dma("sp", DMA(kcv_sh[127:128, 0:NT - 1, :], V_scr[0:1, 1:NT, 1170:1554]), "kcvsh", reads=[r_Vscr], writes=[kcv_sh])
            fw.dma("sp", DMA(kcv_sh[127:128, NT - 1, :], expand[0:1, 64:448]), "kcvsh", writes=[kcv_sh])
            biasT = A.t("biasT", [2], F32)
            xs = A.t("xs", [256], F32)
            x2 = A.t("x2", [256], F32)
            uu = A.t("uu", [256], F32)
            sg = A.t("sg", [256], F32)
            gTs = [A.t(f"gTc{i}", [2, 256], BF16) for i in range(2)]
            kcrs = [A.t(f"kcr{i}", [2, 128], BF16) for i in range(2)]
            kcn = A.t("kcn", [2, 64], F32)
            ktr = A.t("ktr", [4, 2, 32], F32)
            sqc = A.t("sqc", [2, 64], F32)
            stc = A.t("stc", [4], F32)
            for t_ in gTs + kcrs:
                fw.op("pool", MSET(t_[:], 0.0), writes=[t_])
            fw.op("pool", MSET(vc1[:], 1.0), writes=[vc1])
            pT = [bank(0, 1024, BF16, "cpT"), bank(1, 1024, BF16, "cpT")]
            pH = [bank(2, 512, F32, "cpH"), bank(3, 512, F32, "cpH")]
            pB = bank(4, 512, F32, "cpB")
            pK = bank(5, 512, F32, "cpK")
            pKT = bank(6, 1024, BF16, "cpKT")
            for kv in range(2):
                colbase = 1152 + 192 * kv
                w1src, w2src, possrc = (w1k, w2k, posk) if kv == 0 else (w1v, w2v, posv)
                load_weight(w1b, lambda c: w1b[:, 8 * c:8 * c + 8, :], lambda c, w1src=w1src: w1src[:, 8 * c:8 * c + 8, :], 2, 2048, as3=8)
                fw.dma("sp", DMA(w2f[:], w2src), "w2f", writes=[w2f])
                fw.dma("sp", DMA(posf[:], possrc), "posf", writes=[posf])
                fw.op("dve", TC(w2b[:], w2f[:]), reads=[w2f], writes=[w2b])
                fw.op("dve", TC(posb[:], posf[:]), reads=[posf], writes=[posb])
                fw.op("dve", TC(kc2[:, :, :, 0, :], kcv[:, :, 192 * kv:192 * kv + 192].rearrange("p t (g d) -> p t g d", g=3)), reads=[kcv], writes=[kc2])
                fw.op("act", ACP(kc2[:, :, :, 1, :], kcv_sh[:, :, 192 * kv:192 * kv + 192].rearrange("p t (g d) -> p t g d", g=3)), reads=[kcv_sh], writes=[kc2])
                n = 0
                for g in range(3):
                    for tb in range(4):
                        p = pT[n % 2]
                        n += 1
                        fw.group("pe", [TRN(p[:, j * 128:(j + 1) * 128], kc2[:, tb * 8 + j, g, :, :].rearrange("p a b -> p (a b)"), idb[:])
                                        for j in range(8)], reads=[kc2, idb], writes=[p])
                        if n % 2 == 0:
                            fw.op("act", ACP(kT2[:, g, tb * 1024:(tb + 1) * 1024], p[:]), reads=[p], writes=[kT2])
                        else:
                            fw.op("dve", TC(kT2[:, g, tb * 1024:(tb + 1) * 1024], p[:]), reads=[p], writes=[kT2])
                for hc in range(2):
                    fw.group("pe", [MM(pB[:, hc:hc + 1], w1b[:, j, hc * 128:(hc + 1) * 128], posb[:, j:j + 1], start=(j == 0), stop=(j == 15))
                                    for j in range(16)], reads=[w1b, posb], writes=[pB])
                fw.op("dve", TC(biasT[:], pB[:, 0:2]), reads=[pB], writes=[biasT])
                def hidden(g):
                    gTg = gTs[g % 2]
                    for hc in range(2):
                        ph = pH[hc]
                        fw.group("pe", [MM(ph[:, 0:255], w1b[:, j, hc * 128:(hc + 1) * 128], kT2[:, g, 2 * j:2 * j + 16 * 254 + 1:16],
                                           start=(j == 0), stop=(j == 15)) for j in range(16)], reads=[w1b, kT2], writes=[ph])
                        fw.op("dve", TS(xs[:, 0:255], ph[:, 0:255], biasT[:, hc:hc + 1], ALU.add), reads=[ph, biasT], writes=[xs])
                        fw.op("pool", TT(x2[:, 0:255], xs[:, 0:255], xs[:, 0:255], ALU.mult), reads=[xs], writes=[x2])
                        fw.op("dve", TS(x2[:, 0:255], x2[:, 0:255], 0.044715, ALU.mult, 1.0, ALU.add), reads=[x2], writes=[x2])
                        fw.op("pool", TT(uu[:, 0:255], x2[:, 0:255], xs[:, 0:255], ALU.mult), reads=[x2, xs], writes=[uu])
                        fw.op("act", ACTF(sg[:, 0:255], uu[:, 0:255], AF.Sigmoid, scale=C2), reads=[uu], writes=[sg])
                        fw.op("dve", TT(gTg[:, hc, 0:255], xs[:, 0:255], sg[:, 0:255], ALU.mult), reads=[xs, sg], writes=[gTg])

                def second(g, kv=kv):
                    gTg = gTs[g % 2]
                    kcr_g = kcrs[g % 2]
                    for ct in range(2):
                        fw.group("pe", [MM(pK[:, ct * 64:(ct + 1) * 64], gTg[:, hc, ct * 128:(ct + 1) * 128], w2b[:, hc, :],
                                           start=(hc == 0), stop=(hc == 1)) for hc in range(2)], reads=[gTg, w2b], writes=[pK])
                    if kv == 1:
                        fw.op("act", ACP(vc1[:, :, g, 0:64], pK[:, 0:128].rearrange("p (a b) -> p a b", a=2)), reads=[pK], writes=[vc1])
                        return None
                    pk3 = pK[:, 0:128].rearrange("p (a b) -> p a b", a=2)
                    fw.op("act", ACTF(sqc[:], pk3, AF.Square), reads=[pK], writes=[sqc])
                    fw.op("dve", lambda e: e.tensor_reduce(out=stc[:, 0:2], in_=sqc[:], axis=AX.X, op=ALU.add), reads=[sqc], writes=[stc])
                    fw.op("act", ACTF(stc[:, 0:2], stc[:, 0:2], AF.Sqrt, scale=1.0 / 64, bias=epsb[:, 0:1]), reads=[stc, epsb], writes=[stc])
                    fw.op("dve", RECIP(stc[:, 0:2], stc[:, 0:2]), reads=[stc], writes=[stc])
                    fw.op("dve", TT(kcn[:], pk3, bc(stc[:, 0:2].unsqueeze(2), [128, 2, 64]), ALU.mult), reads=[pK, stc], writes=[kcn])
                    fw.op("pool", TT(kcn[:], kcn[:], bc(gc[:].unsqueeze(1), [128, 2, 64]), ALU.mult), reads=[kcn, gc], writes=[kcn])
                    x1, xx2 = kcn[:, :, 0:32], kcn[:, :, 32:64]
                    fw.op("pool", TT(ktr[:, 0], x1, csc[:], ALU.mult), reads=[kcn, csc], writes=[ktr])
                    fw.op("pool", TT(ktr[:, 1], xx2, snc[:], ALU.mult), reads=[kcn, snc], writes=[ktr])
                    fw.op("pool", TT(ktr[:, 2], xx2, csc[:], ALU.mult), reads=[kcn, csc], writes=[ktr])
                    fw.op("pool", TT(ktr[:, 3], x1, snc[:], ALU.mult), reads=[kcn, snc], writes=[ktr])
                    fw.op("pool", TT(kcr_g[:, :, 0:32], ktr[:, 0], ktr[:, 1], ALU.subtract), reads=[ktr], writes=[kcr_g])
                    fw.op("pool", TT(kcr_g[:, :, 32:64], ktr[:, 2], ktr[:, 3], ALU.add), reads=[ktr], writes=[kcr_g])

                    def trp(g=g, kcr_g=kcr_g):
                        fw.group("pe", [TRN(pKT[:, ct * 128:(ct + 1) * 128], kcr_g[:, ct, :], idb[:]) for ct in range(2)],
                                 reads=[kcr_g, idb], writes=[pKT])
                        fw.op("act", ACP(kcT[:, g, :], pKT[:, 0:256]), reads=[pKT], writes=[kcT])
                    return trp

                trq = []
                for g in range(3):
                    hidden(g)
                    if trq:
                        trq.pop(0)()
                    if g >= 1:
                        t_ = second(g - 1)
                        if t_ is not None:
                            trq.append(t_)
                t_ = second(2)
                if t_ is not None:
                    trq.append(t_)
                while trq:
                    trq.pop(0)()
            fw.barrier()

        def share(tile, ap, name):
            t = Tile(ap, name)
            t.res = tile.res
            return t

        def attn_units(units, q_of, k_of, v_of, mask_of, pO_of, first_of, last_of, pS, Pt, ctr, kqv, hooks=None):
            LAG = 3
            pend = []
            Kres, Qres, Vres = kqv
            hooks = hooks or {}

            def emit_pv(u, pt):
                po = pO_of(u)
                fw.group("pe", [MM(po[0:65, :], v_of(u), pt[:], start=first_of(u), stop=last_of(u))],
                         reads=[pt, Vres(u) if callable(Vres) else Vres], writes=[po])

            for ui, u in enumerate(units):
                if ui in hooks:
                    hooks[ui]()
                ps = pS[ctr[0] % len(pS)]
                pt = Pt[ctr[0] % len(Pt)]
                ctr[0] += 1
                fw.group("pe", [MM(ps[:], k_of(u), q_of(u))], reads=[Kres(u) if callable(Kres) else Kres, Qres], writes=[ps])
                fw.op("act", ACTF(pt[:], ps[:], AF.Exp, scale=SCALE), reads=[ps], writes=[pt])
                m = mask_of(u)
                if m is not None:
                    fw.op("dve", TT(pt[:], pt[:], m[0], ALU.mult), reads=[pt, m[1]], writes=[pt])
                pend.append((u, pt))
                if len(pend) > LAG:
                    emit_pv(*pend.pop(0))
            while pend:
                emit_pv(*pend.pop(0))

        def finalize(pO_list, heads_cols, coef_fn, yacc, first_branch, OTsb, pTf, s, rd4, ytmp, defer=False):
            for i, po in enumerate(pO_list):
                ot = OTsb[i % len(OTsb)]
                fw.op("dve", TC(ot[0:65, :], po[0:65, :]), reads=[po], writes=[ot])
            parts = []
            for i, po in enumerate(pO_list):
                parts.append(lambda i=i: fin_head(i, pO_list, heads_cols, coef_fn, yacc, first_branch, OTsb, pTf, rd4, ytmp))
            if defer:
                return parts
            for p in parts:
                p()
            return []

        def fin_head(i, pO_list, heads_cols, coef_fn, yacc, first_branch, OTsb, pTf, rd4, ytmp):
            if True:
                ot = OTsb[i % len(OTsb)]
                fw.group("pe", [TRN(pTf[:, sub * 65:(sub + 1) * 65], ot[0:65, sub * 128:(sub + 1) * 128], idf[0:65, 0:65]) for sub in range(4)],
                         reads=[ot, idf], writes=[pTf])
                p3 = pTf[:, 0:260].rearrange("p (a b) -> p a b", a=4)
                fw.op("dve", TS(rd4[:], p3[:, :, 64], 1e-30, ALU.max), reads=[pTf], writes=[rd4])
                fw.op("dve", RECIP(rd4[:], rd4[:]), reads=[rd4], writes=[rd4])
                gate = coef_fn(i)
                if gate is not None:
                    fw.op("dve", TT(rd4[:], rd4[:], gate, ALU.mult), reads=[rd4, gn], writes=[rd4])
                hc = heads_cols[i]
                cb = bc(rd4[:].unsqueeze(2), [128, 4, 64])
                if first_branch:
                    fw.op("dve", TT(yacc[:, :, hc, :], p3[:, :, 0:64], cb, ALU.mult), reads=[pTf, rd4], writes=[yacc])
                else:
                    fw.op("dve", TT(ytmp[:], p3[:, :, 0:64], cb, ALU.mult), reads=[pTf, rd4], writes=[ytmp])
                    fw.op("pool", TT(yacc[:, :, hc, :], yacc[:, :, hc, :], ytmp[:], ALU.add), reads=[ytmp, yacc], writes=[yacc])

        def y_to_scratch(yacc, nchunk, chunk0, s, ybf, yst, pY):
            fw.op("dve", TC(ybf[:], yacc[:].rearrange("p a h d -> p a (h d)")), reads=[yacc], writes=[ybf])
            for c in range(nchunk):
                pb = pY[c // 2]
                fw.group("pe", [TRN(pb[:, (c % 2) * 512 + sub * 128:(c % 2) * 512 + (sub + 1) * 128], ybf[:, sub, c * 128:(c + 1) * 128], idb[:])
                                for sub in range(4)], reads=[ybf, idb], writes=[pb])
            for c2 in range((nchunk + 1) // 2):
                n2 = min(2, nchunk - 2 * c2)
                src = pY[c2][:, 0:n2 * 512].rearrange("p (a b) -> p a b", a=n2)
                if c2 % 2 == 0:
                    fw.op("act", ACP(yst[:, 2 * c2:2 * c2 + n2, :], src), reads=[pY[c2]], writes=[yst])
                else:
                    fw.op("dve", TC(yst[:, 2 * c2:2 * c2 + n2, :], src), reads=[pY[c2]], writes=[yst])
            fw.dma("sp", DMA(yT_scr[:, chunk0:chunk0 + nchunk, s * 512:(s + 1) * 512], yst[:, 0:nchunk, :]), "yst", reads=[yst], writes=[r_yscr])

        def nsa_phase(kcT, vc1):
            KsT = A.t("KsT", [3, S], BF16)
            KwT = A.t("KwT", [3, S], BF16)
            Vs1 = A.t("Vs1", [NT, 3, 65], BF16)
            Vw1 = A.t("Vw1", [NT, 3, 65], BF16)
            tbc_t = A.t("tbc_t", [2, L_C], BF16)
            tbw_t = A.t("tbw_t", [2, L_W], BF16)
            selA_t = A.t("selA_t", [4, 4, 64], F32)
            selB_t = A.t("selB_t", [4, 4, 64], F32)
            ov1b = A.t("ov1b", [2, 65], BF16)
            Qa_bufs = [A.t("Qa", [12, 512], BF16),
                       Tile(big[:, stage_off // 4:stage_off // 4 + 3072].bitcast(BF16).rearrange("p (a b) -> p a b", a=12), "Qa2")]
            cm = A.t("cm", [2, 512], BF16)
            E8 = [A.t(f"E8_{i}", [512], BF16) for i in range(8)]
            cctr = [0]
            Pt = [A.t(f"Pt{i}", [512], BF16) for i in range(6)]
            OTsb = [A.t(f"OTsb{i}", [512], F32) for i in range(4)]
            yacc = A.t("yacc", [4, 12, 64], F32)
            ybf = A.t("ybf", [4, 768], BF16)
            yst = A.t("yst", [6, 512], BF16)
            imp = A.t("imp", [4, 64], F32)
            itmp = A.t("itmp", [4, 64], F32)
            score = A.t("score", [4, 64], F32)
            sc2 = A.t("sc2", [4, 64], F32)
            m8 = A.t("m8", [2, 8], F32)
            negsel_g = [A.t(f"negsel{g}", [4, 128], BF16) for g in range(3)]
            rd4 = A.t("rd4", [4], F32)
            ytmp = A.t("ytmp", [4, 64], F32)
            fw.op("pool", MSET(KwT[64:128, :, :], 0.0), writes=[KwT])
            Qg2 = [[Res(f"Qg{b}_{g}") for g in range(3)] for b in range(2)]
            for b_ in range(2):
                fw.op("pool", MSET(Qa_bufs[b_][64:128, :, :], 0.0), writes=Qg2[b_])

            def load_qa(s_):
                fw.dma("sp", DMA(Qa_bufs[s_ % 2][0:64, :, :], QaT_scr[:, :, s_ * 512:(s_ + 1) * 512]), f"Qal{s_ % 2}", reads=[r_Qscr], writes=Qg2[s_ % 2])
            for ng in negsel_g:
                fw.op("pool", MSET(ng[:], 0.0), writes=[ng])
            fw.dma("sp", DMA(tbc_t[:], tbc), "tbc", writes=[tbc_t])
            fw.dma("sp", DMA(tbw_t[:], tbw), "tbw", writes=[tbw_t])
            fw.dma("sp", DMA(selA_t[:], selA), "selA", writes=[selA_t])
            fw.dma("sp", DMA(selB_t[:], selB), "selB", writes=[selB_t])
            fw.dma("sp", DMA(ov1b[:], ov1), "ov1b", writes=[ov1b])
            pO = [bank(b, 512, F32, "pO") for b in range(4)]
            pS = [bank(b, 512, F32, "pS") for b in (4, 5, 6)]
            b7 = bank(7, 512, F32, "b7")
            b7b = share(b7, psum[:, 512 * 7:512 * 8].bitcast(BF16), "b7b")
            pY = [share(pS[i], psum[:, 512 * (4 + i):512 * (5 + i)].bitcast(BF16), "pY") for i in range(3)]
            ctr = [0]

            for s in range(4):
                par = s % 2
                Qa, Qg = Qa_bufs[s % 2], Qg2[s % 2]
                if s == 0:
                    load_qa(0)
                fw.dma("sp", DMA(cm[:], cmask[:, s]), "cml", writes=[cm])
                if s == 0:
                    for g in range(3):
                        fw.dma("sp", DMA(KsT[0:64, g, :], KsT_scr[:, g, :]), "KsTl", reads=[r_Kscr], writes=[Res()])
                        fw.dma("sp", DMA(KsT[64:128, g, :], expand), "KsTl", writes=[KsT] if g == 2 else [Res()])
                        fw.dma("sp", DMA(KwT[0:64, g, :], KwT_scr[:, g, :]), "KwTl", reads=[r_Kscr], writes=[KwT] if g == 2 else [Res()])
                    for hq in range(2):
                        fw.dma("sp", DMA(Vs1[:, 16 * hq:16 * hq + 16].rearrange("p t g c -> p t (g c)"), V_scr[0:128, 16 * hq:16 * hq + 16, 0:195]), "Vs1l",
                               reads=[r_Vscr], writes=[Vs1] if hq == 1 else [Res()])
                        fw.dma("sp", DMA(Vw1[:, 16 * hq:16 * hq + 16].rearrange("p t g c -> p t (g c)"), V_scr[0:128, 16 * hq:16 * hq + 16, 195:390]), "Vw1l",
                               reads=[r_Vscr], writes=[Vw1] if hq == 1 else [Res()])
                if s + 1 < 4:
                    load_qa(s + 1)
                def select(g):
                    negsel = negsel_g[g]
                    fw.op("dve", TT(score[:], imp[:], selA_t[:, s], ALU.mult), reads=[imp, selA_t], writes=[score])
                    fw.op("dve", TT(score[:], score[:], selB_t[:, s], ALU.add), reads=[score, selB_t], writes=[score])
                    for sub in range(4):
                        fw.op("dve", lambda e, sub=sub: e.max(out=m8[:, 0, :], in_=score[:, sub, :]), reads=[score], writes=[m8])
                        fw.op("dve", lambda e, sub=sub: e.match_replace(out=sc2[:, sub, :], in_to_replace=m8[:, 0, :], in_values=score[:, sub, :],
                                                                        imm_value=-3.0e38), reads=[score, m8], writes=[sc2])
                        fw.op("dve", lambda e, sub=sub: e.max(out=m8[:, 1, :], in_=sc2[:, sub, :]), reads=[sc2], writes=[m8])
                        fw.op("dve", TS(negsel[:, sub, 64:128], score[:, sub, :], m8[:, 1, 7:8], ALU.is_lt, NEGSEL, ALU.mult),
                              reads=[score, m8], writes=[negsel])
                    def tail(g=g):
                        fw.group("pe", [TRN(b7b[:, sub * 128:(sub + 1) * 128], negsel[:, sub, :], idb[:]) for sub in range(4)],
                                 reads=[negsel, idb], writes=[b7])
                        fw.op("act", ACP(Qa[64:128, 4 * g:4 * g + 4, :], bc(b7b[64:128, 0:512].unsqueeze(1), [64, 4, 512])), reads=[b7], writes=[Qg[g]])
                    return tail

                fin_q = []
                impb = [b7, pS[2]]
                sbk = [pS[0], pS[1]]
                for g in range(3):
                    pend = []

                    def emit_back(hh, e0, e1, g=g):
                        ets = (e0, e1)
                        fw.group("pe", [MM(pO[hh][0:65, :], vc1[:, ct, g, :], ets[ct][:], start=(ct == 0), stop=(ct == 1)) for ct in range(2)],
                                 reads=[e0, e1, vc1], writes=[pO[hh]])
                        ib = impb[hh % 2]
                        fw.group("pe", [MM(ib[:, sub * 65:(sub + 1) * 65], ets[ct][:, sub * 128:(sub + 1) * 128], ov1b[:, ct, :],
                                           start=(ct == 0), stop=(ct == 1)) for sub in range(4) for ct in range(2)],
                                 reads=[e0, e1, ov1b], writes=[ib])
                        p3 = ib[:, 0:260].rearrange("p (a b) -> p a b", a=4)
                        fw.op("dve", TS(rd4[:], p3[:, :, 64], 1e-30, ALU.max), reads=[ib], writes=[rd4])
                        fw.op("dve", RECIP(rd4[:], rd4[:]), reads=[rd4], writes=[rd4])
                        cb = bc(rd4[:].unsqueeze(2), [128, 4, 64])
                        if hh == 0:
                            fw.op("dve", TT(imp[:], p3[:, :, 0:64], cb, ALU.mult), reads=[ib, rd4], writes=[imp])
                        else:
                            fw.op("dve", TT(itmp[:], p3[:, :, 0:64], cb, ALU.mult), reads=[ib, rd4], writes=[itmp])
                            fw.op("pool", TT(imp[:], imp[:], itmp[:], ALU.add), reads=[itmp, imp], writes=[imp])
                        for _ in range(2):
                            if fin_q:
                                fin_q.pop(0)()

                    for hh in range(4):
                        ets = []
                        for ct in range(2):
                            ps = sbk[cctr[0] % 2]
                            et = E8[cctr[0] % 8]
                            cctr[0] += 1
                            fw.group("pe", [MM(ps[:], kcT[:, g, ct * 128:(ct + 1) * 128], Qa[:, 4 * g + hh, :])], reads=[kcT, Qg[g]], writes=[ps])
                            fw.op("act", ACTF(et[:], ps[:], AF.Exp, scale=SCALE), reads=[ps], writes=[et])
                            fw.op("dve", TT(et[:], et[:], cm[:, ct, :], ALU.mult), reads=[et, cm], writes=[et])
                            ets.append(et)
                        pend.append((hh, ets[0], ets[1]))
                        if len(pend) > 2:
                            emit_back(*pend.pop(0))
                    while pend:
                        emit_back(*pend.pop(0))
                    fin_q.extend(finalize([pO[hh] for hh in range(4)], [4 * g + hh for hh in range(4)],
                                          lambda i, g=g, s=s: gn[:, 4 * s:4 * s + 4, 3 * (4 * g + i) + 0], yacc, True, OTsb, b7, s, rd4, ytmp,
                                          defer=True))
                    fin_q.append(select(g))
                pending = fin_q
                for br in (1, 2):
                    for g in range(3):
                        if br == 1:
                            KT, V1, tab, kts = KsT, Vs1, tbc_t, list(range(0, 8 * s + 8))
                            need_mask = lambda kt, s=s: kt >= 8 * s
                        else:
                            KT, V1, tab, kts = KwT, Vw1, tbw_t, list(range(max(0, 8 * s - 4), 8 * s + 8))
                            need_mask = lambda kt: True
                        units = [(kt, hh) for kt in kts for hh in range(4)]

                        def mask_of(u, tab=tab, need_mask=need_mask, s=s, par=par):
                            kt = u[0]
                            if not need_mask(kt):
                                return None
                            off = 1024 * s - 128 * kt + 896
                            return (tab[:, par, off:off + 512], tab)
                        attn_units(units,
                                   q_of=lambda u, g=g: Qa[:, 4 * g + u[1], :],
                                   k_of=lambda u, KT=KT, g=g: KT[:, g, u[0] * 128:(u[0] + 1) * 128],
                                   v_of=lambda u, V1=V1, g=g: V1[:, u[0], g, :],
                                   mask_of=mask_of,
                                   pO_of=lambda u: pO[u[1]],
                                   first_of=lambda u, kts=kts: u[0] == kts[0],
                                   last_of=lambda u, kts=kts: u[0] == kts[-1],
                                   pS=pS, Pt=Pt, ctr=ctr, kqv=(KT.res, Qg[g], V1.res),
                                   hooks={6 + 4 * k: p for k, p in enumerate(pending)})
                        pending = finalize([pO[hh] for hh in range(4)], [4 * g + hh for hh in range(4)],
                                           lambda i, g=g, s=s, br=br: gn[:, 4 * s:4 * s + 4, 3 * (4 * g + i) + br], yacc, False, OTsb, b7, s, rd4, ytmp,
                                           defer=True)
                for p in pending:
                    p()
                y_to_scratch(yacc, 6, 0, s, ybf, yst, pY)
            fw.barrier()

        def dil_phase():
            KbT = A.t("KbT", [6, S], BF16)
            Vb1 = A.t("Vb1", [NT, 12, 65], BF16)
            tabs = [A.t("tbd0_t", [2, L_D0], BF16), A.t("tbd1_t", [2, L_D1], BF16), A.t("tbd2_t", [2, L_D2], BF16)]
            Qb2 = [A.t(f"Qb{i}", [12, 512], BF16) for i in range(2)]
            Pt = [A.t(f"Pt{i}", [512], BF16) for i in range(6)]
            OTsb = [A.t(f"OTsb{i}", [512], F32) for i in range(2)]
            yacc = A.t("yaccb", [4, 4, 64], F32)
            ybf = A.t("ybfb", [4, 256], BF16)
            yst = A.t("ystb", [2, 512], BF16)
            rd4 = A.t("rd4", [4], F32)
            ytmp = A.t("ytmp", [4, 64], F32)
            Kc = [Res(f"KbTc{c}") for c in range(4)]
            Vc = [Res(f"Vb1c{c}") for c in range(4)]
            def load_chunk(c):
                fw.dma("sp", DMA(KbT[:, :, 1024 * c:1024 * c + 1024], KbT_scr[:, :, 1024 * c:1024 * c + 1024]), f"KbTl{c}", reads=[r_Kscr], writes=[Kc[c]])
                fw.dma("sp", DMA(Vb1[:, 8 * c:8 * c + 8, :, :].rearrange("p t h c -> p t (h c)"), V_scr[0:128, 8 * c:8 * c + 8, 390:1170]), f"Vb1l{c}",
                       reads=[r_Vscr], writes=[Vc[c]])
            for q_ in Qb2:
                fw.op("pool", MSET(q_[:], 0.0), writes=[q_])
            for g, (src, t) in enumerate(zip((tbd0, tbd1, tbd2), tabs)):
                fw.dma("sp", DMA(t[:], src), f"tbd{g}", writes=[t])
            load_chunk(0)
            pO = [bank(b, 512, F32, "pO") for b in range(2)]
            pS = [bank(b, 512, F32, "pS") for b in (4, 5, 6)]
            b7 = bank(7, 512, F32, "b7")
            pY = [share(pS[i], psum[:, 512 * (4 + i):512 * (5 + i)].bitcast(BF16), "pY") for i in range(3)]
            ctr = [0]
            def load_qb(s):
                Qb = Qb2[s % 2]
                Qv = Qb[:].rearrange("p (j two) t -> p j two t", two=2)
                fw.dma("sp", DMA(Qv[0:64, :, 0, :], QbT_scr[0:64, :, s * 512:(s + 1) * 512]), f"Qbl{s % 2}", reads=[r_Qscr], writes=[Qb])
                fw.dma("sp", DMA(Qv[64:128, :, 1, :], QbT_scr[64:128, :, s * 512:(s + 1) * 512]), f"Qbl{s % 2}", reads=[r_Qscr], writes=[Qb])
            load_qb(0)
            for s in range(4):
                par = s % 2
                Qb = Qb2[s % 2]
                if s + 1 < 4:
                    load_qb(s + 1)
                if s == 0:
                    for c in range(1, 4):
                        load_chunk(c)
                pending = []
                for hp in range(2):
                    units = []
                    for g, back in enumerate((1, 4, 16)):
                        for kt in range(max(0, 8 * s - back), 8 * s + 8):
                            for e_ in range(2):
                                units.append((g, kt, e_))
                    first, last = units[0], units[-1]

                    def mask_of(u, s=s, par=par):
                        off = 1024 * s - 128 * u[1] + 896
                        t = tabs[u[0]]
                        return (t[:, par, off:off + 512], t)
                    attn_units(units,
                               q_of=lambda u, hp=hp: Qb[:, 4 * u[0] + 2 * hp + u[2], :],
                               k_of=lambda u, hp=hp: KbT[:, 2 * u[0] + hp, u[1] * 128:(u[1] + 1) * 128],
                               v_of=lambda u, hp=hp: Vb1[:, u[1], 4 * u[0] + 2 * hp + u[2], :],
                               mask_of=mask_of,
                               pO_of=lambda u: pO[u[2]],
                               first_of=lambda u, first=first: (u[0], u[1]) == (first[0], first[1]),
                               last_of=lambda u, last=last: (u[0], u[1]) == (last[0], last[1]),
                               pS=pS, Pt=Pt, ctr=ctr, kqv=(lambda u: Kc[u[1] // 8], Qb.res, lambda u: Vc[u[1] // 8]),
                               hooks={6 + 4 * k: p for k, p in enumerate(pending)})
                    pending = finalize([pO[0], pO[1]], [2 * hp, 2 * hp + 1], lambda i: None, yacc, True, OTsb, b7, s, rd4, ytmp, defer=True)
                for p in pending:
                    p()
                y_to_scratch(yacc, 2, 6, s, ybf, yst, pY)
            fw.barrier()

        def e_phase():
            xm = A.t("xm", [NOWN, D], F32)
            xr = [Res(f"xm{i}") for i in range(NOWN)]
            for s4 in range(4):
                fw.dma("pool", DMA(xm[:, 4 * s4:4 * s4 + 4, :], x_own[512 * s4:512 * s4 + 512, :].rearrange("(t p) c -> p t c", p=128)), f"xml{s4}",
                       writes=xr[4 * s4:4 * s4 + 4])
            e1_mark = A.top
            woa_b = A.t("woa_b", [6, 1024], BF16)
            wob_b = A.t("wob_b", [2, 1024], BF16)
            wout_b = A.t("wout_b", [8, 1024], BF16)
            load_weight(woa_b, lambda c: woa_b[:, 2 * c:2 * c + 2, :], lambda c: woa[:, 2 * c:2 * c + 2, :], 3, 2048, as3=2)
            load_weight(wob_b, lambda c: wob_b[:, 0:2, :], lambda c: wob[:, 0:2, :], 1, 2048, as3=2)
            load_weight(wout_b, lambda c: wout_b[:, 2 * c:2 * c + 2, :], lambda c: wout[:, 2 * c:2 * c + 2, :], 4, 2048, as3=2)
            yT = [A.t(f"yT{i}", [8, 512], BF16) for i in range(2)]
            gT = [A.t(f"gT{i}", [16, 512], BF16) for i in range(1)]
            mixT = A.t("mixT", [8, 512], BF16)
            t1 = [A.t(f"t1_{i}", [512], F32) for i in range(2)]
            t2 = [A.t(f"t2_{i}", [512], F32) for i in range(2)]
            junk = A.t("junkE", [D], BF16)
            st2 = A.t("st2", [4], F32)
            h2 = [A.t(f"h2_{i}", [D], BF16) for i in range(2)]
            h2st = [A.t(f"h2st{i}", [8, 512], BF16) for i in range(1)]
            pU = [bank(b, 512, F32, "pU") for b in range(4)]
            pD = [bank(b, 512, F32, "pD") for b in (4, 5)]
            pT2 = [bank(b, 1024, BF16, "pT2") for b in (6, 7)]
            for s in range(4):
                y_t, g_t, hst = yT[s % 2], gT[0], h2st[0]
                if s == 0:
                    fw.dma("sp", DMA(y_t[:], yT_scr[:, :, 0:512]), "yTl0", reads=[r_yscr], writes=[y_t])
                fw.dma("sp", DMA(g_t[:, 0:8, :], gT_scr[:, 0:8, s * 512:(s + 1) * 512]), "gTl", reads=[r_gscr], writes=[g_t])
                fw.dma("pool", DMA(g_t[:, 8:16, :], gT_scr[:, 8:16, s * 512:(s + 1) * 512]), "gTl2", reads=[r_gscr], writes=[g_t])
                if s + 1 < 4:
                    fw.dma("sp", DMA(yT[(s + 1) % 2][:], yT_scr[:, :, (s + 1) * 512:(s + 2) * 512]), f"yTl{(s + 1) % 2}", reads=[r_yscr], writes=[yT[(s + 1) % 2]])
                for cc in range(8):
                    pa, pb = pU[(2 * cc) % 4], pU[(2 * cc + 1) % 4]
                    fw.group("pe", [MM(pa[:], woa_b[:, fc, cc * 128:(cc + 1) * 128], y_t[:, fc, :], start=(fc == 0), stop=(fc == 5)) for fc in range(6)],
                             reads=[woa_b, y_t], writes=[pa])
                    fw.group("pe", [MM(pb[:], wob_b[:, fc, cc * 128:(cc + 1) * 128], y_t[:, 6 + fc, :], start=(fc == 0), stop=(fc == 1)) for fc in range(2)],
                             reads=[wob_b, y_t], writes=[pb])
                    ta, tb = t1[cc % 2], t2[cc % 2]
                    fw.op("dve", TT(ta[:], pa[:], g_t[:, cc, :], ALU.mult), reads=[pa, g_t], writes=[ta])
                    fw.op("dve", TT(tb[:], pb[:], g_t[:, 8 + cc, :], ALU.mult), reads=[pb, g_t], writes=[tb])
                    fw.op("pool", TT(mixT[:, cc, :], ta[:], tb[:], ALU.add), reads=[ta, tb], writes=[mixT])
                tr_pending = []
                for sub in range(4):
                    ti = 4 * s + sub
                    for cb in range(2):
                        pd = pD[cb]
                        fw.group("pe", [MM(pd[:], mixT[:, fc, sub * 128:(sub + 1) * 128], wout_b[:, fc, cb * 512:(cb + 1) * 512],
                                           start=(fc == 0), stop=(fc == 7)) for fc in range(8)], reads=[mixT, wout_b], writes=[pd])
                        fw.op("dve", TT(xm[:, ti, cb * 512:(cb + 1) * 512], pd[:], xm[:, ti, cb * 512:(cb + 1) * 512], ALU.add),
                              reads=[pd, xr[ti]], writes=[xr[ti]])
                    fw.op("act", ACTF(junk[:], xm[:, ti, :], AF.Square, accum=st2[:, 0:1]), reads=[xr[ti]], writes=[junk, st2])
                    fw.op("act", ACTF(st2[:, 1:2], st2[:, 0:1], AF.Sqrt, scale=1.0 / D, bias=epsb[:, 0:1]), reads=[st2, epsb], writes=[st2])
                    fw.op("dve", RECIP(st2[:, 2:3], st2[:, 1:2]), reads=[st2], writes=[st2])
                    hh = h2[sub % 2]
                    fw.op("dve", TS(hh[:], xm[:, ti, :], st2[:, 2:3], ALU.mult), reads=[xr[ti], st2], writes=[hh])
                    def tr_part(sub=sub, hh=hh):
                        pt = pT2[sub % 2]
                        fw.group("pe", [TRN(pt[:, kc * 128:(kc + 1) * 128], hh[:, kc * 128:(kc + 1) * 128], idb[:]) for kc in range(8)],
                                 reads=[hh, idb], writes=[pt])
                        fw.op("act", ACP(hst[:, :, sub * 128:(sub + 1) * 128], pt[:].rearrange("p (a b) -> p a b", a=8)), reads=[pt], writes=[hst])
                    if tr_pending:
                        tr_pending.pop(0)()
                    tr_pending.append(tr_part)
                while tr_pending:
                    tr_pending.pop(0)()
                fw.dma("sp", DMA(h2T_scr[:, :, s * 512:(s + 1) * 512], hst[:]), "h2stl", reads=[hst], writes=[r_h2scr])
            fw.barrier()
            A.top = e1_mark
            g2t = A.t("g2t", [8], F32)
            fw.dma("sp", DMA(g2t[:], g2), "g2t", writes=[g2t])
            wupq = [A.t(f"wupq{i}", [8, 1024], BF16) for i in range(2)]
            wdnq = [A.t(f"wdnq{i}", [8, 1024], BF16) for i in range(2)]
            h2T = [A.t(f"h2T{i}", [8, 512], BF16) for i in range(2)]
            aT = [A.t(f"aT{i}", [8, 512], BF16) for i in range(2)]
            rl = [A.t(f"rl{i}", [512], F32) for i in range(2)]
            pUp = [bank(b, 512, F32, "pUp") for b in range(4)]
            pDn = [bank(b, 512, F32, "pDn") for b in (4, 5, 6, 7)]
            n = 0
            def load_quarter(fq):
                wu, wd = wupq[fq % 2], wdnq[fq % 2]
                for kc in range(8):
                    i_ = stage_ctr[0]
                    st = stage[i_ % 4]
                    key = f"wstage{i_ % 4}"
                    stage_ctr[0] += 1
                    qn = "sp" if fq == 0 else "pool"
                    fw.dma(qn, DMA(st[:, 0:1024], wup[:, kc, fq * 1024:(fq + 1) * 1024]), key, writes=[st])
                    if fq > 0:
                        fw.op("pool", TT(wu[:, kc, :], st[:, 0:1024], bc(g2t[:, kc:kc + 1], [128, 1024]), ALU.mult), reads=[st, g2t], writes=[wu])
                    elif kc % 2:
                        fw.op("act", ACTF(wu[:, kc, :], st[:, 0:1024], AF.Copy, scale=g2t[:, kc:kc + 1]), reads=[st, g2t], writes=[wu])
                    else:
                        fw.op("dve", TS(wu[:, kc, :], st[:, 0:1024], g2t[:, kc:kc + 1], ALU.mult), reads=[st, g2t], writes=[wu])
                for c in range(4):
                    i_ = stage_ctr[0]
                    st = stage[i_ % 4]
                    key = f"wstage{i_ % 4}"
                    stage_ctr[0] += 1
                    qn = "sp" if fq == 0 else "pool"
                    sv = st[:, 0:2048].rearrange("p (a b) -> p a b", a=2)
                    fw.dma(qn, DMA(sv, wdown[:, 8 * fq + 2 * c:8 * fq + 2 * c + 2, :]), key, writes=[st])
                    if fq > 0:
                        fw.op("pool", TC(wd[:, 2 * c:2 * c + 2, :], sv), reads=[st], writes=[wd])
                    else:
                        fw.op("act" if c % 2 else "dve", (ACP if c % 2 else TC)(wd[:, 2 * c:2 * c + 2, :], sv), reads=[st], writes=[wd])
            load_quarter(0)
            for fq in range(4):
                wu, wd = wupq[fq % 2], wdnq[fq % 2]
                for s in range(4):
                    if s == 0 and fq < 3:
                        load_quarter(fq + 1)
                    h_t = h2T[n % 2]
                    a_t = aT[n % 2]
                    fw.dma("sp", DMA(h_t[:], h2T_scr[:, :, s * 512:(s + 1) * 512]), f"h2Tl{n % 2}", reads=[r_h2scr], writes=[h_t])
                    n += 1
                    for fcb in range(8):
                        pu = pUp[fcb % 4]
                        fw.group("pe", [MM(pu[:], wu[:, kc, fcb * 128:(fcb + 1) * 128], h_t[:, kc, :], start=(kc == 0), stop=(kc == 7)) for kc in range(8)],
                                 reads=[wu, h_t], writes=[pu])
                        r = rl[fcb % 2]
                        fw.op("act", ACTF(r[:], pu[:], AF.Relu), reads=[pu], writes=[r])
                        fw.op("dve", TT(a_t[:, fcb, :], r[:], r[:], ALU.mult), reads=[r], writes=[a_t])
                    for sub in range(4):
                        ti = 4 * s + sub
                        for cb in range(2):
                            pd = pDn[(2 * sub + cb) % 4]
                            fw.group("pe", [MM(pd[:], a_t[:, fc, sub * 128:(sub + 1) * 128], wd[:, fc, cb * 512:(cb + 1) * 512],
                                               start=(fc == 0), stop=(fc == 7)) for fc in range(8)], reads=[a_t, wd], writes=[pd])
                            fw.op("dve", TT(xm[:, ti, cb * 512:(cb + 1) * 512], pd[:], xm[:, ti, cb * 512:(cb + 1) * 512], ALU.add),
                                  reads=[pd, xr[ti]], writes=[xr[ti]])
                        if fq == 3:
                            fw.dma("sp", DMA(out[ti * 128:(ti + 1) * 128, :], xm[:, ti, :]), "outst", reads=[xr[ti]], writes=[r_out])
            fw.barrier()

        epsb = A.t("epsb", [1], F32)
        fw.op("pool", lambda e: e.memset(epsb[:], EPS), writes=[epsb])
        base_mark = A.top
        r_Vscr, r_Kscr, r_Qscr, r_gscr = Res("Vscr"), Res("Kscr"), Res("Qscr"), Res("gscr")
        r_yscr, r_h2scr, r_out = Res("yscr"), Res("h2scr"), Res("out")

        order = ["A", "B", "C", "D1", "D2", "all"]
        lvl = order.index(upto)
        fw.dma("sp", DMA(V_scr[128:129, :, 1170:1554], bass.AP(expand.tensor, 64, [[0, 1], [0, NT], [1, 384]])), "vpad", writes=[r_Vscr])
        proj_phase("A")
        if lvl >= 1:
            proj_phase("B")
        if debug and lvl >= 1:
            fw.dma("sp", DMA(dbg_gn, gn[:]), "dbggn", reads=[gn], writes=[r_out])
        if lvl >= 2:
            A.top = base_mark
            kcT = A.t("kcT", [3, 256], BF16)
            vc1 = A.t("vc1", [2, 3, 65], BF16)
            c_mark = A.top
            cmp_phase(kcT, vc1)
            if debug:
                fw.dma("sp", DMA(dbg_kc, kcT[:]), "dbgkc", reads=[kcT], writes=[r_out])
                fw.dma("sp", DMA(dbg_vc, vc1[:]), "dbgvc", reads=[vc1], writes=[r_out])
                fw.barrier()
        if lvl >= 3:
            A.top = c_mark
            nsa_phase(kcT, vc1)
        if lvl >= 4:
            A.top = base_mark
            dil_phase()
        if lvl >= 5:
            A.top = base_mark
            e_phase()

        fw.barrier()
        fw.replay()
    return nc


def _rope_tables(pos):
    half = 32
    inv_freq = np.power(np.float32(10000.0), -np.arange(half, dtype=np.float32) / np.float32(half)).astype(np.float32)
    ang = pos.astype(np.float32)[:, None] * inv_freq[None, :]
    return np.cos(ang).astype(np.float32), np.sin(ang).astype(np.float32)


def _tok_major(a, ntile):
    return np.ascontiguousarray(a.reshape(ntile, 128, -1).transpose(1, 0, 2))


def _pmajor(w, nchunk):
    return np.ascontiguousarray(w.reshape(nchunk, 128, -1).transpose(1, 0, 2))


def _toeplitz(L, delta, fn):
    k = np.arange(128)[:, None]
    j = np.arange(L)[None, :]
    d = delta + j - 896 - k
    return fn(d)


def host_prep(inputs):
    bf = ml_dtypes.bfloat16
    x = np.asarray(inputs["x"], np.float32)
    w_in = np.asarray(inputs["w_in"], np.float32)[0]
    sizes = (768, 192, 192, 192, 192, 192, 192, 36, 768, 768, 768, 1024, 1024)
    offs = np.concatenate([[0], np.cumsum(sizes)])
    seg = {n: w_in[:, offs[i]:offs[i + 1]] for i, n in enumerate(
        ["q_a", "k_c", "v_c", "k_s", "v_s", "k_w", "v_w", "g_nsa", "q_b", "k_b", "v_b", "g_ma", "g_mb"])}
    wkv = np.concatenate([seg["k_s"], seg["k_w"], seg["k_b"], seg["v_s"], seg["v_w"], seg["v_b"], seg["k_c"], seg["v_c"]], axis=1)
    wq = np.concatenate([seg["q_a"], seg["q_b"], seg["g_nsa"], seg["g_ma"], seg["g_mb"]], axis=1)
    g = lambda n: np.asarray(inputs[n], np.float32)[0]
    common = {
        "wkv": _pmajor(wkv, 8), "wq": _pmajor(wq, 8),
        "g1": np.ascontiguousarray(g("norm1_g").reshape(8, 128).T), "g2": np.ascontiguousarray(g("norm2_g").reshape(8, 128).T),
        "gcolK": np.concatenate([np.tile(g("k_norm_slc"), 3), np.tile(g("k_norm_win"), 3), np.tile(g("k_norm_b"), 12)])[None, :].astype(np.float32),
        "gcolQ": np.concatenate([np.tile(g("q_norm_a"), 12), np.tile(g("q_norm_b"), 12)])[None, :].astype(np.float32),
        "gcmp": g("k_norm_cmp")[None, :].astype(np.float32),
        "ident": np.eye(128, dtype=np.float32),
        "expand": (np.arange(S)[None, :] // 64 == np.arange(64)[:, None]).astype(bf),
        "woa": _pmajor(g("w_o_a"), 6), "wob": _pmajor(g("w_o_b"), 2), "wout": _pmajor(g("w_out"), 8),
        "wup": _pmajor(g("w_up"), 8), "wdown": _pmajor(g("w_down"), 32),
        "w1k": _pmajor(g("cmp_k_w1"), 16), "w2k": _pmajor(g("cmp_k_w2"), 2),
        "w1v": _pmajor(g("cmp_v_w1"), 16), "w2v": _pmajor(g("cmp_v_w2"), 2),
        "posk": np.ascontiguousarray(g("cmp_k_pos").reshape(16, 128).T), "posv": np.ascontiguousarray(g("cmp_v_pos").reshape(16, 128).T),
    }
    cos, sin = _rope_tables(np.arange(S))
    common["cs_all"] = _tok_major(cos, NT)
    common["sn_all"] = _tok_major(sin, NT)
    c_end = np.arange(256) * 16 + 31
    cc, sc = _rope_tables(c_end)
    common["cs_cmp"] = _tok_major(cc, 2)
    common["sn_cmp"] = _tok_major(sc, 2)
    cs_ = np.arange(256)[:, None] * 16
    ss_ = np.arange(64)[None, :] * 64
    ov = np.clip(np.minimum(cs_ + 32, ss_ + 64) - np.maximum(cs_, ss_), 0, None).astype(np.float32) / 32.0
    ov1 = np.concatenate([ov, np.ones((256, 1), np.float32)], axis=1)
    ov1[255] = 0.0
    common["ov1"] = _tok_major(ov1, 2).astype(bf)

    in_maps = []
    for c in range(8):
        b, half = c // 2, c % 2
        qts = QT[half]
        own_idx = np.concatenate([np.arange(q * 512, (q + 1) * 512) for q in qts])
        m = dict(common)
        m["x_all"] = np.ascontiguousarray(x[b])
        m["x_own"] = np.ascontiguousarray(x[b][own_idx])
        m["cs_own"] = _tok_major(cos[own_idx], NOWN)
        m["sn_own"] = _tok_major(sin[own_idx], NOWN)
        cm = (c_end[:, None] <= own_idx[None, :]) & (np.arange(256)[:, None] < 255)
        cm = cm.reshape(2, 128, 4, 512).transpose(1, 2, 0, 3)
        m["cmask"] = np.ascontiguousarray(cm).astype(bf)
        t = own_idx
        cur = t // 64
        blk = np.arange(64)[None, :]
        forced = (blk == 0) | (blk == cur[:, None]) | (blk == cur[:, None] - 1)
        elig = blk <= cur[:, None]
        sA = (elig & ~forced).astype(np.float32)
        sB = np.where(forced, np.float32(1e30), np.where(elig, np.float32(0.0), np.float32(-1e30))).astype(np.float32)
        m["selA"] = np.ascontiguousarray(sA.reshape(4, 4, 128, 64).transpose(2, 0, 1, 3))
        m["selB"] = np.ascontiguousarray(sB.reshape(4, 4, 128, 64).transpose(2, 0, 1, 3))
        def tabs(L, fn):
            out = np.zeros((128, 2, L), np.float32)
            for p in range(2):
                delta = 512 * (qts[p] - 2 * p)
                out[:, p, :] = _toeplitz(L, delta, fn)
            return out.astype(bf)
        m["tbc"] = tabs(L_C, lambda d: d >= 0)
        m["tbw"] = tabs(L_W, lambda d: (d >= 0) & (d < 512))
        m["tbd0"] = tabs(L_D0, lambda d: (d >= 0) & (d <= 128))
        m["tbd1"] = tabs(L_D1, lambda d: (d >= 0) & (d <= 512) & (d % 4 == 0))
        m["tbd2"] = tabs(L_D2, lambda d: (d >= 0) & (d <= 2048) & (d % 16 == 0))
        in_maps.append(m)
    return in_maps


def kernel(**inputs):
    in_maps = host_prep(inputs)
    nc = build()
    res = run_bass_kernel_spmd(nc, in_maps, core_ids=list(range(8)))
    out = np.zeros((4, S, D), np.float32)
    for c in range(8):
        b, half = c // 2, c % 2
        o = np.asarray(res.results[c]["out"], np.float32)
        for s_, q in enumerate(QT[half]):
            out[b, q * 512:(q + 1) * 512] = o[s_ * 512:(s_ + 1) * 512]
    return out
```

```python
import contextlib
import math
import numpy as np
import ml_dtypes
import concourse.bass as bass
import concourse.mybir as mybir
from concourse.bass_utils import run_bass_kernel_spmd

F32 = mybir.dt.float32
BF16 = mybir.dt.bfloat16
AF = mybir.ActivationFunctionType
ALU = mybir.AluOpType
AX = mybir.AxisListType

S = 4096
D = 1024
NT = 32
NOWN = 16
EPS = 1e-6
SCALE = 0.125
QT = ([0, 3, 4, 7], [1, 2, 5, 6])
KV_COLS = 2688
Q_COLS = 3620
NEGSEL = -2048.0
L_C, L_W, L_D0, L_D1, L_D2 = 1408, 1920, 1536, 1920, 3456
SBUF_BYTES = 206 * 1024


class Res:
    __slots__ = ("name", "lw", "rd")

    def __init__(self, name=""):
        self.name = name
        self.lw = None
        self.rd = []


class Tile:
    def __init__(self, ap, name):
        self.ap = ap
        self.res = Res(name)

    def __getitem__(self, k):
        return self.ap[k]


class FW:
    ENG = ["pe", "act", "dve", "pool", "sp"]

    def __init__(self, nc, es):
        self.nc = nc
        self.es = es
        self.q = {e: [] for e in self.ENG}
        self.cnt = {e: 0 for e in self.ENG}
        self.sem = {e: es.enter_context(nc.semaphore("s_" + e)) for e in ["pe", "act", "dve", "pool"]}
        self.waited = {e: {} for e in self.ENG}
        self.dsem = {}

    @staticmethod
    def _r(x):
        return x.res if isinstance(x, Tile) else x

    def _deps(self, reads, writes):
        deps = []
        for r in reads:
            if r.lw is not None:
                deps.append(r.lw)
        for w in writes:
            deps.extend(w.rd)
            if w.lw is not None:
                deps.append(w.lw)
        return deps

    def _emit_waits(self, eng, deps):
        need = {}
        for (k, v) in deps:
            if k == eng and eng == "pe":
                continue
            if v > need.get(k, 0):
                need[k] = v
        for k, v in need.items():
            if self.waited[eng].get(k, 0) >= v:
                continue
            self.waited[eng][k] = v
            semh = self.sem[k] if k in self.sem else self.dsem[k][0]
            self.q[eng].append(("wait", semh, v))

    def _mark(self, key, seq, reads, writes):
        for r in reads:
            r.rd.append((key, seq))
            if len(r.rd) > 48:
                mx = {}
                for (k, v) in r.rd:
                    if v > mx.get(k, 0):
                        mx[k] = v
                r.rd = list(mx.items())
        for w in writes:
            w.lw = (key, seq)
            w.rd = []

    def op(self, eng, fn, reads=(), writes=()):
        return self.group(eng, [fn], reads, writes)

    def group(self, eng, fns, reads=(), writes=()):
        reads = [self._r(x) for x in reads]
        writes = [self._r(x) for x in writes]
        self._emit_waits(eng, self._deps(reads, writes))
        self.cnt[eng] += 1
        seq = self.cnt[eng]
        for f in fns[:-1]:
            self.q[eng].append(("op", f, None))
        self.q[eng].append(("op", fns[-1], self.sem[eng]))
        self._mark(eng, seq, reads, writes)
        return seq

    def dma(self, queue, fn, key, reads=(), writes=()):
        reads = [self._r(x) for x in reads]
        writes = [self._r(x) for x in writes]
        if key not in self.dsem:
            self.dsem[key] = [self.es.enter_context(self.nc.semaphore("d_" + key)), 0]
        self._emit_waits(queue, self._deps(reads, writes))
        ent = self.dsem[key]
        ent[1] += 16
        self.q[queue].append(("dma", fn, ent[0]))
        self._mark(key, ent[1], reads, writes)

    def barrier(self):
        allv = [(e, self.cnt[e]) for e in self.sem if self.cnt[e] > 0]
        allv += [(k, v[1]) for k, v in self.dsem.items() if v[1] > 0]
        for e in self.ENG:
            self._emit_waits(e, [(k, v) for (k, v) in allv if k != e])
            if e in self.sem and e != "pe" and self.cnt[e] > 0:
                self._emit_waits(e, [(e, self.cnt[e])])

    def final_wait(self, eng, ress):
        deps = []
        for r in ress:
            r = self._r(r)
            if r.lw is not None:
                deps.append(r.lw)
        self._emit_waits(eng, deps)

    def replay(self):
        nc = self.nc
        with nc.Block() as block:
            def run(engname):
                def f(e):
                    for it in self.q[engname]:
                        if it[0] == "wait":
                            e.wait_ge(it[1], it[2])
                        elif it[0] == "op":
                            ins = it[1](e)
                            if it[2] is not None:
                                ins.then_inc(it[2], 1)
                        else:
                            it[1](e).then_inc(it[2], 16)
                return f
            block.tensor(run("pe"))
            block.scalar(run("act"))
            block.vector(run("dve"))
            block.gpsimd(run("pool"))
            block.sync(run("sp"))


class Arena:
    def __init__(self, big, nbytes):
        self.big = big
        self.cap = nbytes
        self.top = 0
        self.hi = 0

    def t(self, name, shape, dt):
        esz = 2 if dt == BF16 else 4
        n = int(np.prod(shape))
        nb = (n * esz + 31) // 32 * 32
        off = self.top
        assert off + nb <= self.cap, (name, off, nb, self.cap)
        self.top = off + nb
        self.hi = max(self.hi, self.top)
        ap = self.big[:, off // 4:(off + nb) // 4]
        if dt == BF16:
            ap = ap.bitcast(BF16)
        ap = ap[:, 0:n]
        if len(shape) == 2:
            ap = ap.rearrange("p (a b) -> p a b", a=shape[0])
        elif len(shape) == 3:
            ap = ap.rearrange("p (a b c) -> p a b c", a=shape[0], b=shape[1])
        elif len(shape) == 4:
            ap = ap.rearrange("p (a b c d) -> p a b c d", a=shape[0], b=shape[1], c=shape[2])
        return Tile(ap, name)


def bc(ap, shape):
    return ap.to_broadcast(list(shape))


def MM(out, lhsT, rhs, start=True, stop=True):
    return lambda e: e.matmul(out, lhsT=lhsT, rhs=rhs, start=start, stop=stop)


def TT(out, in0, in1, op):
    return lambda e: e.tensor_tensor(out=out, in0=in0, in1=in1, op=op)


def TS(out, in0, s1, op0, s2=None, op1=None):
    if op1 is None:
        return lambda e: e.tensor_scalar(out=out, in0=in0, scalar1=s1, scalar2=None, op0=op0)
    return lambda e: e.tensor_scalar(out=out, in0=in0, scalar1=s1, scalar2=s2, op0=op0, op1=op1)


def ACTF(out, in_, func, scale=1.0, bias=None, accum=None):
    kw = {}
    if bias is not None:
        kw["bias"] = bias
    if accum is not None:
        kw["accum_out"] = accum
    return lambda e: e.activation(out=out, in_=in_, func=func, scale=scale, **kw)


def ACP(out, in_):
    return lambda e: e.copy(out=out, in_=in_)


def TC(out, in_):
    return lambda e: e.tensor_copy(out=out, in_=in_)


def TRN(out, in_, ident):
    return lambda e: e.transpose(out=out, in_=in_, identity=ident)


def DMA(out, in_):
    return lambda e: e.dma_start(out=out, in_=in_)


def MSET(ap, v):
    return lambda e: e.memset(ap, v)


def RECIP(out, in_):
    return lambda e: e.reciprocal(out=out, in_=in_)


def build(upto="all", debug=False):
    nc = bass.Bass("TRN2", target_bir_lowering=False)
    dbg_kind = "ExternalOutput" if debug else "Internal"

    def din(name, shape, dt=F32):
        return nc.dram_tensor(name, list(shape), dt, kind="ExternalInput").ap()

    def dscr(name, shape, dt=BF16):
        return nc.dram_tensor(name, list(shape), dt, kind=dbg_kind).ap()

    x_all = din("x_all", [S, D])
    x_own = din("x_own", [2048, D])
    wkv = din("wkv", [128, 8, KV_COLS])
    wq = din("wq", [128, 8, Q_COLS])
    g1 = din("g1", [128, 8])
    g2 = din("g2", [128, 8])
    gcolK = din("gcolK", [1, 1152])
    gcolQ = din("gcolQ", [1, 1536])
    gcmp = din("gcmp", [1, 64])
    cs_all = din("cs_all", [128, NT, 32])
    sn_all = din("sn_all", [128, NT, 32])
    cs_own = din("cs_own", [128, NOWN, 32])
    sn_own = din("sn_own", [128, NOWN, 32])
    cs_cmp = din("cs_cmp", [128, 2, 32])
    sn_cmp = din("sn_cmp", [128, 2, 32])
    ident = din("ident", [128, 128])
    expand = din("expand", [64, S], BF16)
    ov1 = din("ov1", [128, 2, 65], BF16)
    cmask = din("cmask", [128, 4, 2, 512], BF16)
    selA = din("selA", [128, 4, 4, 64])
    selB = din("selB", [128, 4, 4, 64])
    tbc = din("tbc", [128, 2, L_C], BF16)
    tbw = din("tbw", [128, 2, L_W], BF16)
    tbd0 = din("tbd0", [128, 2, L_D0], BF16)
    tbd1 = din("tbd1", [128, 2, L_D1], BF16)
    tbd2 = din("tbd2", [128, 2, L_D2], BF16)
    w1k = din("w1k", [128, 16, 256])
    w2k = din("w2k", [128, 2, 64])
    posk = din("posk", [128, 16])
    w1v = din("w1v", [128, 16, 256])
    w2v = din("w2v", [128, 2, 64])
    posv = din("posv", [128, 16])
    woa = din("woa", [128, 6, 1024])
    wob = din("wob", [128, 2, 1024])
    wout = din("wout", [128, 8, 1024])
    wup = din("wup", [128, 8, 4096])
    wdown = din("wdown", [128, 32, 1024])
    out = nc.dram_tensor("out", [2048, D], F32, kind="ExternalOutput").ap()

    KsT_scr = dscr("KsT_scr", [64, 3, S])
    KwT_scr = dscr("KwT_scr", [64, 3, S])
    KbT_scr = dscr("KbT_scr", [128, 6, S])
    V_scr = dscr("V_scr", [129, NT, 1554])
    QaT_scr = dscr("QaT_scr", [64, 12, 2048])
    QbT_scr = dscr("QbT_scr", [128, 6, 2048])
    gT_scr = dscr("gT_scr", [128, 16, 2048])
    yT_scr = dscr("yT_scr", [128, 8, 2048])
    h2T_scr = dscr("h2T_scr", [128, 8, 2048])
    dbg_gn = dscr("dbg_gn", [128, NOWN, 36], F32) if debug else None
    dbg_kc = dscr("dbg_kc", [128, 3, 256], BF16) if debug else None
    dbg_vc = dscr("dbg_vc", [128, 2, 3, 65], BF16) if debug else None

    with contextlib.ExitStack() as es:
        fw = FW(nc, es)
        big = es.enter_context(nc.sbuf_tensor("big", [128, SBUF_BYTES // 4], F32))
        psum = es.enter_context(nc.psum_tensor("psum", [128, 4096], F32))
        A = Arena(big, SBUF_BYTES)

        def bank(b, n=512, dt=F32, name="ps"):
            ap = psum[:, 512 * b:512 * b + 512]
            if dt == BF16:
                ap = ap.bitcast(BF16)
            return Tile(ap[:, 0:n], f"{name}{b}")

        idf = A.t("idf", [128], F32)
        idb = A.t("idb", [128], BF16)
        gn = A.t("gn", [NOWN, 36], F32)
        fw.dma("sp", lambda e: e.dma_start(out=idf[:], in_=ident), "idf", writes=[idf])
        fw.op("dve", lambda e: e.tensor_copy(out=idb[:], in_=idf[:]), reads=[idf], writes=[idb])
        stage_off = A.top
        stage = [A.t(f"wstage{i}", [2048], F32) for i in range(4)]
        stage_ctr = [0]

        def load_weight(dst_tile, dst_ap_fn, src_ap_fn, nchunks, width, gain=None, eng_cycle=("dve", "pool"), as3=None):
            for c in range(nchunks):
                i = stage_ctr[0]
                stage_ctr[0] += 1
                st = stage[i % 4]
                assert width <= 2048
                sview = st[:, 0:width] if as3 is None else st[:, 0:width].rearrange("p (a b) -> p a b", a=as3)
                fw.dma("sp", DMA(sview, src_ap_fn(c)), f"wstage{i % 4}", writes=[st])
                eng = ("dve", "act")[i % 2]
                if gain is None:
                    fw.op(eng, (TC if eng == "dve" else ACP)(dst_ap_fn(c), sview), reads=[st], writes=[dst_tile])
                elif eng == "dve":
                    fw.op(eng, TS(dst_ap_fn(c), sview, gain[:, c:c + 1], ALU.mult), reads=[st, gain], writes=[dst_tile])
                else:
                    fw.op(eng, ACTF(dst_ap_fn(c), sview, AF.Copy, scale=gain[:, c:c + 1]), reads=[st, gain], writes=[dst_tile])

        base_mark = A.top

        def proj_phase(which):
            A.top = base_mark
            isA = which == "A"
            ntile = NT if isA else NOWN
            ncols = KV_COLS if isA else Q_COLS
            nrope = 18 if isA else 24
            xsrc = x_all if isA else x_own
            W = A.t("W", [8, ncols], BF16)
            g1t = A.t("g1t", [8], F32)
            fw.dma("sp", lambda e: e.dma_start(out=g1t[:], in_=g1), "g1t", writes=[g1t])
            gcol = A.t("gcol", [nrope, 64], F32)
            gsrc = gcolK if isA else gcolQ
            fw.dma("sp", lambda e: e.dma_start(out=gcol[:].rearrange("p a b -> p (a b)"),
                                               in_=gsrc.partition_broadcast(128).rearrange("p a b -> p (a b)")),
                   "gcol", writes=[gcol])
            cs = A.t("cs", [ntile, 32], F32)
            sn = A.t("sn", [ntile, 32], F32)
            fw.dma("sp", lambda e: e.dma_start(out=cs[:], in_=cs_all if isA else cs_own), "cs", writes=[cs])
            fw.dma("sp", lambda e: e.dma_start(out=sn[:], in_=sn_all if isA else sn_own), "sn", writes=[sn])
            wsrc = wkv if isA else wq
            csplit = 1152 if isA else 1572
            W_lo, W_hi = Res("W_lo"), Res("W_hi")

            xin = [A.t(f"xin{i}", [D], F32) for i in range(2)]
            junk = A.t("junk", [D], BF16)
            hb = [A.t(f"hb{i}", [D], BF16) for i in range(2)]
            stat = [A.t(f"stat{i}", [4], F32) for i in range(2)]
            sq = [A.t(f"sq{i}", [512], F32) for i in range(2)]
            ssh = [A.t(f"ssh{i}", [nrope], F32) for i in range(2)]
            kn = [A.t(f"kn{i}", [nrope, 64], F32) for i in range(2)]
            tr = [A.t(f"tr{i}", [4, nrope, 32], F32) for i in range(2)]
            kr = [A.t(f"kr{i}", [nrope, 64], BF16) for i in range(2)]
            pTh = bank(7, 1024, BF16, "pTh")
            pTk = [bank(5, 1024, BF16, "pTkA"), bank(6, 1024, BF16, "pTkB")]
            nmm = 5
            pmm = [bank(b, 512, F32, "pmm") for b in range(nmm)]
            mmctr = [0]
            if isA:
                hT = [A.t(f"hT{i}", [8, 128], BF16) for i in range(2)]
                KTst = [A.t(f"KTst{i}", [12, 512], BF16) for i in range(2)]
                Vst = [A.t(f"Vst{i}", [4, 1554], BF16) for i in range(2)]
                for v_ in Vst:
                    fw.op("pool", MSET(v_[:], 1.0), writes=[v_])
            else:
                hT = [A.t(f"hTs{i}", [8, 512], BF16) for i in range(2)]
                QTst = [A.t(f"QTst{i}", [12, 512], BF16) for i in range(2)]
                gst = [A.t(f"gst{i}", [512], BF16) for i in range(4)]

            def s_load(i):
                t = xin[i % 2]
                fw.dma("sp", lambda e: e.dma_start(out=t[:], in_=xsrc[i * 128:(i + 1) * 128, :]), f"xin{i % 2}", writes=[t])

            def s_norm(i):
                p = i % 2
                x, st, h = xin[p], stat[p], hb[p]
                fw.op("act", lambda e: e.activation(out=junk[:], in_=x[:], func=AF.Square, accum_out=st[:, 0:1]),
                      reads=[x], writes=[junk, st])
                fw.op("act", lambda e: e.activation(out=st[:, 1:2], in_=st[:, 0:1], func=AF.Sqrt, scale=1.0 / D, bias=epsb[:, 0:1]),
                      reads=[st, epsb], writes=[st])
                fw.op("dve", lambda e: e.reciprocal(out=st[:, 2:3], in_=st[:, 1:2]), reads=[st], writes=[st])
                fw.op("dve", lambda e: e.tensor_scalar(out=h[:], in0=x[:], scalar1=st[:, 2:3], scalar2=None, op0=ALU.mult),
                      reads=[x, st], writes=[h])

            def s_transp_h(i):
                p = i % 2
                h = hb[p]
                fw.group("pe", [lambda e, kc=kc: e.transpose(out=pTh[:, kc * 128:(kc + 1) * 128], in_=h[:, kc * 128:(kc + 1) * 128],
                                                            identity=idb[:]) for kc in range(8)],
                         reads=[h, idb], writes=[pTh])
                if isA:
                    dst = hT[p]
                    fw.op("act", lambda e: e.copy(out=dst[:].rearrange("p a b -> p (a b)"), in_=pTh[:]), reads=[pTh], writes=[dst])
                else:
                    dst = hT[(i // 4) % 2]
                    sub = i % 4
                    fw.op("act", lambda e: e.copy(out=dst[:, :, sub * 128:(sub + 1) * 128],
                                                  in_=pTh[:].rearrange("p (a b) -> p a b", a=8)), reads=[pTh], writes=[dst])

            def lhs_of(i, kc):
                if isA:
                    return hT[i % 2][:, kc, :]
                return hT[(i // 4) % 2][:, kc, (i % 4) * 128:(i % 4 + 1) * 128]

            def hT_of(i):
                return hT[i % 2] if isA else hT[(i // 4) % 2]

            def mm_block(i, c0, w):
                pm = pmm[mmctr[0] % nmm]
                mmctr[0] += 1
                fw.group("pe", [lambda e, kc=kc: e.matmul(pm[:, 0:w], lhsT=lhs_of(i, kc), rhs=W[:, kc, c0:c0 + w],
                                                          start=(kc == 0), stop=(kc == 7)) for kc in range(8)],
                         reads=[hT_of(i), W_lo if c0 + w <= csplit else W_hi], writes=[pm])
                return pm

            def rope_epilogue(i, blocks):
                p = i % 2
                k_n, s_h, t_r, k_r, s_q = kn[p], ssh[p], tr[p], kr[p], sq
                for bi, (pm, h0, nh) in enumerate(blocks):
                    sqt = s_q[bi % 2]
                    w = nh * 64
                    fw.op("act", lambda e, pm=pm, sqt=sqt, w=w: e.activation(out=sqt[:, 0:w], in_=pm[:, 0:w], func=AF.Square),
                          reads=[pm], writes=[sqt])
                    fw.op("dve", lambda e, sqt=sqt, h0=h0, nh=nh, w=w: e.tensor_reduce(
                        out=s_h[:, h0:h0 + nh], in_=sqt[:, 0:w].rearrange("p (a b) -> p a b", a=nh), axis=AX.X, op=ALU.add),
                        reads=[sqt], writes=[s_h])
                fw.op("act", lambda e: e.activation(out=s_h[:], in_=s_h[:], func=AF.Sqrt, scale=1.0 / 64, bias=epsb[:, 0:1]),
                      reads=[s_h, epsb], writes=[s_h])
                fw.op("dve", lambda e: e.reciprocal(out=s_h[:], in_=s_h[:]), reads=[s_h], writes=[s_h])
                for (pm, h0, nh) in blocks:
                    w = nh * 64
                    fw.op("dve", lambda e, pm=pm, h0=h0, nh=nh, w=w: e.tensor_tensor(
                        out=k_n[:, h0:h0 + nh, :], in0=pm[:, 0:w].rearrange("p (a b) -> p a b", a=nh),
                        in1=bc(s_h[:, h0:h0 + nh].unsqueeze(2), [128, nh, 64]), op=ALU.mult),
                        reads=[pm, s_h], writes=[k_n])
                fw.op("dve", lambda e: e.tensor_tensor(out=k_n[:], in0=k_n[:], in1=gcol[:], op=ALU.mult),
                      reads=[k_n, gcol], writes=[k_n])
                cosb = bc(cs[:, i, :].unsqueeze(1), [128, nrope, 32])
                sinb = bc(sn[:, i, :].unsqueeze(1), [128, nrope, 32])
                x1 = k_n[:, :, 0:32]
                x2 = k_n[:, :, 32:64]
                fw.op("pool", lambda e: e.tensor_tensor(out=t_r[:, 0], in0=x1, in1=cosb, op=ALU.mult), reads=[k_n, cs], writes=[t_r])
                fw.op("pool", lambda e: e.tensor_tensor(out=t_r[:, 1], in0=x2, in1=sinb, op=ALU.mult), reads=[k_n, sn], writes=[t_r])
                fw.op("pool", lambda e: e.tensor_tensor(out=t_r[:, 2], in0=x2, in1=cosb, op=ALU.mult), reads=[k_n, cs], writes=[t_r])
                fw.op("pool", lambda e: e.tensor_tensor(out=t_r[:, 3], in0=x1, in1=sinb, op=ALU.mult), reads=[k_n, sn], writes=[t_r])
                fw.op("pool", lambda e: e.tensor_tensor(out=k_r[:, :, 0:32], in0=t_r[:, 0], in1=t_r[:, 1], op=ALU.subtract),
                      reads=[t_r], writes=[k_r])
                fw.op("pool", lambda e: e.tensor_tensor(out=k_r[:, :, 32:64], in0=t_r[:, 2], in1=t_r[:, 3], op=ALU.add),
                      reads=[t_r], writes=[k_r])

            def s_mm_A(i):
                blocks = []
                for (c0, w, h0, nh) in [(0, 512, 0, 8), (512, 512, 8, 8), (1024, 128, 16, 2)]:
                    blocks.append((mm_block(i, c0, w), h0, nh))
                rope_epilogue(i, blocks)
                vs = Vst[(i // 4) % 2]
                t4 = i % 4
                pm = mm_block(i, 1152, 512)
                fw.op("act", ACP(vs[:, t4, 0:520].rearrange("p (h c) -> p h c", c=65)[:, :, 0:64], pm[:].rearrange("p (h c) -> p h c", c=64)),
                      reads=[pm], writes=[vs])
                pm = mm_block(i, 1664, 512)
                fw.op("dve", TC(vs[:, t4, 520:1040].rearrange("p (h c) -> p h c", c=65)[:, :, 0:64], pm[:].rearrange("p (h c) -> p h c", c=64)),
                      reads=[pm], writes=[vs])
                pm = mm_block(i, 2176, 512)
                fw.op("act", ACP(vs[:, t4, 1040:1170].rearrange("p (h c) -> p h c", c=65)[:, :, 0:64], pm[:, 0:128].rearrange("p (h c) -> p h c", c=64)),
                      reads=[pm], writes=[vs])
                fw.op("act", ACP(vs[:, t4, 1170:1554], pm[:, 128:512]), reads=[pm], writes=[vs])
                if i % 4 == 3:
                    tb = i // 4
                    fw.dma("sp", DMA(V_scr[0:128, 4 * tb:4 * tb + 4, :], vs[:]), f"Vst{(i // 4) % 2}", reads=[vs], writes=[r_Vscr])

            def s_transp_k_A(i):
                p = i % 2
                k_r = kr[p]
                st = KTst[(i // 4) % 2]
                sub = i % 4
                fns = []
                for j in range(6):
                    fns.append(lambda e, j=j: e.transpose(out=pTk[0][0:64, j * 128:(j + 1) * 128], in_=k_r[:, j, :], identity=idb[:]))
                for j in range(2):
                    fns.append(lambda e, j=j: e.transpose(out=pTk[0][:, (6 + j) * 128:(7 + j) * 128],
                                                          in_=k_r[:, 6 + 2 * j:8 + 2 * j, :].rearrange("p a b -> p (a b)"), identity=idb[:]))
                fw.group("pe", fns, reads=[k_r, idb], writes=[pTk[0]])
                fns = []
                for j in range(2, 6):
                    fns.append(lambda e, j=j: e.transpose(out=pTk[1][:, (j - 2) * 128:(j - 1) * 128],
                                                          in_=k_r[:, 6 + 2 * j:8 + 2 * j, :].rearrange("p a b -> p (a b)"), identity=idb[:]))
                fw.group("pe", fns, reads=[k_r, idb], writes=[pTk[1]])
                cols = slice(sub * 128, (sub + 1) * 128)
                fw.op("act", lambda e: e.copy(out=st[0:64, 0:6, cols], in_=pTk[0][0:64, 0:768].rearrange("p (a b) -> p a b", a=6)),
                      reads=[pTk[0]], writes=[st])
                fw.op("dve", lambda e: e.tensor_copy(out=st[:, 6:8, cols], in_=pTk[0][:, 768:1024].rearrange("p (a b) -> p a b", a=2)),
                      reads=[pTk[0]], writes=[st])
                fw.op("act", lambda e: e.copy(out=st[:, 8:12, cols], in_=pTk[1][:, 0:512].rearrange("p (a b) -> p a b", a=4)),
                      reads=[pTk[1]], writes=[st])
                if sub == 3:
                    t0 = (i // 4) * 512
                    key = f"KTst{(i // 4) % 2}"
                    fw.dma("sp", lambda e: e.dma_start(out=KsT_scr[:, :, t0:t0 + 512], in_=st[0:64, 0:3, :]), key, reads=[st], writes=[r_Kscr])
                    fw.dma("sp", lambda e: e.dma_start(out=KwT_scr[:, :, t0:t0 + 512], in_=st[0:64, 3:6, :]), key, reads=[st], writes=[r_Kscr])
                    fw.dma("sp", lambda e: e.dma_start(out=KbT_scr[:, :, t0:t0 + 512], in_=st[:, 6:12, :]), key, reads=[st], writes=[r_Kscr])

            def s_mm_B(i):
                blocks = []
                for j in range(3):
                    blocks.append((mm_block(i, 512 * j, 512), 8 * j, 8))
                rope_epilogue(i, blocks)
                pmg = mm_block(i, 1536, 36)
                fw.op("act", lambda e: e.activation(out=gn[:, i, :], in_=pmg[:, 0:36], func=AF.Sigmoid), reads=[pmg], writes=[gn])
                if i % 4 == 3:
                    slot = i // 4
                    hTs = hT[slot % 2]
                    for cc in range(16):
                        pm = pmm[mmctr[0] % nmm]
                        mmctr[0] += 1
                        c0 = 1572 + cc * 128
                        fw.group("pe", [lambda e, kc=kc, pm=pm, c0=c0: e.matmul(pm[:], lhsT=W[:, kc, c0:c0 + 128], rhs=hTs[:, kc, :],
                                                                              start=(kc == 0), stop=(kc == 7)) for kc in range(8)],
                                 reads=[hTs, W_hi], writes=[pm])
                        g = gst[cc % 4]
                        fw.op("act", lambda e, pm=pm, g=g: e.activation(out=g[:], in_=pm[:], func=AF.Sigmoid), reads=[pm], writes=[g])
                        fw.dma("sp", lambda e, g=g, cc=cc: e.dma_start(out=gT_scr[:, cc, slot * 512:(slot + 1) * 512], in_=g[:]),
                               f"gst{cc % 4}", reads=[g], writes=[r_gscr])

            def s_transp_q_B(i):
                p = i % 2
                k_r = kr[p]
                st = QTst[(i // 4) % 2]
                sub = i % 4
                for half in range(2):
                    fns = [lambda e, j=j, half=half: e.transpose(out=pTk[half][:, j * 128:(j + 1) * 128],
                                                                 in_=k_r[:, 12 * half + 2 * j:12 * half + 2 * j + 2, :].rearrange("p a b -> p (a b)"),
                                                                 identity=idb[:]) for j in range(6)]
                    fw.group("pe", fns, reads=[k_r, idb], writes=[pTk[half]])
                    eng = "act" if half == 0 else "dve"
                    dst = st[:, 6 * half:6 * half + 6, sub * 128:(sub + 1) * 128]
                    src = pTk[half][:, 0:768].rearrange("p (a b) -> p a b", a=6)
                    if eng == "act":
                        fw.op("act", lambda e, dst=dst, src=src: e.copy(out=dst, in_=src), reads=[pTk[half]], writes=[st])
                    else:
                        fw.op("dve", lambda e, dst=dst, src=src: e.tensor_copy(out=dst, in_=src), reads=[pTk[half]], writes=[st])
                if sub == 3:
                    t0 = (i // 4) * 512
                    key = f"QTst{(i // 4) % 2}"
                    qa_v = QaT_scr.rearrange("p (j two) t -> p j two t", two=2)
                    fw.dma("sp", lambda e: e.dma_start(out=qa_v[:, :, 0, t0:t0 + 512], in_=st[0:64, 0:6, :]), key, reads=[st], writes=[r_Qscr])
                    fw.dma("sp", lambda e: e.dma_start(out=qa_v[:, :, 1, t0:t0 + 512], in_=st[64:128, 0:6, :]), key, reads=[st], writes=[r_Qscr])
                    fw.dma("sp", lambda e: e.dma_start(out=QbT_scr[:, :, t0:t0 + 512], in_=st[:, 6:12, :]), key, reads=[st], writes=[r_Qscr])

            s_mm = s_mm_A if isA else s_mm_B
            s_tk = s_transp_k_A if isA else s_transp_q_B
            s_load(0)
            s_load(1)
            load_weight(W_lo, lambda c: W[:, c, 0:csplit], lambda c: wsrc[:, c, 0:csplit], 8, csplit, gain=g1t)
            s_norm(0)
            s_load(2)
            s_transp_h(0)
            s_norm(1)
            load_weight(W_hi, lambda c: W[:, c, csplit:ncols], lambda c: wsrc[:, c, csplit:ncols], 8, ncols - csplit, gain=g1t)
            for i in range(ntile):
                if i + 1 < ntile:
                    s_transp_h(i + 1)
                if i + 2 < ntile:
                    s_norm(i + 2)
                if i + 3 < ntile:
                    s_load(i + 3)
                s_mm(i)
                if i >= 1:
                    s_tk(i - 1)
            s_tk(ntile - 1)
            fw.barrier()

        def cmp_phase(kcT, vc1):
            C2 = 2.0 * math.sqrt(2.0 / math.pi)
            w1b = A.t("w1b", [16, 256], BF16)
            w2b = A.t("w2b", [2, 64], BF16)
            posb = A.t("posb", [16], BF16)
            w2f = A.t("w2f", [2, 64], F32)
            posf = A.t("posf", [16], F32)
            gc = A.t("gc", [64], F32)
            csc = A.t("csc", [2, 32], F32)
            snc = A.t("snc", [2, 32], F32)
            fw.dma("sp", DMA(gc[:], gcmp.partition_broadcast(128).rearrange("p a b -> p (a b)")), "gc", writes=[gc])
            fw.dma("sp", DMA(csc[:], cs_cmp), "csc", writes=[csc])
            fw.dma("sp", DMA(snc[:], sn_cmp), "snc", writes=[snc])
            kc2 = A.t("kc2", [NT, 3, 2, 64], BF16)
            kT2 = A.t("kT2", [3, S], BF16)
            kcv = A.t("kcv", [NT, 384], BF16)
            kcv_sh = A.t("kcv_sh", [NT, 384], BF16)
            fw.dma("sp", DMA(kcv[:], V_scr[0:128, :, 1170:1554]), "kcv", reads=[r_Vscr], writes=[kcv])
            fw.dma("sp", DMA(kcv_sh[:, :, :], V_scr[1:129, :, 1170:1554]), "kcvsh", reads=[r_Vscr], writes=[kcv_sh])
            fw.dma("sp", DMA(kcv_sh[127:128, 0:NT - 1, :], V_scr[0:1, 1:NT, 1170:1554]), "kcvsh", reads=[r_Vscr], writes=[kcv_sh])
            fw.dma("sp", DMA(kcv_sh[127:128, NT - 1, :], expand[0:1, 64:448]), "kcvsh", writes=[kcv_sh])
            biasT = A.t("biasT", [2], F32)
            xs = A.t("xs", [256], F32)
            x2 = A.t("x2", [256], F32)
            uu = A.t("uu", [256], F32)
            sg = A.t("sg", [256], F32)
            gTs = [A.t(f"gTc{i}", [2, 256], BF16) for i in range(2)]
            kcrs = [A.t(f"kcr{i}", [2, 128], BF16) for i in range(2)]
            kcn = A.t("kcn", [2, 64], F32)
            ktr = A.t("ktr", [4, 2, 32], F32)
            sqc = A.t("sqc", [2, 64], F32)
            stc = A.t("stc", [4], F32)
            for t_ in gTs + kcrs:
                fw.op("pool", MSET(t_[:], 0.0), writes=[t_])
            fw.op("pool", MSET(vc1[:], 1.0), writes=[vc1])
            pT = [bank(0, 1024, BF16, "cpT"), bank(1, 1024, BF16, "cpT")]
            pH = [bank(2, 512, F32, "cpH"), bank(3, 512, F32, "cpH")]
            pB = bank(4, 512, F32, "cpB")
            pK = bank(5, 512, F32, "cpK")
            pKT = bank(6, 1024, BF16, "cpKT")
            for kv in range(2):
                colbase = 1152 + 192 * kv
                w1src, w2src, possrc = (w1k, w2k, posk) if kv == 0 else (w1v, w2v, posv)
                load_weight(w1b, lambda c: w1b[:, 8 * c:8 * c + 8, :], lambda c, w1src=w1src: w1src[:, 8 * c:8 * c + 8, :], 2, 2048, as3=8)
                fw.dma("sp", DMA(w2f[:], w2src), "w2f", writes=[w2f])
                fw.dma("sp", DMA(posf[:], possrc), "posf", writes=[posf])
                fw.op("dve", TC(w2b[:], w2f[:]), reads=[w2f], writes=[w2b])
                fw.op("dve", TC(posb[:], posf[:]), reads=[posf], writes=[posb])
                fw.op("dve", TC(kc2[:, :, :, 0, :], kcv[:, :, 192 * kv:192 * kv + 192].rearrange("p t (g d) -> p t g d", g=3)), reads=[kcv], writes=[kc2])
                fw.op("act", ACP(kc2[:, :, :, 1, :], kcv_sh[:, :, 192 * kv:192 * kv + 192].rearrange("p t (g d) -> p t g d", g=3)), reads=[kcv_sh], writes=[kc2])
                n = 0
                for g in range(3):
                    for tb in range(4):
                        p = pT[n % 2]
                        n += 1
                        fw.group("pe", [TRN(p[:, j * 128:(j + 1) * 128], kc2[:, tb * 8 + j, g, :, :].rearrange("p a b -> p (a b)"), idb[:])
                                        for j in range(8)], reads=[kc2, idb], writes=[p])
                        if n % 2 == 0:
                            fw.op("act", ACP(kT2[:, g, tb * 1024:(tb + 1) * 1024], p[:]), reads=[p], writes=[kT2])
                        else:
                            fw.op("dve", TC(kT2[:, g, tb * 1024:(tb + 1) * 1024], p[:]), reads=[p], writes=[kT2])
                for hc in range(2):
                    fw.group("pe", [MM(pB[:, hc:hc + 1], w1b[:, j, hc * 128:(hc + 1) * 128], posb[:, j:j + 1], start=(j == 0), stop=(j == 15))
                                    for j in range(16)], reads=[w1b, posb], writes=[pB])
                fw.op("dve", TC(biasT[:], pB[:, 0:2]), reads=[pB], writes=[biasT])
                def hidden(g):
                    gTg = gTs[g % 2]
                    for hc in range(2):
                        ph = pH[hc]
                        fw.group("pe", [MM(ph[:, 0:255], w1b[:, j, hc * 128:(hc + 1) * 128], kT2[:, g, 2 * j:2 * j + 16 * 254 + 1:16],
                                           start=(j == 0), stop=(j == 15)) for j in range(16)], reads=[w1b, kT2], writes=[ph])
                        fw.op("dve", TS(xs[:, 0:255], ph[:, 0:255], biasT[:, hc:hc + 1], ALU.add), reads=[ph, biasT], writes=[xs])
                        fw.op("pool", TT(x2[:, 0:255], xs[:, 0:255], xs[:, 0:255], ALU.mult), reads=[xs], writes=[x2])
                        fw.op("dve", TS(x2[:, 0:255], x2[:, 0:255], 0.044715, ALU.mult, 1.0, ALU.add), reads=[x2], writes=[x2])
                        fw.op("pool", TT(uu[:, 0:255], x2[:, 0:255], xs[:, 0:255], ALU.mult), reads=[x2, xs], writes=[uu])
                        fw.op("act", ACTF(sg[:, 0:255], uu[:, 0:255], AF.Sigmoid, scale=C2), reads=[uu], writes=[sg])
                        fw.op("dve", TT(gTg[:, hc, 0:255], xs[:, 0:255], sg[:, 0:255], ALU.mult), reads=[xs, sg], writes=[gTg])

                def second(g, kv=kv):
                    gTg = gTs[g % 2]
                    kcr_g = kcrs[g % 2]
                    for ct in range(2):
                        fw.group("pe", [MM(pK[:, ct * 64:(ct + 1) * 64], gTg[:, hc, ct * 128:(ct + 1) * 128], w2b[:, hc, :],
                                           start=(hc == 0), stop=(hc == 1)) for hc in range(2)], reads=[gTg, w2b], writes=[pK])
                    if kv == 1:
                        fw.op("act", ACP(vc1[:, :, g, 0:64], pK[:, 0:128].rearrange("p (a b) -> p a b", a=2)), reads=[pK], writes=[vc1])
                        return None
                    pk3 = pK[:, 0:128].rearrange("p (a b) -> p a b", a=2)
                    fw.op("act", ACTF(sqc[:], pk3, AF.Square), reads=[pK], writes=[sqc])
                    fw.op("dve", lambda e: e.tensor_reduce(out=stc[:, 0:2], in_=sqc[:], axis=AX.X, op=ALU.add), reads=[sqc], writes=[stc])
                    fw.op("act", ACTF(stc[:, 0:2], stc[:, 0:2], AF.Sqrt, scale=1.0 / 64, bias=epsb[:, 0:1]), reads=[stc, epsb], writes=[stc])
                    fw.op("dve", RECIP(stc[:, 0:2], stc[:, 0:2]), reads=[stc], writes=[stc])
                    fw.op("dve", TT(kcn[:], pk3, bc(stc[:, 0:2].unsqueeze(2), [128, 2, 64]), ALU.mult), reads=[pK, stc], writes=[kcn])
                    fw.op("pool", TT(kcn[:], kcn[:], bc(gc[:].unsqueeze(1), [128, 2, 64]), ALU.mult), reads=[kcn, gc], writes=[kcn])
                    x1, xx2 = kcn[:, :, 0:32], kcn[:, :, 32:64]
                    fw.op("pool", TT(ktr[:, 0], x1, csc[:], ALU.mult), reads=[kcn, csc], writes=[ktr])
                    fw.op("pool", TT(ktr[:, 1], xx2, snc[:], ALU.mult), reads=[kcn, snc], writes=[ktr])
                    fw.op("pool", TT(ktr[:, 2], xx2, csc[:], ALU.mult), reads=[kcn, csc], writes=[ktr])
                    fw.op("pool", TT(ktr[:, 3], x1, snc[:], ALU.mult), reads=[kcn, snc], writes=[ktr])
                    fw.op("pool", TT(kcr_g[:, :, 0:32], ktr[:, 0], ktr[:, 1], ALU.subtract), reads=[ktr], writes=[kcr_g])
                    fw.op("pool", TT(kcr_g[:, :, 32:64], ktr[:, 2], ktr[:, 3], ALU.add), reads=[ktr], writes=[kcr_g])

                    def trp(g=g, kcr_g=kcr_g):
                        fw.group("pe", [TRN(pKT[:, ct * 128:(ct + 1) * 128], kcr_g[:, ct, :], idb[:]) for ct in range(2)],
                                 reads=[kcr_g, idb], writes=[pKT])
                        fw.op("act", ACP(kcT[:, g, :], pKT[:, 0:256]), reads=[pKT], writes=[kcT])
                    return trp

                trq = []
                for g in range(3):
                    hidden(g)
                    if trq:
                        trq.pop(0)()
                    if g >= 1:
                        t_ = second(g - 1)
                        if t_ is not None:
                            trq.append(t_)
                t_ = second(2)
                if t_ is not None:
                    trq.append(t_)
                while trq:
                    trq.pop(0)()
            fw.barrier()

        def share(tile, ap, name):
            t = Tile(ap, name)
            t.res = tile.res
            return t

        def attn_units(units, q_of, k_of, v_of, mask_of, pO_of, first_of, last_of, pS, Pt, ctr, kqv, hooks=None):
            LAG = 3
            pend = []
            Kres, Qres, Vres = kqv
            hooks = hooks or {}

            def emit_pv(u, pt):
                po = pO_of(u)
                fw.group("pe", [MM(po[0:65, :], v_of(u), pt[:], start=first_of(u), stop=last_of(u))],
                         reads=[pt, Vres(u) if callable(Vres) else Vres], writes=[po])

            for ui, u in enumerate(units):
                if ui in hooks:
                    hooks[ui]()
                ps = pS[ctr[0] % len(pS)]
                pt = Pt[ctr[0] % len(Pt)]
                ctr[0] += 1
                fw.group("pe", [MM(ps[:], k_of(u), q_of(u))], reads=[Kres(u) if callable(Kres) else Kres, Qres], writes=[ps])
                fw.op("act", ACTF(pt[:], ps[:], AF.Exp, scale=SCALE), reads=[ps], writes=[pt])
                m = mask_of(u)
                if m is not None:
                    fw.op("dve", TT(pt[:], pt[:], m[0], ALU.mult), reads=[pt, m[1]], writes=[pt])
                pend.append((u, pt))
                if len(pend) > LAG:
                    emit_pv(*pend.pop(0))
            while pend:
                emit_pv(*pend.pop(0))

        def finalize(pO_list, heads_cols, coef_fn, yacc, first_branch, OTsb, pTf, s, rd4, ytmp, defer=False):
            for i, po in enumerate(pO_list):
                ot = OTsb[i % len(OTsb)]
                fw.op("dve", TC(ot[0:65, :], po[0:65, :]), reads=[po], writes=[ot])
            parts = []
            for i, po in enumerate(pO_list):
                parts.append(lambda i=i: fin_head(i, pO_list, heads_cols, coef_fn, yacc, first_branch, OTsb, pTf, rd4, ytmp))
            if defer:
                return parts
            for p in parts:
                p()
            return []

        def fin_head(i, pO_list, heads_cols, coef_fn, yacc, first_branch, OTsb, pTf, rd4, ytmp):
            if True:
                ot = OTsb[i % len(OTsb)]
                fw.group("pe", [TRN(pTf[:, sub * 65:(sub + 1) * 65], ot[0:65, sub * 128:(sub + 1) * 128], idf[0:65, 0:65]) for sub in range(4)],
                         reads=[ot, idf], writes=[pTf])
                p3 = pTf[:, 0:260].rearrange("p (a b) -> p a b", a=4)
                fw.op("dve", TS(rd4[:], p3[:, :, 64], 1e-30, ALU.max), reads=[pTf], writes=[rd4])
                fw.op("dve", RECIP(rd4[:], rd4[:]), reads=[rd4], writes=[rd4])
                gate = coef_fn(i)
                if gate is not None:
                    fw.op("dve", TT(rd4[:], rd4[:], gate, ALU.mult), reads=[rd4, gn], writes=[rd4])
                hc = heads_cols[i]
                cb = bc(rd4[:].unsqueeze(2), [128, 4, 64])
                if first_branch:
                    fw.op("dve", TT(yacc[:, :, hc, :], p3[:, :, 0:64], cb, ALU.mult), reads=[pTf, rd4], writes=[yacc])
                else:
                    fw.op("dve", TT(ytmp[:], p3[:, :, 0:64], cb, ALU.mult), reads=[pTf, rd4], writes=[ytmp])
                    fw.op("pool", TT(yacc[:, :, hc, :], yacc[:, :, hc, :], ytmp[:], ALU.add), reads=[ytmp, yacc], writes=[yacc])

        def y_to_scratch(yacc, nchunk, chunk0, s, ybf, yst, pY):
            fw.op("dve", TC(ybf[:], yacc[:].rearrange("p a h d -> p a (h d)")), reads=[yacc], writes=[ybf])
            for c in range(nchunk):
                pb = pY[c // 2]
                fw.group("pe", [TRN(pb[:, (c % 2) * 512 + sub * 128:(c % 2) * 512 + (sub + 1) * 128], ybf[:, sub, c * 128:(c + 1) * 128], idb[:])
                                for sub in range(4)], reads=[ybf, idb], writes=[pb])
            for c2 in range((nchunk + 1) // 2):
                n2 = min(2, nchunk - 2 * c2)
                src = pY[c2][:, 0:n2 * 512].rearrange("p (a b) -> p a b", a=n2)
                if c2 % 2 == 0:
                    fw.op("act", ACP(yst[:, 2 * c2:2 * c2 + n2, :], src), reads=[pY[c2]], writes=[yst])
                else:
                    fw.op("dve", TC(yst[:, 2 * c2:2 * c2 + n2, :], src), reads=[pY[c2]], writes=[yst])
            fw.dma("sp", DMA(yT_scr[:, chunk0:chunk0 + nchunk, s * 512:(s + 1) * 512], yst[:, 0:nchunk, :]), "yst", reads=[yst], writes=[r_yscr])

        def nsa_phase(kcT, vc1):
            KsT = A.t("KsT", [3, S], BF16)
            KwT = A.t("KwT", [3, S], BF16)
            Vs1 = A.t("Vs1", [NT, 3, 65], BF16)
            Vw1 = A.t("Vw1", [NT, 3, 65], BF16)
            tbc_t = A.t("tbc_t", [2, L_C], BF16)
            tbw_t = A.t("tbw_t", [2, L_W], BF16)
            selA_t = A.t("selA_t", [4, 4, 64], F32)
            selB_t = A.t("selB_t", [4, 4, 64], F32)
            ov1b = A.t("ov1b", [2, 65], BF16)
            Qa_bufs = [A.t("Qa", [12, 512], BF16),
                       Tile(big[:, stage_off // 4:stage_off // 4 + 3072].bitcast(BF16).rearrange("p (a b) -> p a b", a=12), "Qa2")]
            cm = A.t("cm", [2, 512], BF16)
            E8 = [A.t(f"E8_{i}", [512], BF16) for i in range(8)]
            cctr = [0]
            Pt = [A.t(f"Pt{i}", [512], BF16) for i in range(6)]
            OTsb = [A.t(f"OTsb{i}", [512], F32) for i in range(4)]
            yacc = A.t("yacc", [4, 12, 64], F32)
            ybf = A.t("ybf", [4, 768], BF16)
            yst = A.t("yst", [6, 512], BF16)
            imp = A.t("imp", [4, 64], F32)
            itmp = A.t("itmp", [4, 64], F32)
            score = A.t("score", [4, 64], F32)
            sc2 = A.t("sc2", [4, 64], F32)
            m8 = A.t("m8", [2, 8], F32)
            negsel_g = [A.t(f"negsel{g}", [4, 128], BF16) for g in range(3)]
            rd4 = A.t("rd4", [4], F32)
            ytmp = A.t("ytmp", [4, 64], F32)
            fw.op("pool", MSET(KwT[64:128, :, :], 0.0), writes=[KwT])
            Qg2 = [[Res(f"Qg{b}_{g}") for g in range(3)] for b in range(2)]
            for b_ in range(2):
                fw.op("pool", MSET(Qa_bufs[b_][64:128, :, :], 0.0), writes=Qg2[b_])

            def load_qa(s_):
                fw.dma("sp", DMA(Qa_bufs[s_ % 2][0:64, :, :], QaT_scr[:, :, s_ * 512:(s_ + 1) * 512]), f"Qal{s_ % 2}", reads=[r_Qscr], writes=Qg2[s_ % 2])
            for ng in negsel_g:
                fw.op("pool", MSET(ng[:], 0.0), writes=[ng])
            fw.dma("sp", DMA(tbc_t[:], tbc), "tbc", writes=[tbc_t])
            fw.dma("sp", DMA(tbw_t[:], tbw), "tbw", writes=[tbw_t])
            fw.dma("sp", DMA(selA_t[:], selA), "selA", writes=[selA_t])
            fw.dma("sp", DMA(selB_t[:], selB), "selB", writes=[selB_t])
            fw.dma("sp", DMA(ov1b[:], ov1), "ov1b", writes=[ov1b])
            pO = [bank(b, 512, F32, "pO") for b in range(4)]
            pS = [bank(b, 512, F32, "pS") for b in (4, 5, 6)]
            b7 = bank(7, 512, F32, "b7")
            b7b = share(b7, psum[:, 512 * 7:512 * 8].bitcast(BF16), "b7b")
            pY = [share(pS[i], psum[:, 512 * (4 + i):512 * (5 + i)].bitcast(BF16), "pY") for i in range(3)]
            ctr = [0]

            for s in range(4):
                par = s % 2
                Qa, Qg = Qa_bufs[s % 2], Qg2[s % 2]
                if s == 0:
                    load_qa(0)
                fw.dma("sp", DMA(cm[:], cmask[:, s]), "cml", writes=[cm])
                if s == 0:
                    for g in range(3):
                        fw.dma("sp", DMA(KsT[0:64, g, :], KsT_scr[:, g, :]), "KsTl", reads=[r_Kscr], writes=[Res()])
                        fw.dma("sp", DMA(KsT[64:128, g, :], expand), "KsTl", writes=[KsT] if g == 2 else [Res()])
                        fw.dma("sp", DMA(KwT[0:64, g, :], KwT_scr[:, g, :]), "KwTl", reads=[r_Kscr], writes=[KwT] if g == 2 else [Res()])
                    for hq in range(2):
                        fw.dma("sp", DMA(Vs1[:, 16 * hq:16 * hq + 16].rearrange("p t g c -> p t (g c)"), V_scr[0:128, 16 * hq:16 * hq + 16, 0:195]), "Vs1l",
                               reads=[r_Vscr], writes=[Vs1] if hq == 1 else [Res()])
                        fw.dma("sp", DMA(Vw1[:, 16 * hq:16 * hq + 16].rearrange("p t g c -> p t (g c)"), V_scr[0:128, 16 * hq:16 * hq + 16, 195:390]), "Vw1l",
                               reads=[r_Vscr], writes=[Vw1] if hq == 1 else [Res()])
                if s + 1 < 4:
                    load_qa(s + 1)
                def select(g):
                    negsel = negsel_g[g]
                    fw.op("dve", TT(score[:], imp[:], selA_t[:, s], ALU.mult), reads=[imp, selA_t], writes=[score])
                    fw.op("dve", TT(score[:], score[:], selB_t[:, s], ALU.add), reads=[score, selB_t], writes=[score])
                    for sub in range(4):
                        fw.op("dve", lambda e, sub=sub: e.max(out=m8[:, 0, :], in_=score[:, sub, :]), reads=[score], writes=[m8])
                        fw.op("dve", lambda e, sub=sub: e.match_replace(out=sc2[:, sub, :], in_to_replace=m8[:, 0, :], in_values=score[:, sub, :],
                                                                        imm_value=-3.0e38), reads=[score, m8], writes=[sc2])
                        fw.op("dve", lambda e, sub=sub: e.max(out=m8[:, 1, :], in_=sc2[:, sub, :]), reads=[sc2], writes=[m8])
                        fw.op("dve", TS(negsel[:, sub, 64:128], score[:, sub, :], m8[:, 1, 7:8], ALU.is_lt, NEGSEL, ALU.mult),
                              reads=[score, m8], writes=[negsel])
                    def tail(g=g):
                        fw.group("pe", [TRN(b7b[:, sub * 128:(sub + 1) * 128], negsel[:, sub, :], idb[:]) for sub in range(4)],
                                 reads=[negsel, idb], writes=[b7])
                        fw.op("act", ACP(Qa[64:128, 4 * g:4 * g + 4, :], bc(b7b[64:128, 0:512].unsqueeze(1), [64, 4, 512])), reads=[b7], writes=[Qg[g]])
                    return tail

                fin_q = []
                impb = [b7, pS[2]]
                sbk = [pS[0], pS[1]]
                for g in range(3):
                    pend = []

                    def emit_back(hh, e0, e1, g=g):
                        ets = (e0, e1)
                        fw.group("pe", [MM(pO[hh][0:65, :], vc1[:, ct, g, :], ets[ct][:], start=(ct == 0), stop=(ct == 1)) for ct in range(2)],
                                 reads=[e0, e1, vc1], writes=[pO[hh]])
                        ib = impb[hh % 2]
                        fw.group("pe", [MM(ib[:, sub * 65:(sub + 1) * 65], ets[ct][:, sub * 128:(sub + 1) * 128], ov1b[:, ct, :],
                                           start=(ct == 0), stop=(ct == 1)) for sub in range(4) for ct in range(2)],
                                 reads=[e0, e1, ov1b], writes=[ib])
                        p3 = ib[:, 0:260].rearrange("p (a b) -> p a b", a=4)
                        fw.op("dve", TS(rd4[:], p3[:, :, 64], 1e-30, ALU.max), reads=[ib], writes=[rd4])
                        fw.op("dve", RECIP(rd4[:], rd4[:]), reads=[rd4], writes=[rd4])
                        cb = bc(rd4[:].unsqueeze(2), [128, 4, 64])
                        if hh == 0:
                            fw.op("dve", TT(imp[:], p3[:, :, 0:64], cb, ALU.mult), reads=[ib, rd4], writes=[imp])
                        else:
                            fw.op("dve", TT(itmp[:], p3[:, :, 0:64], cb, ALU.mult), reads=[ib, rd4], writes=[itmp])
                            fw.op("pool", TT(imp[:], imp[:], itmp[:], ALU.add), reads=[itmp, imp], writes=[imp])
                        for _ in range(2):
                            if fin_q:
                                fin_q.pop(0)()

                    for hh in range(4):
                        ets = []
                        for ct in range(2):
                            ps = sbk[cctr[0] % 2]
                            et = E8[cctr[0] % 8]
                            cctr[0] += 1
                            fw.group("pe", [MM(ps[:], kcT[:, g, ct * 128:(ct + 1) * 128], Qa[:, 4 * g + hh, :])], reads=[kcT, Qg[g]], writes=[ps])
                            fw.op("act", ACTF(et[:], ps[:], AF.Exp, scale=SCALE), reads=[ps], writes=[et])
                            fw.op("dve", TT(et[:], et[:], cm[:, ct, :], ALU.mult), reads=[et, cm], writes=[et])
                            ets.append(et)
                        pend.append((hh, ets[0], ets[1]))
                        if len(pend) > 2:
                            emit_back(*pend.pop(0))
                    while pend:
                        emit_back(*pend.pop(0))
                    fin_q.extend(finalize([pO[hh] for hh in range(4)], [4 * g + hh for hh in range(4)],
                                          lambda i, g=g, s=s: gn[:, 4 * s:4 * s + 4, 3 * (4 * g + i) + 0], yacc, True, OTsb, b7, s, rd4, ytmp,
                                          defer=True))
                    fin_q.append(select(g))
                pending = fin_q
                for br in (1, 2):
                    for g in range(3):
                        if br == 1:
                            KT, V1, tab, kts = KsT, Vs1, tbc_t, list(range(0, 8 * s + 8))
                            need_mask = lambda kt, s=s: kt >= 8 * s
                        else:
                            KT, V1, tab, kts = KwT, Vw1, tbw_t, list(range(max(0, 8 * s - 4), 8 * s + 8))
                            need_mask = lambda kt: True
                        units = [(kt, hh) for kt in kts for hh in range(4)]

                        def mask_of(u, tab=tab, need_mask=need_mask, s=s, par=par):
                            kt = u[0]
                            if not need_mask(kt):
                                return None
                            off = 1024 * s - 128 * kt + 896
                            return (tab[:, par, off:off + 512], tab)
                        attn_units(units,
                                   q_of=lambda u, g=g: Qa[:, 4 * g + u[1], :],
                                   k_of=lambda u, KT=KT, g=g: KT[:, g, u[0] * 128:(u[0] + 1) * 128],
                                   v_of=lambda u, V1=V1, g=g: V1[:, u[0], g, :],
                                   mask_of=mask_of,
                                   pO_of=lambda u: pO[u[1]],
                                   first_of=lambda u, kts=kts: u[0] == kts[0],
                                   last_of=lambda u, kts=kts: u[0] == kts[-1],
                                   pS=pS, Pt=Pt, ctr=ctr, kqv=(KT.res, Qg[g], V1.res),
                                   hooks={6 + 4 * k: p for k, p in enumerate(pending)})
                        pending = finalize([pO[hh] for hh in range(4)], [4 * g + hh for hh in range(4)],
                                           lambda i, g=g, s=s, br=br: gn[:, 4 * s:4 * s + 4, 3 * (4 * g + i) + br], yacc, False, OTsb, b7, s, rd4, ytmp,
                                           defer=True)
                for p in pending:
                    p()
                y_to_scratch(yacc, 6, 0, s, ybf, yst, pY)
            fw.barrier()

        def dil_phase():
            KbT = A.t("KbT", [6, S], BF16)
            Vb1 = A.t("Vb1", [NT, 12, 65], BF16)
            tabs = [A.t("tbd0_t", [2, L_D0], BF16), A.t("tbd1_t", [2, L_D1], BF16), A.t("tbd2_t", [2, L_D2], BF16)]
            Qb2 = [A.t(f"Qb{i}", [12, 512], BF16) for i in range(2)]
            Pt = [A.t(f"Pt{i}", [512], BF16) for i in range(6)]
            OTsb = [A.t(f"OTsb{i}", [512], F32) for i in range(2)]
            yacc = A.t("yaccb", [4, 4, 64], F32)
            ybf = A.t("ybfb", [4, 256], BF16)
            yst = A.t("ystb", [2, 512], BF16)
            rd4 = A.t("rd4", [4], F32)
            ytmp = A.t("ytmp", [4, 64], F32)
            Kc = [Res(f"KbTc{c}") for c in range(4)]
            Vc = [Res(f"Vb1c{c}") for c in range(4)]
            def load_chunk(c):
                fw.dma("sp", DMA(KbT[:, :, 1024 * c:1024 * c + 1024], KbT_scr[:, :, 1024 * c:1024 * c + 1024]), f"KbTl{c}", reads=[r_Kscr], writes=[Kc[c]])
                fw.dma("sp", DMA(Vb1[:, 8 * c:8 * c + 8, :, :].rearrange("p t h c -> p t (h c)"), V_scr[0:128, 8 * c:8 * c + 8, 390:1170]), f"Vb1l{c}",
                       reads=[r_Vscr], writes=[Vc[c]])
            for q_ in Qb2:
                fw.op("pool", MSET(q_[:], 0.0), writes=[q_])
            for g, (src, t) in enumerate(zip((tbd0, tbd1, tbd2), tabs)):
                fw.dma("sp", DMA(t[:], src), f"tbd{g}", writes=[t])
            load_chunk(0)
            pO = [bank(b, 512, F32, "pO") for b in range(2)]
            pS = [bank(b, 512, F32, "pS") for b in (4, 5, 6)]
            b7 = bank(7, 512, F32, "b7")
            pY = [share(pS[i], psum[:, 512 * (4 + i):512 * (5 + i)].bitcast(BF16), "pY") for i in range(3)]
            ctr = [0]
            def load_qb(s):
                Qb = Qb2[s % 2]
                Qv = Qb[:].rearrange("p (j two) t -> p j two t", two=2)
                fw.dma("sp", DMA(Qv[0:64, :, 0, :], QbT_scr[0:64, :, s * 512:(s + 1) * 512]), f"Qbl{s % 2}", reads=[r_Qscr], writes=[Qb])
                fw.dma("sp", DMA(Qv[64:128, :, 1, :], QbT_scr[64:128, :, s * 512:(s + 1) * 512]), f"Qbl{s % 2}", reads=[r_Qscr], writes=[Qb])
            load_qb(0)
            for s in range(4):
                par = s % 2
                Qb = Qb2[s % 2]
                if s + 1 < 4:
                    load_qb(s + 1)
                if s == 0:
                    for c in range(1, 4):
                        load_chunk(c)
                pending = []
                for hp in range(2):
                    units = []
                    for g, back in enumerate((1, 4, 16)):
                        for kt in range(max(0, 8 * s - back), 8 * s + 8):
                            for e_ in range(2):
                                units.append((g, kt, e_))
                    first, last = units[0], units[-1]

                    def mask_of(u, s=s, par=par):
                        off = 1024 * s - 128 * u[1] + 896
                        t = tabs[u[0]]
                        return (t[:, par, off:off + 512], t)
                    attn_units(units,
                               q_of=lambda u, hp=hp: Qb[:, 4 * u[0] + 2 * hp + u[2], :],
                               k_of=lambda u, hp=hp: KbT[:, 2 * u[0] + hp, u[1] * 128:(u[1] + 1) * 128],
                               v_of=lambda u, hp=hp: Vb1[:, u[1], 4 * u[0] + 2 * hp + u[2], :],
                               mask_of=mask_of,
                               pO_of=lambda u: pO[u[2]],
                               first_of=lambda u, first=first: (u[0], u[1]) == (first[0], first[1]),
                               last_of=lambda u, last=last: (u[0], u[1]) == (last[0], last[1]),
                               pS=pS, Pt=Pt, ctr=ctr, kqv=(lambda u: Kc[u[1] // 8], Qb.res, lambda u: Vc[u[1] // 8]),
                               hooks={6 + 4 * k: p for k, p in enumerate(pending)})
                    pending = finalize([pO[0], pO[1]], [2 * hp, 2 * hp + 1], lambda i: None, yacc, True, OTsb, b7, s, rd4, ytmp, defer=True)
                for p in pending:
                    p()
                y_to_scratch(yacc, 2, 6, s, ybf, yst, pY)
            fw.barrier()

        def e_phase():
            xm = A.t("xm", [NOWN, D], F32)
            xr = [Res(f"xm{i}") for i in range(NOWN)]
            for s4 in range(4):
                fw.dma("pool", DMA(xm[:, 4 * s4:4 * s4 + 4, :], x_own[512 * s4:512 * s4 + 512, :].rearrange("(t p) c -> p t c", p=128)), f"xml{s4}",
                       writes=xr[4 * s4:4 * s4 + 4])
            e1_mark = A.top
            woa_b = A.t("woa_b", [6, 1024], BF16)
            wob_b = A.t("wob_b", [2, 1024], BF16)
            wout_b = A.t("wout_b", [8, 1024], BF16)
            load_weight(woa_b, lambda c: woa_b[:, 2 * c:2 * c + 2, :], lambda c: woa[:, 2 * c:2 * c + 2, :], 3, 2048, as3=2)
            load_weight(wob_b, lambda c: wob_b[:, 0:2, :], lambda c: wob[:, 0:2, :], 1, 2048, as3=2)
            load_weight(wout_b, lambda c: wout_b[:, 2 * c:2 * c + 2, :], lambda c: wout[:, 2 * c:2 * c + 2, :], 4, 2048, as3=2)
            yT = [A.t(f"yT{i}", [8, 512], BF16) for i in range(2)]
            gT = [A.t(f"gT{i}", [16, 512], BF16) for i in range(1)]
            mixT = A.t("mixT", [8, 512], BF16)
            t1 = [A.t(f"t1_{i}", [512], F32) for i in range(2)]
            t2 = [A.t(f"t2_{i}", [512], F32) for i in range(2)]
            junk = A.t("junkE", [D], BF16)
            st2 = A.t("st2", [4], F32)
            h2 = [A.t(f"h2_{i}", [D], BF16) for i in range(2)]
            h2st = [A.t(f"h2st{i}", [8, 512], BF16) for i in range(1)]
            pU = [bank(b, 512, F32, "pU") for b in range(4)]
            pD = [bank(b, 512, F32, "pD") for b in (4, 5)]
            pT2 = [bank(b, 1024, BF16, "pT2") for b in (6, 7)]
            for s in range(4):
                y_t, g_t, hst = yT[s % 2], gT[0], h2st[0]
                if s == 0:
                    fw.dma("sp", DMA(y_t[:], yT_scr[:, :, 0:512]), "yTl0", reads=[r_yscr], writes=[y_t])
                fw.dma("sp", DMA(g_t[:, 0:8, :], gT_scr[:, 0:8, s * 512:(s + 1) * 512]), "gTl", reads=[r_gscr], writes=[g_t])
                fw.dma("pool", DMA(g_t[:, 8:16, :], gT_scr[:, 8:16, s * 512:(s + 1) * 512]), "gTl2", reads=[r_gscr], writes=[g_t])
                if s + 1 < 4:
                    fw.dma("sp", DMA(yT[(s + 1) % 2][:], yT_scr[:, :, (s + 1) * 512:(s + 2) * 512]), f"yTl{(s + 1) % 2}", reads=[r_yscr], writes=[yT[(s + 1) % 2]])
                for cc in range(8):
                    pa, pb = pU[(2 * cc) % 4], pU[(2 * cc + 1) % 4]
                    fw.group("pe", [MM(pa[:], woa_b[:, fc, cc * 128:(cc + 1) * 128], y_t[:, fc, :], start=(fc == 0), stop=(fc == 5)) for fc in range(6)],
                             reads=[woa_b, y_t], writes=[pa])
                    fw.group("pe", [MM(pb[:], wob_b[:, fc, cc * 128:(cc + 1) * 128], y_t[:, 6 + fc, :], start=(fc == 0), stop=(fc == 1)) for fc in range(2)],
                             reads=[wob_b, y_t], writes=[pb])
                    ta, tb = t1[cc % 2], t2[cc % 2]
                    fw.op("dve", TT(ta[:], pa[:], g_t[:, cc, :], ALU.mult), reads=[pa, g_t], writes=[ta])
                    fw.op("dve", TT(tb[:], pb[:], g_t[:, 8 + cc, :], ALU.mult), reads=[pb, g_t], writes=[tb])
                    fw.op("pool", TT(mixT[:, cc, :], ta[:], tb[:], ALU.add), reads=[ta, tb], writes=[mixT])
                tr_pending = []
                for sub in range(4):
                    ti = 4 * s + sub
                    for cb in range(2):
                        pd = pD[cb]
                        fw.group("pe", [MM(pd[:], mixT[:, fc, sub * 128:(sub + 1) * 128], wout_b[:, fc, cb * 512:(cb + 1) * 512],
                                           start=(fc == 0), stop=(fc == 7)) for fc in range(8)], reads=[mixT, wout_b], writes=[pd])
                        fw.op("dve", TT(xm[:, ti, cb * 512:(cb + 1) * 512], pd[:], xm[:, ti, cb * 512:(cb + 1) * 512], ALU.add),
                              reads=[pd, xr[ti]], writes=[xr[ti]])
                    fw.op("act", ACTF(junk[:], xm[:, ti, :], AF.Square, accum=st2[:, 0:1]), reads=[xr[ti]], writes=[junk, st2])
                    fw.op("act", ACTF(st2[:, 1:2], st2[:, 0:1], AF.Sqrt, scale=1.0 / D, bias=epsb[:, 0:1]), reads=[st2, epsb], writes=[st2])
                    fw.op("dve", RECIP(st2[:, 2:3], st2[:, 1:2]), reads=[st2], writes=[st2])
                    hh = h2[sub % 2]
                    fw.op("dve", TS(hh[:], xm[:, ti, :], st2[:, 2:3], ALU.mult), reads=[xr[ti], st2], writes=[hh])
                    def tr_part(sub=sub, hh=hh):
                        pt = pT2[sub % 2]
                        fw.group("pe", [TRN(pt[:, kc * 128:(kc + 1) * 128], hh[:, kc * 128:(kc + 1) * 128], idb[:]) for kc in range(8)],
                                 reads=[hh, idb], writes=[pt])
                        fw.op("act", ACP(hst[:, :, sub * 128:(sub + 1) * 128], pt[:].rearrange("p (a b) -> p a b", a=8)), reads=[pt], writes=[hst])
                    if tr_pending:
                        tr_pending.pop(0)()
                    tr_pending.append(tr_part)
                while tr_pending:
                    tr_pending.pop(0)()
                fw.dma("sp", DMA(h2T_scr[:, :, s * 512:(s + 1) * 512], hst[:]), "h2stl", reads=[hst], writes=[r_h2scr])
            fw.barrier()
            A.top = e1_mark
            g2t = A.t("g2t", [8], F32)
            fw.dma("sp", DMA(g2t[:], g2), "g2t", writes=[g2t])
            wupq = [A.t(f"wupq{i}", [8, 1024], BF16) for i in range(2)]
            wdnq = [A.t(f"wdnq{i}", [8, 1024], BF16) for i in range(2)]
            h2T = [A.t(f"h2T{i}", [8, 512], BF16) for i in range(2)]
            aT = [A.t(f"aT{i}", [8, 512], BF16) for i in range(2)]
            rl = [A.t(f"rl{i}", [512], F32) for i in range(2)]
            pUp = [bank(b, 512, F32, "pUp") for b in range(4)]
            pDn = [bank(b, 512, F32, "pDn") for b in (4, 5, 6, 7)]
            n = 0
            def load_quarter(fq):
                wu, wd = wupq[fq % 2], wdnq[fq % 2]
                for kc in range(8):
                    i_ = stage_ctr[0]
                    st = stage[i_ % 4]
                    key = f"wstage{i_ % 4}" if fq == 0 else f"pws{i_ % 4}"
                    stage_ctr[0] += 1
                    qn = "sp" if fq == 0 else "pool"
                    fw.dma(qn, DMA(st[:, 0:1024], wup[:, kc, fq * 1024:(fq + 1) * 1024]), key, writes=[st])
                    if fq > 0:
                        fw.op("pool", TT(wu[:, kc, :], st[:, 0:1024], bc(g2t[:, kc:kc + 1], [128, 1024]), ALU.mult), reads=[st, g2t], writes=[wu])
                    elif kc % 2:
                        fw.op("act", ACTF(wu[:, kc, :], st[:, 0:1024], AF.Copy, scale=g2t[:, kc:kc + 1]), reads=[st, g2t], writes=[wu])
                    else:
                        fw.op("dve", TS(wu[:, kc, :], st[:, 0:1024], g2t[:, kc:kc + 1], ALU.mult), reads=[st, g2t], writes=[wu])
                for c in range(4):
                    i_ = stage_ctr[0]
                    st = stage[i_ % 4]
                    key = f"wstage{i_ % 4}" if fq == 0 else f"pws{i_ % 4}"
                    stage_ctr[0] += 1
                    qn = "sp" if fq == 0 else "pool"
                    sv = st[:, 0:2048].rearrange("p (a b) -> p a b", a=2)
                    fw.dma(qn, DMA(sv, wdown[:, 8 * fq + 2 * c:8 * fq + 2 * c + 2, :]), key, writes=[st])
                    if fq > 0:
                        fw.op("pool", TC(wd[:, 2 * c:2 * c + 2, :], sv), reads=[st], writes=[wd])
                    else:
                        fw.op("act" if c % 2 else "dve", (ACP if c % 2 else TC)(wd[:, 2 * c:2 * c + 2, :], sv), reads=[st], writes=[wd])
            load_quarter(0)
            for fq in range(4):
                wu, wd = wupq[fq % 2], wdnq[fq % 2]
                for s in range(4):
                    if s == 0 and fq < 3:
                        load_quarter(fq + 1)
                    h_t = h2T[n % 2]
                    a_t = aT[n % 2]
                    fw.dma("sp", DMA(h_t[:], h2T_scr[:, :, s * 512:(s + 1) * 512]), f"h2Tl{n % 2}", reads=[r_h2scr], writes=[h_t])
                    n += 1
                    for fcb in range(8):
                        pu = pUp[fcb % 4]
                        fw.group("pe", [MM(pu[:], wu[:, kc, fcb * 128:(fcb + 1) * 128], h_t[:, kc, :], start=(kc == 0), stop=(kc == 7)) for kc in range(8)],
                                 reads=[wu, h_t], writes=[pu])
                        r = rl[fcb % 2]
                        fw.op("act", ACTF(r[:], pu[:], AF.Relu), reads=[pu], writes=[r])
                        fw.op("dve", TT(a_t[:, fcb, :], r[:], r[:], ALU.mult), reads=[r], writes=[a_t])
                    for sub in range(4):
                        ti = 4 * s + sub
                        for cb in range(2):
                            pd = pDn[(2 * sub + cb) % 4]
                            fw.group("pe", [MM(pd[:], a_t[:, fc, sub * 128:(sub + 1) * 128], wd[:, fc, cb * 512:(cb + 1) * 512],
                                               start=(fc == 0), stop=(fc == 7)) for fc in range(8)], reads=[a_t, wd], writes=[pd])
                            fw.op("dve", TT(xm[:, ti, cb * 512:(cb + 1) * 512], pd[:], xm[:, ti, cb * 512:(cb + 1) * 512], ALU.add),
                                  reads=[pd, xr[ti]], writes=[xr[ti]])
                        if fq == 3:
                            fw.dma("sp", DMA(out[ti * 128:(ti + 1) * 128, :], xm[:, ti, :]), "outst", reads=[xr[ti]], writes=[r_out])
            fw.barrier()

        epsb = A.t("epsb", [1], F32)
        fw.op("pool", lambda e: e.memset(epsb[:], EPS), writes=[epsb])
        base_mark = A.top
        r_Vscr, r_Kscr, r_Qscr, r_gscr = Res("Vscr"), Res("Kscr"), Res("Qscr"), Res("gscr")
        r_yscr, r_h2scr, r_out = Res("yscr"), Res("h2scr"), Res("out")

        order = ["A", "B", "C", "D1", "D2", "all"]
        lvl = order.index(upto)
        fw.dma("sp", DMA(V_scr[128:129, :, 1170:1554], bass.AP(expand.tensor, 64, [[0, 1], [0, NT], [1, 384]])), "vpad", writes=[r_Vscr])
        proj_phase("A")
        if lvl >= 1:
            proj_phase("B")
        if debug and lvl >= 1:
            fw.dma("sp", DMA(dbg_gn, gn[:]), "dbggn", reads=[gn], writes=[r_out])
        if lvl >= 2:
            A.top = base_mark
            kcT = A.t("kcT", [3, 256], BF16)
            vc1 = A.t("vc1", [2, 3, 65], BF16)
            c_mark = A.top
            cmp_phase(kcT, vc1)
            if debug:
                fw.dma("sp", DMA(dbg_kc, kcT[:]), "dbgkc", reads=[kcT], writes=[r_out])
                fw.dma("sp", DMA(dbg_vc, vc1[:]), "dbgvc", reads=[vc1], writes=[r_out])
                fw.barrier()
        if lvl >= 3:
            A.top = c_mark
            nsa_phase(kcT, vc1)
        if lvl >= 4:
            A.top = base_mark
            dil_phase()
        if lvl >= 5:
            A.top = base_mark
            e_phase()

        fw.barrier()
        fw.replay()
    return nc


def _rope_tables(pos):
    half = 32
    inv_freq = np.power(np.float32(10000.0), -np.arange(half, dtype=np.float32) / np.float32(half)).astype(np.float32)
    ang = pos.astype(np.float32)[:, None] * inv_freq[None, :]
    return np.cos(ang).astype(np.float32), np.sin(ang).astype(np.float32)


def _tok_major(a, ntile):
    return np.ascontiguousarray(a.reshape(ntile, 128, -1).transpose(1, 0, 2))


def _pmajor(w, nchunk):
    return np.ascontiguousarray(w.reshape(nchunk, 128, -1).transpose(1, 0, 2))


def _toeplitz(L, delta, fn):
    k = np.arange(128)[:, None]
    j = np.arange(L)[None, :]
    d = delta + j - 896 - k
    return fn(d)


def host_prep(inputs):
    bf = ml_dtypes.bfloat16
    x = np.asarray(inputs["x"], np.float32)
    w_in = np.asarray(inputs["w_in"], np.float32)[0]
    sizes = (768, 192, 192, 192, 192, 192, 192, 36, 768, 768, 768, 1024, 1024)
    offs = np.concatenate([[0], np.cumsum(sizes)])
    seg = {n: w_in[:, offs[i]:offs[i + 1]] for i, n in enumerate(
        ["q_a", "k_c", "v_c", "k_s", "v_s", "k_w", "v_w", "g_nsa", "q_b", "k_b", "v_b", "g_ma", "g_mb"])}
    wkv = np.concatenate([seg["k_s"], seg["k_w"], seg["k_b"], seg["v_s"], seg["v_w"], seg["v_b"], seg["k_c"], seg["v_c"]], axis=1)
    wq = np.concatenate([seg["q_a"], seg["q_b"], seg["g_nsa"], seg["g_ma"], seg["g_mb"]], axis=1)
    g = lambda n: np.asarray(inputs[n], np.float32)[0]
    common = {
        "wkv": _pmajor(wkv, 8), "wq": _pmajor(wq, 8),
        "g1": np.ascontiguousarray(g("norm1_g").reshape(8, 128).T), "g2": np.ascontiguousarray(g("norm2_g").reshape(8, 128).T),
        "gcolK": np.concatenate([np.tile(g("k_norm_slc"), 3), np.tile(g("k_norm_win"), 3), np.tile(g("k_norm_b"), 12)])[None, :].astype(np.float32),
        "gcolQ": np.concatenate([np.tile(g("q_norm_a"), 12), np.tile(g("q_norm_b"), 12)])[None, :].astype(np.float32),
        "gcmp": g("k_norm_cmp")[None, :].astype(np.float32),
        "ident": np.eye(128, dtype=np.float32),
        "expand": (np.arange(S)[None, :] // 64 == np.arange(64)[:, None]).astype(bf),
        "woa": _pmajor(g("w_o_a"), 6), "wob": _pmajor(g("w_o_b"), 2), "wout": _pmajor(g("w_out"), 8),
        "wup": _pmajor(g("w_up"), 8), "wdown": _pmajor(g("w_down"), 32),
        "w1k": _pmajor(g("cmp_k_w1"), 16), "w2k": _pmajor(g("cmp_k_w2"), 2),
        "w1v": _pmajor(g("cmp_v_w1"), 16), "w2v": _pmajor(g("cmp_v_w2"), 2),
        "posk": np.ascontiguousarray(g("cmp_k_pos").reshape(16, 128).T), "posv": np.ascontiguousarray(g("cmp_v_pos").reshape(16, 128).T),
    }
    cos, sin = _rope_tables(np.arange(S))
    common["cs_all"] = _tok_major(cos, NT)
    common["sn_all"] = _tok_major(sin, NT)
    c_end = np.arange(256) * 16 + 31
    cc, sc = _rope_tables(c_end)
    common["cs_cmp"] = _tok_major(cc, 2)
    common["sn_cmp"] = _tok_major(sc, 2)
    cs_ = np.arange(256)[:, None] * 16
    ss_ = np.arange(64)[None, :] * 64
    ov = np.clip(np.minimum(cs_ + 32, ss_ + 64) - np.maximum(cs_, ss_), 0, None).astype(np.float32) / 32.0
    ov1 = np.concatenate([ov, np.ones((256, 1), np.float32)], axis=1)
    ov1[255] = 0.0
    common["ov1"] = _tok_major(ov1, 2).astype(bf)

    in_maps = []
    for c in range(8):
        b, half = c // 2, c % 2
        qts = QT[half]
        own_idx = np.concatenate([np.arange(q * 512, (q + 1) * 512) for q in qts])
        m = dict(common)
        m["x_all"] = np.ascontiguousarray(x[b])
        m["x_own"] = np.ascontiguousarray(x[b][own_idx])
        m["cs_own"] = _tok_major(cos[own_idx], NOWN)
        m["sn_own"] = _tok_major(sin[own_idx], NOWN)
        cm = (c_end[:, None] <= own_idx[None, :]) & (np.arange(256)[:, None] < 255)
        cm = cm.reshape(2, 128, 4, 512).transpose(1, 2, 0, 3)
        m["cmask"] = np.ascontiguousarray(cm).astype(bf)
        t = own_idx
        cur = t // 64
        blk = np.arange(64)[None, :]
        forced = (blk == 0) | (blk == cur[:, None]) | (blk == cur[:, None] - 1)
        elig = blk <= cur[:, None]
        sA = (elig & ~forced).astype(np.float32)
        sB = np.where(forced, np.float32(1e30), np.where(elig, np.float32(0.0), np.float32(-1e30))).astype(np.float32)
        m["selA"] = np.ascontiguousarray(sA.reshape(4, 4, 128, 64).transpose(2, 0, 1, 3))
        m["selB"] = np.ascontiguousarray(sB.reshape(4, 4, 128, 64).transpose(2, 0, 1, 3))
        def tabs(L, fn):
            out = np.zeros((128, 2, L), np.float32)
            for p in range(2):
                delta = 512 * (qts[p] - 2 * p)
                out[:, p, :] = _toeplitz(L, delta, fn)
            return out.astype(bf)
        m["tbc"] = tabs(L_C, lambda d: d >= 0)
        m["tbw"] = tabs(L_W, lambda d: (d >= 0) & (d < 512))
        m["tbd0"] = tabs(L_D0, lambda d: (d >= 0) & (d <= 128))
        m["tbd1"] = tabs(L_D1, lambda d: (d >= 0) & (d <= 512) & (d % 4 == 0))
        m["tbd2"] = tabs(L_D2, lambda d: (d >= 0) & (d <= 2048) & (d % 16 == 0))
        in_maps.append(m)
    return in_maps


def kernel(**inputs):
    in_maps = host_prep(inputs)
    nc = build()
    res = run_bass_kernel_spmd(nc, in_maps, core_ids=list(range(8)))
    out = np.zeros((4, S, D), np.float32)
    for c in range(8):
        b, half = c // 2, c % 2
        o = np.asarray(res.results[c]["out"], np.float32)
        for s_, q in enumerate(QT[half]):
            out[b, q * 512:(q + 1) * 512] = o[s_ * 512:(s_ + 1) * 512]
    return out
```

```python
import contextlib
import math
import numpy as np
import ml_dtypes
import concourse.bass as bass
import concourse.mybir as mybir
from concourse.bass_utils import run_bass_kernel_spmd

F32 = mybir.dt.float32
BF16 = mybir.dt.bfloat16
AF = mybir.ActivationFunctionType
ALU = mybir.AluOpType
AX = mybir.AxisListType

S = 4096
D = 1024
NT = 32
NOWN = 16
EPS = 1e-6
SCALE = 0.125
QT = ([0, 3, 4, 7], [1, 2, 5, 6])
KV_COLS = 2688
Q_COLS = 3620
NEGSEL = -2048.0
L_C, L_W, L_D0, L_D1, L_D2 = 1408, 1920, 1536, 1920, 3456
SBUF_BYTES = 206 * 1024


class Res:
    __slots__ = ("name", "lw", "rd")

    def __init__(self, name=""):
        self.name = name
        self.lw = None
        self.rd = []


class Tile:
    def __init__(self, ap, name):
        self.ap = ap
        self.res = Res(name)

    def __getitem__(self, k):
        return self.ap[k]


class FW:
    ENG = ["pe", "act", "dve", "pool", "sp"]

    def __init__(self, nc, es):
        self.nc = nc
        self.es = es
        self.q = {e: [] for e in self.ENG}
        self.cnt = {e: 0 for e in self.ENG}
        self.sem = {e: es.enter_context(nc.semaphore("s_" + e)) for e in ["pe", "act", "dve", "pool"]}
        self.waited = {e: {} for e in self.ENG}
        self.dsem = {}

    @staticmethod
    def _r(x):
        return x.res if isinstance(x, Tile) else x

    def _deps(self, reads, writes):
        deps = []
        for r in reads:
            if r.lw is not None:
                deps.append(r.lw)
        for w in writes:
            deps.extend(w.rd)
            if w.lw is not None:
                deps.append(w.lw)
        return deps

    def _emit_waits(self, eng, deps):
        need = {}
        for (k, v) in deps:
            if k == eng and eng == "pe":
                continue
            if v > need.get(k, 0):
                need[k] = v
        for k, v in need.items():
            if self.waited[eng].get(k, 0) >= v:
                continue
            self.waited[eng][k] = v
            semh = self.sem[k] if k in self.sem else self.dsem[k][0]
            self.q[eng].append(("wait", semh, v))

    def _mark(self, key, seq, reads, writes):
        for r in reads:
            r.rd.append((key, seq))
            if len(r.rd) > 48:
                mx = {}
                for (k, v) in r.rd:
                    if v > mx.get(k, 0):
                        mx[k] = v
                r.rd = list(mx.items())
        for w in writes:
            w.lw = (key, seq)
            w.rd = []

    def op(self, eng, fn, reads=(), writes=()):
        return self.group(eng, [fn], reads, writes)

    def group(self, eng, fns, reads=(), writes=()):
        reads = [self._r(x) for x in reads]
        writes = [self._r(x) for x in writes]
        self._emit_waits(eng, self._deps(reads, writes))
        self.cnt[eng] += 1
        seq = self.cnt[eng]
        for f in fns[:-1]:
            self.q[eng].append(("op", f, None))
        self.q[eng].append(("op", fns[-1], self.sem[eng]))
        self._mark(eng, seq, reads, writes)
        return seq

    def dma(self, queue, fn, key, reads=(), writes=()):
        reads = [self._r(x) for x in reads]
        writes = [self._r(x) for x in writes]
        if key not in self.dsem:
            self.dsem[key] = [self.es.enter_context(self.nc.semaphore("d_" + key)), 0]
        self._emit_waits(queue, self._deps(reads, writes))
        ent = self.dsem[key]
        ent[1] += 16
        self.q[queue].append(("dma", fn, ent[0]))
        self._mark(key, ent[1], reads, writes)

    def barrier(self):
        allv = [(e, self.cnt[e]) for e in self.sem if self.cnt[e] > 0]
        allv += [(k, v[1]) for k, v in self.dsem.items() if v[1] > 0]
        for e in self.ENG:
            self._emit_waits(e, [(k, v) for (k, v) in allv if k != e])
            if e in self.sem and e != "pe" and self.cnt[e] > 0:
                self._emit_waits(e, [(e, self.cnt[e])])

    def final_wait(self, eng, ress):
        deps = []
        for r in ress:
            r = self._r(r)
            if r.lw is not None:
                deps.append(r.lw)
        self._emit_waits(eng, deps)

    def replay(self):
        nc = self.nc
        with nc.Block() as block:
            def run(engname):
                def f(e):
                    for it in self.q[engname]:
                        if it[0] == "wait":
                            e.wait_ge(it[1], it[2])
                        elif it[0] == "op":
                            ins = it[1](e)
                            if it[2] is not None:
                                ins.then_inc(it[2], 1)
                        else:
                            it[1](e).then_inc(it[2], 16)
                return f
            block.tensor(run("pe"))
            block.scalar(run("act"))
            block.vector(run("dve"))
            block.gpsimd(run("pool"))
            block.sync(run("sp"))


class Arena:
    def __init__(self, big, nbytes):
        self.big = big
        self.cap = nbytes
        self.top = 0
        self.hi = 0

    def t(self, name, shape, dt):
        esz = 2 if dt == BF16 else 4
        n = int(np.prod(shape))
        nb = (n * esz + 31) // 32 * 32
        off = self.top
        assert off + nb <= self.cap, (name, off, nb, self.cap)
        self.top = off + nb
        self.hi = max(self.hi, self.top)
        ap = self.big[:, off // 4:(off + nb) // 4]
        if dt == BF16:
            ap = ap.bitcast(BF16)
        ap = ap[:, 0:n]
        if len(shape) == 2:
            ap = ap.rearrange("p (a b) -> p a b", a=shape[0])
        elif len(shape) == 3:
            ap = ap.rearrange("p (a b c) -> p a b c", a=shape[0], b=shape[1])
        elif len(shape) == 4:
            ap = ap.rearrange("p (a b c d) -> p a b c d", a=shape[0], b=shape[1], c=shape[2])
        return Tile(ap, name)


def bc(ap, shape):
    return ap.to_broadcast(list(shape))


def MM(out, lhsT, rhs, start=True, stop=True):
    return lambda e: e.matmul(out, lhsT=lhsT, rhs=rhs, start=start, stop=stop)


def TT(out, in0, in1, op):
    return lambda e: e.tensor_tensor(out=out, in0=in0, in1=in1, op=op)


def TS(out, in0, s1, op0, s2=None, op1=None):
    if op1 is None:
        return lambda e: e.tensor_scalar(out=out, in0=in0, scalar1=s1, scalar2=None, op0=op0)
    return lambda e: e.tensor_scalar(out=out, in0=in0, scalar1=s1, scalar2=s2, op0=op0, op1=op1)


def ACTF(out, in_, func, scale=1.0, bias=None, accum=None):
    kw = {}
    if bias is not None:
        kw["bias"] = bias
    if accum is not None:
        kw["accum_out"] = accum
    return lambda e: e.activation(out=out, in_=in_, func=func, scale=scale, **kw)


def ACP(out, in_):
    return lambda e: e.copy(out=out, in_=in_)


def TC(out, in_):
    return lambda e: e.tensor_copy(out=out, in_=in_)


def TRN(out, in_, ident):
    return lambda e: e.transpose(out=out, in_=in_, identity=ident)


def DMA(out, in_):
    return lambda e: e.dma_start(out=out, in_=in_)


def MSET(ap, v):
    return lambda e: e.memset(ap, v)


def RECIP(out, in_):
    return lambda e: e.reciprocal(out=out, in_=in_)


def build(upto="all", debug=False):
    nc = bass.Bass("TRN2", target_bir_lowering=False)
    dbg_kind = "ExternalOutput" if debug else "Internal"

    def din(name, shape, dt=F32):
        return nc.dram_tensor(name, list(shape), dt, kind="ExternalInput").ap()

    def dscr(name, shape, dt=BF16):
        return nc.dram_tensor(name, list(shape), dt, kind=dbg_kind).ap()

    x_all = din("x_all", [S, D])
    x_own = din("x_own", [2048, D])
    wkv = din("wkv", [128, 8, KV_COLS])
    wq = din("wq", [128, 8, Q_COLS])
    g1 = din("g1", [128, 8])
    g2 = din("g2", [128, 8])
    gcolK = din("gcolK", [1, 1152])
    gcolQ = din("gcolQ", [1, 1536])
    gcmp = din("gcmp", [1, 64])
    cs_all = din("cs_all", [128, NT, 32])
    sn_all = din("sn_all", [128, NT, 32])
    cs_own = din("cs_own", [128, NOWN, 32])
    sn_own = din("sn_own", [128, NOWN, 32])
    cs_cmp = din("cs_cmp", [128, 2, 32])
    sn_cmp = din("sn_cmp", [128, 2, 32])
    ident = din("ident", [128, 128])
    expand = din("expand", [64, S], BF16)
    ov1 = din("ov1", [128, 2, 65], BF16)
    cmask = din("cmask", [128, 4, 2, 512], BF16)
    selA = din("selA", [128, 4, 4, 64])
    selB = din("selB", [128, 4, 4, 64])
    tbc = din("tbc", [128, 2, L_C], BF16)
    tbw = din("tbw", [128, 2, L_W], BF16)
    tbd0 = din("tbd0", [128, 2, L_D0], BF16)
    tbd1 = din("tbd1", [128, 2, L_D1], BF16)
    tbd2 = din("tbd2", [128, 2, L_D2], BF16)
    w1k = din("w1k", [128, 16, 256])
    w2k = din("w2k", [128, 2, 64])
    posk = din("posk", [128, 16])
    w1v = din("w1v", [128, 16, 256])
    w2v = din("w2v", [128, 2, 64])
    posv = din("posv", [128, 16])
    woa = din("woa", [128, 6, 1024])
    wob = din("wob", [128, 2, 1024])
    wout = din("wout", [128, 8, 1024])
    wup = din("wup", [128, 8, 4096])
    wdown = din("wdown", [128, 32, 1024])
    out = nc.dram_tensor("out", [2048, D], F32, kind="ExternalOutput").ap()

    KsT_scr = dscr("KsT_scr", [64, 3, S])
    KwT_scr = dscr("KwT_scr", [64, 3, S])
    KbT_scr = dscr("KbT_scr", [128, 6, S])
    V_scr = dscr("V_scr", [129, NT, 1554])
    QaT_scr = dscr("QaT_scr", [64, 12, 2048])
    QbT_scr = dscr("QbT_scr", [128, 6, 2048])
    gT_scr = dscr("gT_scr", [128, 16, 2048])
    yT_scr = dscr("yT_scr", [128, 8, 2048])
    h2T_scr = dscr("h2T_scr", [128, 8, 2048])
    dbg_gn = dscr("dbg_gn", [128, NOWN, 36], F32) if debug else None
    dbg_kc = dscr("dbg_kc", [128, 3, 256], BF16) if debug else None
    dbg_vc = dscr("dbg_vc", [128, 2, 3, 65], BF16) if debug else None

    with contextlib.ExitStack() as es:
        fw = FW(nc, es)
        big = es.enter_context(nc.sbuf_tensor("big", [128, SBUF_BYTES // 4], F32))
        psum = es.enter_context(nc.psum_tensor("psum", [128, 4096], F32))
        A = Arena(big, SBUF_BYTES)

        def bank(b, n=512, dt=F32, name="ps"):
            ap = psum[:, 512 * b:512 * b + 512]
            if dt == BF16:
                ap = ap.bitcast(BF16)
            return Tile(ap[:, 0:n], f"{name}{b}")

        idf = A.t("idf", [128], F32)
        idb = A.t("idb", [128], BF16)
        gn = A.t("gn", [NOWN, 36], F32)
        fw.dma("sp", lambda e: e.dma_start(out=idf[:], in_=ident), "idf", writes=[idf])
        fw.op("dve", lambda e: e.tensor_copy(out=idb[:], in_=idf[:]), reads=[idf], writes=[idb])
        stage_off = A.top
        stage = [A.t(f"wstage{i}", [2048], F32) for i in range(4)]
        stage_ctr = [0]

        def load_weight(dst_tile, dst_ap_fn, src_ap_fn, nchunks, width, gain=None, eng_cycle=("dve", "pool"), as3=None):
            for c in range(nchunks):
                i = stage_ctr[0]
                stage_ctr[0] += 1
                st = stage[i % 4]
                assert width <= 2048
                sview = st[:, 0:width] if as3 is None else st[:, 0:width].rearrange("p (a b) -> p a b", a=as3)
                fw.dma("sp", DMA(sview, src_ap_fn(c)), f"wstage{i % 4}", writes=[st])
                eng = ("dve", "act")[i % 2]
                if gain is None:
                    fw.op(eng, (TC if eng == "dve" else ACP)(dst_ap_fn(c), sview), reads=[st], writes=[dst_tile])
                elif eng == "dve":
                    fw.op(eng, TS(dst_ap_fn(c), sview, gain[:, c:c + 1], ALU.mult), reads=[st, gain], writes=[dst_tile])
                else:
                    fw.op(eng, ACTF(dst_ap_fn(c), sview, AF.Copy, scale=gain[:, c:c + 1]), reads=[st, gain], writes=[dst_tile])

        base_mark = A.top

        def proj_phase(which):
            A.top = base_mark
            isA = which == "A"
            ntile = NT if isA else NOWN
            ncols = KV_COLS if isA else Q_COLS
            nrope = 18 if isA else 24
            xsrc = x_all if isA else x_own
            W = A.t("W", [8, ncols], BF16)
            g1t = A.t("g1t", [8], F32)
            fw.dma("sp", lambda e: e.dma_start(out=g1t[:], in_=g1), "g1t", writes=[g1t])
            gcol = A.t("gcol", [nrope, 64], F32)
            gsrc = gcolK if isA else gcolQ
            fw.dma("sp", lambda e: e.dma_start(out=gcol[:].rearrange("p a b -> p (a b)"),
                                               in_=gsrc.partition_broadcast(128).rearrange("p a b -> p (a b)")),
                   "gcol", writes=[gcol])
            cs = A.t("cs", [ntile, 32], F32)
            sn = A.t("sn", [ntile, 32], F32)
            fw.dma("sp", lambda e: e.dma_start(out=cs[:], in_=cs_all if isA else cs_own), "cs", writes=[cs])
            fw.dma("sp", lambda e: e.dma_start(out=sn[:], in_=sn_all if isA else sn_own), "sn", writes=[sn])
            wsrc = wkv if isA else wq
            csplit = 1152 if isA else 1572
            W_lo, W_hi = Res("W_lo"), Res("W_hi")

            xin = [A.t(f"xin{i}", [D], F32) for i in range(2)]
            junk = A.t("junk", [D], BF16)
            hb = [A.t(f"hb{i}", [D], BF16) for i in range(2)]
            stat = [A.t(f"stat{i}", [4], F32) for i in range(2)]
            sq = [A.t(f"sq{i}", [512], F32) for i in range(2)]
            ssh = [A.t(f"ssh{i}", [nrope], F32) for i in range(2)]
            kn = [A.t(f"kn{i}", [nrope, 64], F32) for i in range(2)]
            tr = [A.t(f"tr{i}", [4, nrope, 32], F32) for i in range(2)]
            kr = [A.t(f"kr{i}", [nrope, 64], BF16) for i in range(2)]
            pTh = bank(7, 1024, BF16, "pTh")
            pTk = [bank(5, 1024, BF16, "pTkA"), bank(6, 1024, BF16, "pTkB")]
            nmm = 5
            pmm = [bank(b, 512, F32, "pmm") for b in range(nmm)]
            mmctr = [0]
            if isA:
                hT = [A.t(f"hT{i}", [8, 128], BF16) for i in range(2)]
                KTst = [A.t(f"KTst{i}", [12, 512], BF16) for i in range(2)]
                Vst = [A.t(f"Vst{i}", [4, 1554], BF16) for i in range(2)]
                for v_ in Vst:
                    fw.op("pool", MSET(v_[:], 1.0), writes=[v_])
            else:
                hT = [A.t(f"hTs{i}", [8, 512], BF16) for i in range(2)]
                QTst = [A.t(f"QTst{i}", [12, 512], BF16) for i in range(2)]
                gst = [A.t(f"gst{i}", [512], BF16) for i in range(4)]

            def s_load(i):
                t = xin[i % 2]
                fw.dma("sp", lambda e: e.dma_start(out=t[:], in_=xsrc[i * 128:(i + 1) * 128, :]), f"xin{i % 2}", writes=[t])

            def s_norm(i):
                p = i % 2
                x, st, h = xin[p], stat[p], hb[p]
                fw.op("act", lambda e: e.activation(out=junk[:], in_=x[:], func=AF.Square, accum_out=st[:, 0:1]),
                      reads=[x], writes=[junk, st])
                fw.op("act", lambda e: e.activation(out=st[:, 1:2], in_=st[:, 0:1], func=AF.Sqrt, scale=1.0 / D, bias=epsb[:, 0:1]),
                      reads=[st, epsb], writes=[st])
                fw.op("dve", lambda e: e.reciprocal(out=st[:, 2:3], in_=st[:, 1:2]), reads=[st], writes=[st])
                fw.op("dve", lambda e: e.tensor_scalar(out=h[:], in0=x[:], scalar1=st[:, 2:3], scalar2=None, op0=ALU.mult),
                      reads=[x, st], writes=[h])

            def s_transp_h(i):
                p = i % 2
                h = hb[p]
                fw.group("pe", [lambda e, kc=kc: e.transpose(out=pTh[:, kc * 128:(kc + 1) * 128], in_=h[:, kc * 128:(kc + 1) * 128],
                                                            identity=idb[:]) for kc in range(8)],
                         reads=[h, idb], writes=[pTh])
                if isA:
                    dst = hT[p]
                    fw.op("act", lambda e: e.copy(out=dst[:].rearrange("p a b -> p (a b)"), in_=pTh[:]), reads=[pTh], writes=[dst])
                else:
                    dst = hT[(i // 4) % 2]
                    sub = i % 4
                    fw.op("act", lambda e: e.copy(out=dst[:, :, sub * 128:(sub + 1) * 128],
                                                  in_=pTh[:].rearrange("p (a b) -> p a b", a=8)), reads=[pTh], writes=[dst])

            def lhs_of(i, kc):
                if isA:
                    return hT[i % 2][:, kc, :]
                return hT[(i // 4) % 2][:, kc, (i % 4) * 128:(i % 4 + 1) * 128]

            def hT_of(i):
                return hT[i % 2] if isA else hT[(i // 4) % 2]

            def mm_block(i, c0, w):
                pm = pmm[mmctr[0] % nmm]
                mmctr[0] += 1
                fw.group("pe", [lambda e, kc=kc: e.matmul(pm[:, 0:w], lhsT=lhs_of(i, kc), rhs=W[:, kc, c0:c0 + w],
                                                          start=(kc == 0), stop=(kc == 7)) for kc in range(8)],
                         reads=[hT_of(i), W_lo if c0 + w <= csplit else W_hi], writes=[pm])
                return pm

            def rope_epilogue(i, blocks):
                p = i % 2
                k_n, s_h, t_r, k_r, s_q = kn[p], ssh[p], tr[p], kr[p], sq
                for bi, (pm, h0, nh) in enumerate(blocks):
                    sqt = s_q[bi % 2]
                    w = nh * 64
                    fw.op("act", lambda e, pm=pm, sqt=sqt, w=w: e.activation(out=sqt[:, 0:w], in_=pm[:, 0:w], func=AF.Square),
                          reads=[pm], writes=[sqt])
                    fw.op("dve", lambda e, sqt=sqt, h0=h0, nh=nh, w=w: e.tensor_reduce(
                        out=s_h[:, h0:h0 + nh], in_=sqt[:, 0:w].rearrange("p (a b) -> p a b", a=nh), axis=AX.X, op=ALU.add),
                        reads=[sqt], writes=[s_h])
                fw.op("act", lambda e: e.activation(out=s_h[:], in_=s_h[:], func=AF.Sqrt, scale=1.0 / 64, bias=epsb[:, 0:1]),
                      reads=[s_h, epsb], writes=[s_h])
                fw.op("dve", lambda e: e.reciprocal(out=s_h[:], in_=s_h[:]), reads=[s_h], writes=[s_h])
                for (pm, h0, nh) in blocks:
                    w = nh * 64
                    fw.op("dve", lambda e, pm=pm, h0=h0, nh=nh, w=w: e.tensor_tensor(
                        out=k_n[:, h0:h0 + nh, :], in0=pm[:, 0:w].rearrange("p (a b) -> p a b", a=nh),
                        in1=bc(s_h[:, h0:h0 + nh].unsqueeze(2), [128, nh, 64]), op=ALU.mult),
                        reads=[pm, s_h], writes=[k_n])
                fw.op("dve", lambda e: e.tensor_tensor(out=k_n[:], in0=k_n[:], in1=gcol[:], op=ALU.mult),
                      reads=[k_n, gcol], writes=[k_n])
                cosb = bc(cs[:, i, :].unsqueeze(1), [128, nrope, 32])
                sinb = bc(sn[:, i, :].unsqueeze(1), [128, nrope, 32])
                x1 = k_n[:, :, 0:32]
                x2 = k_n[:, :, 32:64]
                fw.op("pool", lambda e: e.tensor_tensor(out=t_r[:, 0], in0=x1, in1=cosb, op=ALU.mult), reads=[k_n, cs], writes=[t_r])
                fw.op("pool", lambda e: e.tensor_tensor(out=t_r[:, 1], in0=x2, in1=sinb, op=ALU.mult), reads=[k_n, sn], writes=[t_r])
                fw.op("pool", lambda e: e.tensor_tensor(out=t_r[:, 2], in0=x2, in1=cosb, op=ALU.mult), reads=[k_n, cs], writes=[t_r])
                fw.op("pool", lambda e: e.tensor_tensor(out=t_r[:, 3], in0=x1, in1=sinb, op=ALU.mult), reads=[k_n, sn], writes=[t_r])
                fw.op("pool", lambda e: e.tensor_tensor(out=k_r[:, :, 0:32], in0=t_r[:, 0], in1=t_r[:, 1], op=ALU.subtract),
                      reads=[t_r], writes=[k_r])
                fw.op("pool", lambda e: e.tensor_tensor(out=k_r[:, :, 32:64], in0=t_r[:, 2], in1=t_r[:, 3], op=ALU.add),
                      reads=[t_r], writes=[k_r])

            def s_mm_A(i):
                blocks = []
                for (c0, w, h0, nh) in [(0, 512, 0, 8), (512, 512, 8, 8), (1024, 128, 16, 2)]:
                    blocks.append((mm_block(i, c0, w), h0, nh))
                rope_epilogue(i, blocks)
                vs = Vst[(i // 4) % 2]
                t4 = i % 4
                pm = mm_block(i, 1152, 512)
                fw.op("act", ACP(vs[:, t4, 0:520].rearrange("p (h c) -> p h c", c=65)[:, :, 0:64], pm[:].rearrange("p (h c) -> p h c", c=64)),
                      reads=[pm], writes=[vs])
                pm = mm_block(i, 1664, 512)
                fw.op("dve", TC(vs[:, t4, 520:1040].rearrange("p (h c) -> p h c", c=65)[:, :, 0:64], pm[:].rearrange("p (h c) -> p h c", c=64)),
                      reads=[pm], writes=[vs])
                pm = mm_block(i, 2176, 512)
                fw.op("act", ACP(vs[:, t4, 1040:1170].rearrange("p (h c) -> p h c", c=65)[:, :, 0:64], pm[:, 0:128].rearrange("p (h c) -> p h c", c=64)),
                      reads=[pm], writes=[vs])
                fw.op("act", ACP(vs[:, t4, 1170:1554], pm[:, 128:512]), reads=[pm], writes=[vs])
                if i % 4 == 3:
                    tb = i // 4
                    fw.dma("sp", DMA(V_scr[0:128, 4 * tb:4 * tb + 4, :], vs[:]), f"Vst{(i // 4) % 2}", reads=[vs], writes=[r_Vscr])

            def s_transp_k_A(i):
                p = i % 2
                k_r = kr[p]
                st = KTst[(i // 4) % 2]
                sub = i % 4
                fns = []
                for j in range(6):
                    fns.append(lambda e, j=j: e.transpose(out=pTk[0][0:64, j * 128:(j + 1) * 128], in_=k_r[:, j, :], identity=idb[:]))
                for j in range(2):
                    fns.append(lambda e, j=j: e.transpose(out=pTk[0][:, (6 + j) * 128:(7 + j) * 128],
                                                          in_=k_r[:, 6 + 2 * j:8 + 2 * j, :].rearrange("p a b -> p (a b)"), identity=idb[:]))
                fw.group("pe", fns, reads=[k_r, idb], writes=[pTk[0]])
                fns = []
                for j in range(2, 6):
                    fns.append(lambda e, j=j: e.transpose(out=pTk[1][:, (j - 2) * 128:(j - 1) * 128],
                                                          in_=k_r[:, 6 + 2 * j:8 + 2 * j, :].rearrange("p a b -> p (a b)"), identity=idb[:]))
                fw.group("pe", fns, reads=[k_r, idb], writes=[pTk[1]])
                cols = slice(sub * 128, (sub + 1) * 128)
                fw.op("act", lambda e: e.copy(out=st[0:64, 0:6, cols], in_=pTk[0][0:64, 0:768].rearrange("p (a b) -> p a b", a=6)),
                      reads=[pTk[0]], writes=[st])
                fw.op("dve", lambda e: e.tensor_copy(out=st[:, 6:8, cols], in_=pTk[0][:, 768:1024].rearrange("p (a b) -> p a b", a=2)),
                      reads=[pTk[0]], writes=[st])
                fw.op("act", lambda e: e.copy(out=st[:, 8:12, cols], in_=pTk[1][:, 0:512].rearrange("p (a b) -> p a b", a=4)),
                      reads=[pTk[1]], writes=[st])
                if sub == 3:
                    t0 = (i // 4) * 512
                    key = f"KTst{(i // 4) % 2}"
                    fw.dma("sp", lambda e: e.dma_start(out=KsT_scr[:, :, t0:t0 + 512], in_=st[0:64, 0:3, :]), key, reads=[st], writes=[r_Kscr])
                    fw.dma("sp", lambda e: e.dma_start(out=KwT_scr[:, :, t0:t0 + 512], in_=st[0:64, 3:6, :]), key, reads=[st], writes=[r_Kscr])
                    fw.dma("sp", lambda e: e.dma_start(out=KbT_scr[:, :, t0:t0 + 512], in_=st[:, 6:12, :]), key, reads=[st], writes=[r_Kscr])

            def s_mm_B(i):
                blocks = []
                for j in range(3):
                    blocks.append((mm_block(i, 512 * j, 512), 8 * j, 8))
                rope_epilogue(i, blocks)
                pmg = mm_block(i, 1536, 36)
                fw.op("act", lambda e: e.activation(out=gn[:, i, :], in_=pmg[:, 0:36], func=AF.Sigmoid), reads=[pmg], writes=[gn])
                if i % 4 == 3:
                    slot = i // 4
                    hTs = hT[slot % 2]
                    for cc in range(16):
                        pm = pmm[mmctr[0] % nmm]
                        mmctr[0] += 1
                        c0 = 1572 + cc * 128
                        fw.group("pe", [lambda e, kc=kc, pm=pm, c0=c0: e.matmul(pm[:], lhsT=W[:, kc, c0:c0 + 128], rhs=hTs[:, kc, :],
                                                                              start=(kc == 0), stop=(kc == 7)) for kc in range(8)],
                                 reads=[hTs, W_hi], writes=[pm])
                        g = gst[cc % 4]
                        fw.op("act", lambda e, pm=pm, g=g: e.activation(out=g[:], in_=pm[:], func=AF.Sigmoid), reads=[pm], writes=[g])
                        fw.dma("sp", lambda e, g=g, cc=cc: e.dma_start(out=gT_scr[:, cc, slot * 512:(slot + 1) * 512], in_=g[:]),
                               f"gst{cc % 4}", reads=[g], writes=[r_gscr])

            def s_transp_q_B(i):
                p = i % 2
                k_r = kr[p]
                st = QTst[(i // 4) % 2]
                sub = i % 4
                for half in range(2):
                    fns = [lambda e, j=j, half=half: e.transpose(out=pTk[half][:, j * 128:(j + 1) * 128],
                                                                 in_=k_r[:, 12 * half + 2 * j:12 * half + 2 * j + 2, :].rearrange("p a b -> p (a b)"),
                                                                 identity=idb[:]) for j in range(6)]
                    fw.group("pe", fns, reads=[k_r, idb], writes=[pTk[half]])
                    eng = "act" if half == 0 else "dve"
                    dst = st[:, 6 * half:6 * half + 6, sub * 128:(sub + 1) * 128]
                    src = pTk[half][:, 0:768].rearrange("p (a b) -> p a b", a=6)
                    if eng == "act":
                        fw.op("act", lambda e, dst=dst, src=src: e.copy(out=dst, in_=src), reads=[pTk[half]], writes=[st])
                    else:
                        fw.op("dve", lambda e, dst=dst, src=src: e.tensor_copy(out=dst, in_=src), reads=[pTk[half]], writes=[st])
                if sub == 3:
                    t0 = (i // 4) * 512
                    key = f"QTst{(i // 4) % 2}"
                    qa_v = QaT_scr.rearrange("p (j two) t -> p j two t", two=2)
                    fw.dma("sp", lambda e: e.dma_start(out=qa_v[:, :, 0, t0:t0 + 512], in_=st[0:64, 0:6, :]), key, reads=[st], writes=[r_Qscr])
                    fw.dma("sp", lambda e: e.dma_start(out=qa_v[:, :, 1, t0:t0 + 512], in_=st[64:128, 0:6, :]), key, reads=[st], writes=[r_Qscr])
                    fw.dma("sp", lambda e: e.dma_start(out=QbT_scr[:, :, t0:t0 + 512], in_=st[:, 6:12, :]), key, reads=[st], writes=[r_Qscr])

            s_mm = s_mm_A if isA else s_mm_B
            s_tk = s_transp_k_A if isA else s_transp_q_B
            s_load(0)
            s_load(1)
            load_weight(W_lo, lambda c: W[:, c, 0:csplit], lambda c: wsrc[:, c, 0:csplit], 8, csplit, gain=g1t)
            s_norm(0)
            s_load(2)
            s_transp_h(0)
            s_norm(1)
            load_weight(W_hi, lambda c: W[:, c, csplit:ncols], lambda c: wsrc[:, c, csplit:ncols], 8, ncols - csplit, gain=g1t)
            for i in range(ntile):
                if i + 1 < ntile:
                    s_transp_h(i + 1)
                if i + 2 < ntile:
                    s_norm(i + 2)
                if i + 3 < ntile:
                    s_load(i + 3)
                s_mm(i)
                if i >= 1:
                    s_tk(i - 1)
            s_tk(ntile - 1)
            fw.barrier()

        def cmp_phase(kcT, vc1):
            C2 = 2.0 * math.sqrt(2.0 / math.pi)
            w1b = A.t("w1b", [16, 256], BF16)
            w2b = A.t("w2b", [2, 64], BF16)
            posb = A.t("posb", [16], BF16)
            w2f = A.t("w2f", [2, 64], F32)
            posf = A.t("posf", [16], F32)
            gc = A.t("gc", [64], F32)
            csc = A.t("csc", [2, 32], F32)
            snc = A.t("snc", [2, 32], F32)
            fw.dma("sp", DMA(gc[:], gcmp.partition_broadcast(128).rearrange("p a b -> p (a b)")), "gc", writes=[gc])
            fw.dma("sp", DMA(csc[:], cs_cmp), "csc", writes=[csc])
            fw.dma("sp", DMA(snc[:], sn_cmp), "snc", writes=[snc])
            kc2 = A.t("kc2", [NT, 3, 2, 64], BF16)
            kT2 = A.t("kT2", [3, S], BF16)
            kcv = A.t("kcv", [NT, 384], BF16)
            kcv_sh = A.t("kcv_sh", [NT, 384], BF16)
            fw.dma("sp", DMA(kcv[:], V_scr[0:128, :, 1170:1554]), "kcv", reads=[r_Vscr], writes=[kcv])
            fw.dma("sp", DMA(kcv_sh[:, :, :], V_scr[1:129, :, 1170:1554]), "kcvsh", reads=[r_Vscr], writes=[kcv_sh])
            fw.dma("sp", DMA(kcv_sh[127:128, 0:NT - 1, :], V_scr[0:1, 1:NT, 1170:1554]), "kcvsh", reads=[r_Vscr], writes=[kcv_sh])
            fw.dma("sp", DMA(kcv_sh[127:128, NT - 1, :], expand[0:1, 64:448]), "kcvsh", writes=[kcv_sh])
            biasT = A.t("biasT", [2], F32)
            xs = A.t("xs", [256], F32)
            x2 = A.t("x2", [256], F32)
            uu = A.t("uu", [256], F32)
            sg = A.t("sg", [256], F32)
            gTs = [A.t(f"gTc{i}", [2, 256], BF16) for i in range(2)]
            kcrs = [A.t(f"kcr{i}", [2, 128], BF16) for i in range(2)]
            kcn = A.t("kcn", [2, 64], F32)
            ktr = A.t("ktr", [4, 2, 32], F32)
            sqc = A.t("sqc", [2, 64], F32)
            stc = A.t("stc", [4], F32)
            for t_ in gTs + kcrs:
                fw.op("pool", MSET(t_[:], 0.0), writes=[t_])
            fw.op("pool", MSET(vc1[:], 1.0), writes=[vc1])
            pT = [bank(0, 1024, BF16, "cpT"), bank(1, 1024, BF16, "cpT")]
            pH = [bank(2, 512, F32, "cpH"), bank(3, 512, F32, "cpH")]
            pB = bank(4, 512, F32, "cpB")
            pK = bank(5, 512, F32, "cpK")
            pKT = bank(6, 1024, BF16, "cpKT")
            for kv in range(2):
                colbase = 1152 + 192 * kv
                w1src, w2src, possrc = (w1k, w2k, posk) if kv == 0 else (w1v, w2v, posv)
                load_weight(w1b, lambda c: w1b[:, 8 * c:8 * c + 8, :], lambda c, w1src=w1src: w1src[:, 8 * c:8 * c + 8, :], 2, 2048, as3=8)
                fw.dma("sp", DMA(w2f[:], w2src), "w2f", writes=[w2f])
                fw.dma("sp", DMA(posf[:], possrc), "posf", writes=[posf])
                fw.op("dve", TC(w2b[:], w2f[:]), reads=[w2f], writes=[w2b])
                fw.op("dve", TC(posb[:], posf[:]), reads=[posf], writes=[posb])
                fw.op("dve", TC(kc2[:, :, :, 0, :], kcv[:, :, 192 * kv:192 * kv + 192].rearrange("p t (g d) -> p t g d", g=3)), reads=[kcv], writes=[kc2])
                fw.op("act", ACP(kc2[:, :, :, 1, :], kcv_sh[:, :, 192 * kv:192 * kv + 192].rearrange("p t (g d) -> p t g d", g=3)), reads=[kcv_sh], writes=[kc2])
                n = 0
                for g in range(3):
                    for tb in range(4):
                        p = pT[n % 2]
                        n += 1
                        fw.group("pe", [TRN(p[:, j * 128:(j + 1) * 128], kc2[:, tb * 8 + j, g, :, :].rearrange("p a b -> p (a b)"), idb[:])
                                        for j in range(8)], reads=[kc2, idb], writes=[p])
                        if n % 2 == 0:
                            fw.op("act", ACP(kT2[:, g, tb * 1024:(tb + 1) * 1024], p[:]), reads=[p], writes=[kT2])
                        else:
                            fw.op("dve", TC(kT2[:, g, tb * 1024:(tb + 1) * 1024], p[:]), reads=[p], writes=[kT2])
                for hc in range(2):
                    fw.group("pe", [MM(pB[:, hc:hc + 1], w1b[:, j, hc * 128:(hc + 1) * 128], posb[:, j:j + 1], start=(j == 0), stop=(j == 15))
                                    for j in range(16)], reads=[w1b, posb], writes=[pB])
                fw.op("dve", TC(biasT[:], pB[:, 0:2]), reads=[pB], writes=[biasT])
                def hidden(g):
                    gTg = gTs[g % 2]
                    for hc in range(2):
                        ph = pH[hc]
                        fw.group("pe", [MM(ph[:, 0:255], w1b[:, j, hc * 128:(hc + 1) * 128], kT2[:, g, 2 * j:2 * j + 16 * 254 + 1:16],
                                           start=(j == 0), stop=(j == 15)) for j in range(16)], reads=[w1b, kT2], writes=[ph])
                        fw.op("dve", TS(xs[:, 0:255], ph[:, 0:255], biasT[:, hc:hc + 1], ALU.add), reads=[ph, biasT], writes=[xs])
                        fw.op("pool", TT(x2[:, 0:255], xs[:, 0:255], xs[:, 0:255], ALU.mult), reads=[xs], writes=[x2])
                        fw.op("dve", TS(x2[:, 0:255], x2[:, 0:255], 0.044715, ALU.mult, 1.0, ALU.add), reads=[x2], writes=[x2])
                        fw.op("pool", TT(uu[:, 0:255], x2[:, 0:255], xs[:, 0:255], ALU.mult), reads=[x2, xs], writes=[uu])
                        fw.op("act", ACTF(sg[:, 0:255], uu[:, 0:255], AF.Sigmoid, scale=C2), reads=[uu], writes=[sg])
                        fw.op("dve", TT(gTg[:, hc, 0:255], xs[:, 0:255], sg[:, 0:255], ALU.mult), reads=[xs, sg], writes=[gTg])

                def second(g, kv=kv):
                    gTg = gTs[g % 2]
                    kcr_g = kcrs[g % 2]
                    for ct in range(2):
                        fw.group("pe", [MM(pK[:, ct * 64:(ct + 1) * 64], gTg[:, hc, ct * 128:(ct + 1) * 128], w2b[:, hc, :],
                                           start=(hc == 0), stop=(hc == 1)) for hc in range(2)], reads=[gTg, w2b], writes=[pK])
                    if kv == 1:
                        fw.op("act", ACP(vc1[:, :, g, 0:64], pK[:, 0:128].rearrange("p (a b) -> p a b", a=2)), reads=[pK], writes=[vc1])
                        return None
                    pk3 = pK[:, 0:128].rearrange("p (a b) -> p a b", a=2)
                    fw.op("act", ACTF(sqc[:], pk3, AF.Square), reads=[pK], writes=[sqc])
                    fw.op("dve", lambda e: e.tensor_reduce(out=stc[:, 0:2], in_=sqc[:], axis=AX.X, op=ALU.add), reads=[sqc], writes=[stc])
                    fw.op("act", ACTF(stc[:, 0:2], stc[:, 0:2], AF.Sqrt, scale=1.0 / 64, bias=epsb[:, 0:1]), reads=[stc, epsb], writes=[stc])
                    fw.op("dve", RECIP(stc[:, 0:2], stc[:, 0:2]), reads=[stc], writes=[stc])
                    fw.op("dve", TT(kcn[:], pk3, bc(stc[:, 0:2].unsqueeze(2), [128, 2, 64]), ALU.mult), reads=[pK, stc], writes=[kcn])
                    fw.op("pool", TT(kcn[:], kcn[:], bc(gc[:].unsqueeze(1), [128, 2, 64]), ALU.mult), reads=[kcn, gc], writes=[kcn])
                    x1, xx2 = kcn[:, :, 0:32], kcn[:, :, 32:64]
                    fw.op("pool", TT(ktr[:, 0], x1, csc[:], ALU.mult), reads=[kcn, csc], writes=[ktr])
                    fw.op("pool", TT(ktr[:, 1], xx2, snc[:], ALU.mult), reads=[kcn, snc], writes=[ktr])
                    fw.op("pool", TT(ktr[:, 2], xx2, csc[:], ALU.mult), reads=[kcn, csc], writes=[ktr])
                    fw.op("pool", TT(ktr[:, 3], x1, snc[:], ALU.mult), reads=[kcn, snc], writes=[ktr])
                    fw.op("pool", TT(kcr_g[:, :, 0:32], ktr[:, 0], ktr[:, 1], ALU.subtract), reads=[ktr], writes=[kcr_g])
                    fw.op("pool", TT(kcr_g[:, :, 32:64], ktr[:, 2], ktr[:, 3], ALU.add), reads=[ktr], writes=[kcr_g])

                    def trp(g=g, kcr_g=kcr_g):
                        fw.group("pe", [TRN(pKT[:, ct * 128:(ct + 1) * 128], kcr_g[:, ct, :], idb[:]) for ct in range(2)],
                                 reads=[kcr_g, idb], writes=[pKT])
                        fw.op("act", ACP(kcT[:, g, :], pKT[:, 0:256]), reads=[pKT], writes=[kcT])
                    return trp

                trq = []
                for g in range(3):
                    hidden(g)
                    if trq:
                        trq.pop(0)()
                    if g >= 1:
                        t_ = second(g - 1)
                        if t_ is not None:
                            trq.append(t_)
                t_ = second(2)
                if t_ is not None:
                    trq.append(t_)
                while trq:
                    trq.pop(0)()
            fw.barrier()

        def share(tile, ap, name):
            t = Tile(ap, name)
            t.res = tile.res
            return t

        def attn_units(units, q_of, k_of, v_of, mask_of, pO_of, first_of, last_of, pS, Pt, ctr, kqv, hooks=None):
            LAG = 3
            pend = []
            Kres, Qres, Vres = kqv
            hooks = hooks or {}

            def emit_pv(u, pt):
                po = pO_of(u)
                fw.group("pe", [MM(po[0:65, :], v_of(u), pt[:], start=first_of(u), stop=last_of(u))],
                         reads=[pt, Vres(u) if callable(Vres) else Vres], writes=[po])

            for ui, u in enumerate(units):
                if ui in hooks:
                    hooks[ui]()
                ps = pS[ctr[0] % len(pS)]
                pt = Pt[ctr[0] % len(Pt)]
                ctr[0] += 1
                fw.group("pe", [MM(ps[:], k_of(u), q_of(u))], reads=[Kres(u) if callable(Kres) else Kres, Qres], writes=[ps])
                fw.op("act", ACTF(pt[:], ps[:], AF.Exp, scale=SCALE), reads=[ps], writes=[pt])
                m = mask_of(u)
                if m is not None:
                    fw.op("dve", TT(pt[:], pt[:], m[0], ALU.mult), reads=[pt, m[1]], writes=[pt])
                pend.append((u, pt))
                if len(pend) > LAG:
                    emit_pv(*pend.pop(0))
            while pend:
                emit_pv(*pend.pop(0))

        def finalize(pO_list, heads_cols, coef_fn, yacc, first_branch, OTsb, pTf, s, rd4, ytmp, defer=False):
            for i, po in enumerate(pO_list):
                ot = OTsb[i % len(OTsb)]
                fw.op("dve", TC(ot[0:65, :], po[0:65, :]), reads=[po], writes=[ot])
            parts = []
            for i, po in enumerate(pO_list):
                parts.append(lambda i=i: fin_head(i, pO_list, heads_cols, coef_fn, yacc, first_branch, OTsb, pTf, rd4, ytmp))
            if defer:
                return parts
            for p in parts:
                p()
            return []

        def fin_head(i, pO_list, heads_cols, coef_fn, yacc, first_branch, OTsb, pTf, rd4, ytmp):
            if True:
                ot = OTsb[i % len(OTsb)]
                fw.group("pe", [TRN(pTf[:, sub * 65:(sub + 1) * 65], ot[0:65, sub * 128:(sub + 1) * 128], idf[0:65, 0:65]) for sub in range(4)],
                         reads=[ot, idf], writes=[pTf])
                p3 = pTf[:, 0:260].rearrange("p (a b) -> p a b", a=4)
                fw.op("dve", TS(rd4[:], p3[:, :, 64], 1e-30, ALU.max), reads=[pTf], writes=[rd4])
                fw.op("dve", RECIP(rd4[:], rd4[:]), reads=[rd4], writes=[rd4])
                gate = coef_fn(i)
                if gate is not None:
                    fw.op("dve", TT(rd4[:], rd4[:], gate, ALU.mult), reads=[rd4, gn], writes=[rd4])
                hc = heads_cols[i]
                cb = bc(rd4[:].unsqueeze(2), [128, 4, 64])
                if first_branch:
                    fw.op("dve", TT(yacc[:, :, hc, :], p3[:, :, 0:64], cb, ALU.mult), reads=[pTf, rd4], writes=[yacc])
                else:
                    fw.op("dve", TT(ytmp[:], p3[:, :, 0:64], cb, ALU.mult), reads=[pTf, rd4], writes=[ytmp])
                    fw.op("pool", TT(yacc[:, :, hc, :], yacc[:, :, hc, :], ytmp[:], ALU.add), reads=[ytmp, yacc], writes=[yacc])

        def y_to_scratch(yacc, nchunk, chunk0, s, ybf, yst, pY):
            fw.op("dve", TC(ybf[:], yacc[:].rearrange("p a h d -> p a (h d)")), reads=[yacc], writes=[ybf])
            for c in range(nchunk):
                pb = pY[c // 2]
                fw.group("pe", [TRN(pb[:, (c % 2) * 512 + sub * 128:(c % 2) * 512 + (sub + 1) * 128], ybf[:, sub, c * 128:(c + 1) * 128], idb[:])
                                for sub in range(4)], reads=[ybf, idb], writes=[pb])
            for c2 in range((nchunk + 1) // 2):
                n2 = min(2, nchunk - 2 * c2)
                src = pY[c2][:, 0:n2 * 512].rearrange("p (a b) -> p a b", a=n2)
                if c2 % 2 == 0:
                    fw.op("act", ACP(yst[:, 2 * c2:2 * c2 + n2, :], src), reads=[pY[c2]], writes=[yst])
                else:
                    fw.op("dve", TC(yst[:, 2 * c2:2 * c2 + n2, :], src), reads=[pY[c2]], writes=[yst])
            fw.dma("sp", DMA(yT_scr[:, chunk0:chunk0 + nchunk, s * 512:(s + 1) * 512], yst[:, 0:nchunk, :]), "yst", reads=[yst], writes=[r_yscr])

        def nsa_phase(kcT, vc1):
            KsT = A.t("KsT", [3, S], BF16)
            KwT = A.t("KwT", [3, S], BF16)
            Vs1 = A.t("Vs1", [NT, 3, 65], BF16)
            Vw1 = A.t("Vw1", [NT, 3, 65], BF16)
            tbc_t = A.t("tbc_t", [2, L_C], BF16)
            tbw_t = A.t("tbw_t", [2, L_W], BF16)
            selA_t = A.t("selA_t", [4, 4, 64], F32)
            selB_t = A.t("selB_t", [4, 4, 64], F32)
            ov1b = A.t("ov1b", [2, 65], BF16)
            Qa_bufs = [A.t("Qa", [12, 512], BF16),
                       Tile(big[:, stage_off // 4:stage_off // 4 + 3072].bitcast(BF16).rearrange("p (a b) -> p a b", a=12), "Qa2")]
            cm = A.t("cm", [2, 512], BF16)
            E8 = [A.t(f"E8_{i}", [512], BF16) for i in range(8)]
            cctr = [0]
            Pt = [A.t(f"Pt{i}", [512], BF16) for i in range(6)]
            OTsb = [A.t(f"OTsb{i}", [512], F32) for i in range(4)]
            yacc = A.t("yacc", [4, 12, 64], F32)
            ybf = A.t("ybf", [4, 768], BF16)
            yst = A.t("yst", [6, 512], BF16)
            imp = A.t("imp", [4, 64], F32)
            itmp = A.t("itmp", [4, 64], F32)
            score = A.t("score", [4, 64], F32)
            sc2 = A.t("sc2", [4, 64], F32)
            m8 = A.t("m8", [2, 8], F32)
            negsel_g = [A.t(f"negsel{g}", [4, 128], BF16) for g in range(3)]
            rd4 = A.t("rd4", [4], F32)
            ytmp = A.t("ytmp", [4, 64], F32)
            fw.op("pool", MSET(KwT[64:128, :, :], 0.0), writes=[KwT])
            Qg2 = [[Res(f"Qg{b}_{g}") for g in range(3)] for b in range(2)]
            for b_ in range(2):
                fw.op("pool", MSET(Qa_bufs[b_][64:128, :, :], 0.0), writes=Qg2[b_])

            def load_qa(s_):
                fw.dma("sp", DMA(Qa_bufs[s_ % 2][0:64, :, :], QaT_scr[:, :, s_ * 512:(s_ + 1) * 512]), f"Qal{s_ % 2}", reads=[r_Qscr], writes=Qg2[s_ % 2])
            for ng in negsel_g:
                fw.op("pool", MSET(ng[:], 0.0), writes=[ng])
            fw.dma("sp", DMA(tbc_t[:], tbc), "tbc", writes=[tbc_t])
            fw.dma("sp", DMA(tbw_t[:], tbw), "tbw", writes=[tbw_t])
            fw.dma("sp", DMA(selA_t[:], selA), "selA", writes=[selA_t])
            fw.dma("sp", DMA(selB_t[:], selB), "selB", writes=[selB_t])
            fw.dma("sp", DMA(ov1b[:], ov1), "ov1b", writes=[ov1b])
            pO = [bank(b, 512, F32, "pO") for b in range(4)]
            pS = [bank(b, 512, F32, "pS") for b in (4, 5, 6)]
            b7 = bank(7, 512, F32, "b7")
            b7b = share(b7, psum[:, 512 * 7:512 * 8].bitcast(BF16), "b7b")
            pY = [share(pS[i], psum[:, 512 * (4 + i):512 * (5 + i)].bitcast(BF16), "pY") for i in range(3)]
            ctr = [0]

            for s in range(4):
                par = s % 2
                Qa, Qg = Qa_bufs[s % 2], Qg2[s % 2]
                if s == 0:
                    load_qa(0)
                fw.dma("sp", DMA(cm[:], cmask[:, s]), "cml", writes=[cm])
                if s == 0:
                    for g in range(3):
                        fw.dma("sp", DMA(KsT[0:64, g, :], KsT_scr[:, g, :]), "KsTl", reads=[r_Kscr], writes=[Res()])
                        fw.dma("sp", DMA(KsT[64:128, g, :], expand), "KsTl", writes=[KsT] if g == 2 else [Res()])
                        fw.dma("sp", DMA(KwT[0:64, g, :], KwT_scr[:, g, :]), "KwTl", reads=[r_Kscr], writes=[KwT] if g == 2 else [Res()])
                    for hq in range(2):
                        fw.dma("sp", DMA(Vs1[:, 16 * hq:16 * hq + 16].rearrange("p t g c -> p t (g c)"), V_scr[0:128, 16 * hq:16 * hq + 16, 0:195]), "Vs1l",
                               reads=[r_Vscr], writes=[Vs1] if hq == 1 else [Res()])
                        fw.dma("sp", DMA(Vw1[:, 16 * hq:16 * hq + 16].rearrange("p t g c -> p t (g c)"), V_scr[0:128, 16 * hq:16 * hq + 16, 195:390]), "Vw1l",
                               reads=[r_Vscr], writes=[Vw1] if hq == 1 else [Res()])
                if s + 1 < 4:
                    load_qa(s + 1)
                def select(g):
                    negsel = negsel_g[g]
                    fw.op("dve", TT(score[:], imp[:], selA_t[:, s], ALU.mult), reads=[imp, selA_t], writes=[score])
                    fw.op("dve", TT(score[:], score[:], selB_t[:, s], ALU.add), reads=[score, selB_t], writes=[score])
                    for sub in range(4):
                        fw.op("dve", lambda e, sub=sub: e.max(out=m8[:, 0, :], in_=score[:, sub, :]), reads=[score], writes=[m8])
                        fw.op("dve", lambda e, sub=sub: e.match_replace(out=sc2[:, sub, :], in_to_replace=m8[:, 0, :], in_values=score[:, sub, :],
                                                                        imm_value=-3.0e38), reads=[score, m8], writes=[sc2])
                        fw.op("dve", lambda e, sub=sub: e.max(out=m8[:, 1, :], in_=sc2[:, sub, :]), reads=[sc2], writes=[m8])
                        fw.op("dve", TS(negsel[:, sub, 64:128], score[:, sub, :], m8[:, 1, 7:8], ALU.is_lt, NEGSEL, ALU.mult),
                              reads=[score, m8], writes=[negsel])
                    def tail(g=g):
                        fw.group("pe", [TRN(b7b[:, sub * 128:(sub + 1) * 128], negsel[:, sub, :], idb[:]) for sub in range(4)],
                                 reads=[negsel, idb], writes=[b7])
                        fw.op("act", ACP(Qa[64:128, 4 * g:4 * g + 4, :], bc(b7b[64:128, 0:512].unsqueeze(1), [64, 4, 512])), reads=[b7], writes=[Qg[g]])
                    return tail

                fin_q = []
                impb = [b7, pS[2]]
                sbk = [pS[0], pS[1]]
                for g in range(3):
                    pend = []

                    def emit_back(hh, e0, e1, g=g):
                        ets = (e0, e1)
                        fw.group("pe", [MM(pO[hh][0:65, :], vc1[:, ct, g, :], ets[ct][:], start=(ct == 0), stop=(ct == 1)) for ct in range(2)],
                                 reads=[e0, e1, vc1], writes=[pO[hh]])
                        ib = impb[hh % 2]
                        fw.group("pe", [MM(ib[:, sub * 65:(sub + 1) * 65], ets[ct][:, sub * 128:(sub + 1) * 128], ov1b[:, ct, :],
                                           start=(ct == 0), stop=(ct == 1)) for sub in range(4) for ct in range(2)],
                                 reads=[e0, e1, ov1b], writes=[ib])
                        p3 = ib[:, 0:260].rearrange("p (a b) -> p a b", a=4)
                        fw.op("dve", TS(rd4[:], p3[:, :, 64], 1e-30, ALU.max), reads=[ib], writes=[rd4])
                        fw.op("dve", RECIP(rd4[:], rd4[:]), reads=[rd4], writes=[rd4])
                        cb = bc(rd4[:].unsqueeze(2), [128, 4, 64])
                        if hh == 0:
                            fw.op("dve", TT(imp[:], p3[:, :, 0:64], cb, ALU.mult), reads=[ib, rd4], writes=[imp])
                        else:
                            fw.op("dve", TT(itmp[:], p3[:, :, 0:64], cb, ALU.mult), reads=[ib, rd4], writes=[itmp])
                            fw.op("pool", TT(imp[:], imp[:], itmp[:], ALU.add), reads=[itmp, imp], writes=[imp])
                        for _ in range(2):
                            if fin_q:
                                fin_q.pop(0)()

                    for hh in range(4):
                        ets = []
                        for ct in range(2):
                            ps = sbk[cctr[0] % 2]
                            et = E8[cctr[0] % 8]
                            cctr[0] += 1
                            fw.group("pe", [MM(ps[:], kcT[:, g, ct * 128:(ct + 1) * 128], Qa[:, 4 * g + hh, :])], reads=[kcT, Qg[g]], writes=[ps])
                            fw.op("act", ACTF(et[:], ps[:], AF.Exp, scale=SCALE), reads=[ps], writes=[et])
                            fw.op("dve", TT(et[:], et[:], cm[:, ct, :], ALU.mult), reads=[et, cm], writes=[et])
                            ets.append(et)
                        pend.append((hh, ets[0], ets[1]))
                        if len(pend) > 2:
                            emit_back(*pend.pop(0))
                    while pend:
                        emit_back(*pend.pop(0))
                    fin_q.extend(finalize([pO[hh] for hh in range(4)], [4 * g + hh for hh in range(4)],
                                          lambda i, g=g, s=s: gn[:, 4 * s:4 * s + 4, 3 * (4 * g + i) + 0], yacc, True, OTsb, b7, s, rd4, ytmp,
                                          defer=True))
                    fin_q.append(select(g))
                pending = fin_q
                for br in (1, 2):
                    for g in range(3):
                        if br == 1:
                            KT, V1, tab, kts = KsT, Vs1, tbc_t, list(range(0, 8 * s + 8))
                            need_mask = lambda kt, s=s: kt >= 8 * s
                        else:
                            KT, V1, tab, kts = KwT, Vw1, tbw_t, list(range(max(0, 8 * s - 4), 8 * s + 8))
                            need_mask = lambda kt: True
                        units = [(kt, hh) for kt in kts for hh in range(4)]

                        def mask_of(u, tab=tab, need_mask=need_mask, s=s, par=par):
                            kt = u[0]
                            if not need_mask(kt):
                                return None
                            off = 1024 * s - 128 * kt + 896
                            return (tab[:, par, off:off + 512], tab)
                        attn_units(units,
                                   q_of=lambda u, g=g: Qa[:, 4 * g + u[1], :],
                                   k_of=lambda u, KT=KT, g=g: KT[:, g, u[0] * 128:(u[0] + 1) * 128],
                                   v_of=lambda u, V1=V1, g=g: V1[:, u[0], g, :],
                                   mask_of=mask_of,
                                   pO_of=lambda u: pO[u[1]],
                                   first_of=lambda u, kts=kts: u[0] == kts[0],
                                   last_of=lambda u, kts=kts: u[0] == kts[-1],
                                   pS=pS, Pt=Pt, ctr=ctr, kqv=(KT.res, Qg[g], V1.res),
                                   hooks={6 + 4 * k: p for k, p in enumerate(pending)})
                        pending = finalize([pO[hh] for hh in range(4)], [4 * g + hh for hh in range(4)],
                                           lambda i, g=g, s=s, br=br: gn[:, 4 * s:4 * s + 4, 3 * (4 * g + i) + br], yacc, False, OTsb, b7, s, rd4, ytmp,
                                           defer=True)
                for p in pending:
                    p()
                y_to_scratch(yacc, 6, 0, s, ybf, yst, pY)
            fw.barrier()

        def dil_phase():
            KbT = A.t("KbT", [6, S], BF16)
            Vb1 = A.t("Vb1", [NT, 12, 65], BF16)
            tabs = [A.t("tbd0_t", [2, L_D0], BF16), A.t("tbd1_t", [2, L_D1], BF16), A.t("tbd2_t", [2, L_D2], BF16)]
            Qb2 = [A.t(f"Qb{i}", [12, 512], BF16) for i in range(2)]
            Pt = [A.t(f"Pt{i}", [512], BF16) for i in range(6)]
            OTsb = [A.t(f"OTsb{i}", [512], F32) for i in range(2)]
            yacc = A.t("yaccb", [4, 4, 64], F32)
            ybf = A.t("ybfb", [4, 256], BF16)
            yst = A.t("ystb", [2, 512], BF16)
            rd4 = A.t("rd4", [4], F32)
            ytmp = A.t("ytmp", [4, 64], F32)
            Kc = [Res(f"KbTc{c}") for c in range(4)]
            Vc = [Res(f"Vb1c{c}") for c in range(4)]
            def load_chunk(c):
                fw.dma("sp", DMA(KbT[:, :, 1024 * c:1024 * c + 1024], KbT_scr[:, :, 1024 * c:1024 * c + 1024]), f"KbTl{c}", reads=[r_Kscr], writes=[Kc[c]])
                fw.dma("sp", DMA(Vb1[:, 8 * c:8 * c + 8, :, :].rearrange("p t h c -> p t (h c)"), V_scr[0:128, 8 * c:8 * c + 8, 390:1170]), f"Vb1l{c}",
                       reads=[r_Vscr], writes=[Vc[c]])
            for q_ in Qb2:
                fw.op("pool", MSET(q_[:], 0.0), writes=[q_])
            for g, (src, t) in enumerate(zip((tbd0, tbd1, tbd2), tabs)):
                fw.dma("sp", DMA(t[:], src), f"tbd{g}", writes=[t])
            load_chunk(0)
            pO = [bank(b, 512, F32, "pO") for b in range(2)]
            pS = [bank(b, 512, F32, "pS") for b in (4, 5, 6)]
            b7 = bank(7, 512, F32, "b7")
            pY = [share(pS[i], psum[:, 512 * (4 + i):512 * (5 + i)].bitcast(BF16), "pY") for i in range(3)]
            ctr = [0]
            def load_qb(s):
                Qb = Qb2[s % 2]
                Qv = Qb[:].rearrange("p (j two) t -> p j two t", two=2)
                fw.dma("sp", DMA(Qv[0:64, :, 0, :], QbT_scr[0:64, :, s * 512:(s + 1) * 512]), f"Qbl{s % 2}", reads=[r_Qscr], writes=[Qb])
                fw.dma("sp", DMA(Qv[64:128, :, 1, :], QbT_scr[64:128, :, s * 512:(s + 1) * 512]), f"Qbl{s % 2}", reads=[r_Qscr], writes=[Qb])
            load_qb(0)
            for s in range(4):
                par = s % 2
                Qb = Qb2[s % 2]
                if s + 1 < 4:
                    load_qb(s + 1)
                if s == 0:
                    for c in range(1, 4):
                        load_chunk(c)
                pending = []
                for hp in range(2):
                    units = []
                    for g, back in enumerate((1, 4, 16)):
                        for kt in range(max(0, 8 * s - back), 8 * s + 8):
                            for e_ in range(2):
                                units.append((g, kt, e_))
                    first, last = units[0], units[-1]

                    def mask_of(u, s=s, par=par):
                        off = 1024 * s - 128 * u[1] + 896
                        t = tabs[u[0]]
                        return (t[:, par, off:off + 512], t)
                    attn_units(units,
                               q_of=lambda u, hp=hp: Qb[:, 4 * u[0] + 2 * hp + u[2], :],
                               k_of=lambda u, hp=hp: KbT[:, 2 * u[0] + hp, u[1] * 128:(u[1] + 1) * 128],
                               v_of=lambda u, hp=hp: Vb1[:, u[1], 4 * u[0] + 2 * hp + u[2], :],
                               mask_of=mask_of,
                               pO_of=lambda u: pO[u[2]],
                               first_of=lambda u, first=first: (u[0], u[1]) == (first[0], first[1]),
                               last_of=lambda u, last=last: (u[0], u[1]) == (last[0], last[1]),
                               pS=pS, Pt=Pt, ctr=ctr, kqv=(lambda u: Kc[u[1] // 8], Qb.res, lambda u: Vc[u[1] // 8]),
                               hooks={6 + 4 * k: p for k, p in enumerate(pending)})
                    pending = finalize([pO[0], pO[1]], [2 * hp, 2 * hp + 1], lambda i: None, yacc, True, OTsb, b7, s, rd4, ytmp, defer=True)
                for p in pending:
                    p()
                y_to_scratch(yacc, 2, 6, s, ybf, yst, pY)
            fw.barrier()

        def e_phase():
            xm = A.t("xm", [NOWN, D], F32)
            xr = [Res(f"xm{i}") for i in range(NOWN)]
            for s4 in range(4):
                fw.dma("pool", DMA(xm[:, 4 * s4:4 * s4 + 4, :], x_own[512 * s4:512 * s4 + 512, :].rearrange("(t p) c -> p t c", p=128)), f"xml{s4}",
                       writes=xr[4 * s4:4 * s4 + 4])
            e1_mark = A.top
            woa_b = A.t("woa_b", [6, 1024], BF16)
            wob_b = A.t("wob_b", [2, 1024], BF16)
            wout_b = A.t("wout_b", [8, 1024], BF16)
            load_weight(woa_b, lambda c: woa_b[:, 2 * c:2 * c + 2, :], lambda c: woa[:, 2 * c:2 * c + 2, :], 3, 2048, as3=2)
            load_weight(wob_b, lambda c: wob_b[:, 0:2, :], lambda c: wob[:, 0:2, :], 1, 2048, as3=2)
            load_weight(wout_b, lambda c: wout_b[:, 2 * c:2 * c + 2, :], lambda c: wout[:, 2 * c:2 * c + 2, :], 4, 2048, as3=2)
            yT = [A.t(f"yT{i}", [8, 512], BF16) for i in range(2)]
            gT = [A.t(f"gT{i}", [16, 512], BF16) for i in range(1)]
            mixT = A.t("mixT", [8, 512], BF16)
            t1 = [A.t(f"t1_{i}", [512], F32) for i in range(2)]
            t2 = [A.t(f"t2_{i}", [512], F32) for i in range(2)]
            junk = A.t("junkE", [D], BF16)
            st2 = A.t("st2", [4], F32)
            h2 = [A.t(f"h2_{i}", [D], BF16) for i in range(2)]
            h2st = [A.t(f"h2st{i}", [8, 512], BF16) for i in range(1)]
            pU = [bank(b, 512, F32, "pU") for b in range(4)]
            pD = [bank(b, 512, F32, "pD") for b in (4, 5)]
            pT2 = [bank(b, 1024, BF16, "pT2") for b in (6, 7)]
            for s in range(4):
                y_t, g_t, hst = yT[s % 2], gT[0], h2st[0]
                if s == 0:
                    fw.dma("sp", DMA(y_t[:], yT_scr[:, :, 0:512]), "yTl0", reads=[r_yscr], writes=[y_t])
                fw.dma("sp", DMA(g_t[:, 0:8, :], gT_scr[:, 0:8, s * 512:(s + 1) * 512]), "gTl", reads=[r_gscr], writes=[g_t])
                fw.dma("pool", DMA(g_t[:, 8:16, :], gT_scr[:, 8:16, s * 512:(s + 1) * 512]), "gTl2", reads=[r_gscr], writes=[g_t])
                if s + 1 < 4:
                    fw.dma("sp", DMA(yT[(s + 1) % 2][:], yT_scr[:, :, (s + 1) * 512:(s + 2) * 512]), f"yTl{(s + 1) % 2}", reads=[r_yscr], writes=[yT[(s + 1) % 2]])
                for cc in range(8):
                    pa, pb = pU[(2 * cc) % 4], pU[(2 * cc + 1) % 4]
                    fw.group("pe", [MM(pa[:], woa_b[:, fc, cc * 128:(cc + 1) * 128], y_t[:, fc, :], start=(fc == 0), stop=(fc == 5)) for fc in range(6)],
                             reads=[woa_b, y_t], writes=[pa])
                    fw.group("pe", [MM(pb[:], wob_b[:, fc, cc * 128:(cc + 1) * 128], y_t[:, 6 + fc, :], start=(fc == 0), stop=(fc == 1)) for fc in range(2)],
                             reads=[wob_b, y_t], writes=[pb])
                    ta, tb = t1[cc % 2], t2[cc % 2]
                    fw.op("dve", TT(ta[:], pa[:], g_t[:, cc, :], ALU.mult), reads=[pa, g_t], writes=[ta])
                    fw.op("dve", TT(tb[:], pb[:], g_t[:, 8 + cc, :], ALU.mult), reads=[pb, g_t], writes=[tb])
                    fw.op("pool", TT(mixT[:, cc, :], ta[:], tb[:], ALU.add), reads=[ta, tb], writes=[mixT])
                tr_pending = []
                for sub in range(4):
                    ti = 4 * s + sub
                    for cb in range(2):
                        pd = pD[cb]
                        fw.group("pe", [MM(pd[:], mixT[:, fc, sub * 128:(sub + 1) * 128], wout_b[:, fc, cb * 512:(cb + 1) * 512],
                                           start=(fc == 0), stop=(fc == 7)) for fc in range(8)], reads=[mixT, wout_b], writes=[pd])
                        fw.op("dve", TT(xm[:, ti, cb * 512:(cb + 1) * 512], pd[:], xm[:, ti, cb * 512:(cb + 1) * 512], ALU.add),
                              reads=[pd, xr[ti]], writes=[xr[ti]])
                    fw.op("act", ACTF(junk[:], xm[:, ti, :], AF.Square, accum=st2[:, 0:1]), reads=[xr[ti]], writes=[junk, st2])
                    fw.op("act", ACTF(st2[:, 1:2], st2[:, 0:1], AF.Sqrt, scale=1.0 / D, bias=epsb[:, 0:1]), reads=[st2, epsb], writes=[st2])
                    fw.op("dve", RECIP(st2[:, 2:3], st2[:, 1:2]), reads=[st2], writes=[st2])
                    hh = h2[sub % 2]
                    fw.op("dve", TS(hh[:], xm[:, ti, :], st2[:, 2:3], ALU.mult), reads=[xr[ti], st2], writes=[hh])
                    def tr_part(sub=sub, hh=hh):
                        pt = pT2[sub % 2]
                        fw.group("pe", [TRN(pt[:, kc * 128:(kc + 1) * 128], hh[:, kc * 128:(kc + 1) * 128], idb[:]) for kc in range(8)],
                                 reads=[hh, idb], writes=[pt])
                        fw.op("act", ACP(hst[:, :, sub * 128:(sub + 1) * 128], pt[:].rearrange("p (a b) -> p a b", a=8)), reads=[pt], writes=[hst])
                    if tr_pending:
                        tr_pending.pop(0)()
                    tr_pending.append(tr_part)
                while tr_pending:
                    tr_pending.pop(0)()
                fw.dma("sp", DMA(h2T_scr[:, :, s * 512:(s + 1) * 512], hst[:]), "h2stl", reads=[hst], writes=[r_h2scr])
            fw.barrier()
            A.top = e1_mark
            g2t = A.t("g2t", [8], F32)
            fw.dma("sp", DMA(g2t[:], g2), "g2t", writes=[g2t])
            wupq = [A.t(f"wupq{i}", [8, 1024], BF16) for i in range(2)]
            wdnq = [A.t(f"wdnq{i}", [8, 1024], BF16) for i in range(2)]
            h2T = [A.t(f"h2T{i}", [8, 512], BF16) for i in range(2)]
            aT = [A.t(f"aT{i}", [8, 512], BF16) for i in range(2)]
            rl = [A.t(f"rl{i}", [512], F32) for i in range(2)]
            pUp = [bank(b, 512, F32, "pUp") for b in range(4)]
            pDn = [bank(b, 512, F32, "pDn") for b in (4, 5, 6, 7)]
            n = 0
            def load_quarter(fq):
                wu, wd = wupq[fq % 2], wdnq[fq % 2]
                for kc in range(8):
                    i_ = stage_ctr[0]
                    st = stage[i_ % 4]
                    key = f"wstage{i_ % 4}" if fq == 0 else f"pws{i_ % 4}"
                    stage_ctr[0] += 1
                    qn = "sp" if fq == 0 else "pool"
                    fw.dma(qn, DMA(st[:, 0:1024], wup[:, kc, fq * 1024:(fq + 1) * 1024]), key, writes=[st])
                    if fq > 0:
                        fw.op("pool", TT(wu[:, kc, :], st[:, 0:1024], bc(g2t[:, kc:kc + 1], [128, 1024]), ALU.mult), reads=[st, g2t], writes=[wu])
                    elif kc % 2:
                        fw.op("act", ACTF(wu[:, kc, :], st[:, 0:1024], AF.Copy, scale=g2t[:, kc:kc + 1]), reads=[st, g2t], writes=[wu])
                    else:
                        fw.op("dve", TS(wu[:, kc, :], st[:, 0:1024], g2t[:, kc:kc + 1], ALU.mult), reads=[st, g2t], writes=[wu])
                for c in range(4):
                    i_ = stage_ctr[0]
                    st = stage[i_ % 4]
                    key = f"wstage{i_ % 4}" if fq == 0 else f"pws{i_ % 4}"
                    stage_ctr[0] += 1
                    qn = "sp" if fq == 0 else "pool"
                    sv = st[:, 0:2048].rearrange("p (a b) -> p a b", a=2)
                    fw.dma(qn, DMA(sv, wdown[:, 8 * fq + 2 * c:8 * fq + 2 * c + 2, :]), key, writes=[st])
                    if fq > 0:
                        fw.op("pool", TC(wd[:, 2 * c:2 * c + 2, :], sv), reads=[st], writes=[wd])
                    else:
                        fw.op("act" if c % 2 else "dve", (ACP if c % 2 else TC)(wd[:, 2 * c:2 * c + 2, :], sv), reads=[st], writes=[wd])
            load_quarter(0)
            pend_down = []
            for fq in range(4):
                wu, wd = wupq[fq % 2], wdnq[fq % 2]
                for s in range(4):
                    h_t = h2T[n % 2]
                    a_t = aT[n % 2]
                    fw.dma("sp", DMA(h_t[:], h2T_scr[:, :, s * 512:(s + 1) * 512]), f"h2Tl{n % 2}", reads=[r_h2scr], writes=[h_t])
                    n += 1
                    for fcb in range(8):
                        pu = pUp[fcb % 4]
                        fw.group("pe", [MM(pu[:], wu[:, kc, fcb * 128:(fcb + 1) * 128], h_t[:, kc, :], start=(kc == 0), stop=(kc == 7)) for kc in range(8)],
                                 reads=[wu, h_t], writes=[pu])
                        r = rl[fcb % 2]
                        fw.op("act", ACTF(r[:], pu[:], AF.Relu), reads=[pu], writes=[r])
                        fw.op("dve", TT(a_t[:, fcb, :], r[:], r[:], ALU.mult), reads=[r], writes=[a_t])

                    def down(fq=fq, s=s, a_t=a_t, wd=wd):
                        for sub in range(4):
                            ti = 4 * s + sub
                            for cb in range(2):
                                pd = pDn[(2 * sub + cb) % 4]
                                fw.group("pe", [MM(pd[:], a_t[:, fc, sub * 128:(sub + 1) * 128], wd[:, fc, cb * 512:(cb + 1) * 512],
                                                   start=(fc == 0), stop=(fc == 7)) for fc in range(8)], reads=[a_t, wd], writes=[pd])
                                fw.op("dve", TT(xm[:, ti, cb * 512:(cb + 1) * 512], pd[:], xm[:, ti, cb * 512:(cb + 1) * 512], ALU.add),
                                      reads=[pd, xr[ti]], writes=[xr[ti]])
                            if fq == 3:
                                fw.dma("sp", DMA(out[ti * 128:(ti + 1) * 128, :], xm[:, ti, :]), "outst", reads=[xr[ti]], writes=[r_out])
                    if pend_down:
                        pend_down.pop(0)()
                    if s == 0 and fq < 3:
                        load_quarter(fq + 1)
                    pend_down.append(down)
            while pend_down:
                pend_down.pop(0)()
            fw.barrier()

        epsb = A.t("epsb", [1], F32)
        fw.op("pool", lambda e: e.memset(epsb[:], EPS), writes=[epsb])
        base_mark = A.top
        r_Vscr, r_Kscr, r_Qscr, r_gscr = Res("Vscr"), Res("Kscr"), Res("Qscr"), Res("gscr")
        r_yscr, r_h2scr, r_out = Res("yscr"), Res("h2scr"), Res("out")

        order = ["A", "B", "C", "D1", "D2", "all"]
        lvl = order.index(upto)
        fw.dma("sp", DMA(V_scr[128:129, :, 1170:1554], bass.AP(expand.tensor, 64, [[0, 1], [0, NT], [1, 384]])), "vpad", writes=[r_Vscr])
        proj_phase("A")
        if lvl >= 1:
            proj_phase("B")
        if debug and lvl >= 1:
            fw.dma("sp", DMA(dbg_gn, gn[:]), "dbggn", reads=[gn], writes=[r_out])
        if lvl >= 2:
            A.top = base_mark
            kcT = A.t("kcT", [3, 256], BF16)
            vc1 = A.t("vc1", [2, 3, 65], BF16)
            c_mark = A.top
            cmp_phase(kcT, vc1)
            if debug:
                fw.dma("sp", DMA(dbg_kc, kcT[:]), "dbgkc", reads=[kcT], writes=[r_out])
                fw.dma("sp", DMA(dbg_vc, vc1[:]), "dbgvc", reads=[vc1], writes=[r_out])
                fw.barrier()
        if lvl >= 3:
            A.top = c_mark
            nsa_phase(kcT, vc1)
        if lvl >= 4:
            A.top = base_mark
            dil_phase()
        if lvl >= 5:
            A.top = base_mark
            e_phase()

        fw.barrier()
        fw.replay()
    return nc


def _rope_tables(pos):
    half = 32
    inv_freq = np.power(np.float32(10000.0), -np.arange(half, dtype=np.float32) / np.float32(half)).astype(np.float32)
    ang = pos.astype(np.float32)[:, None] * inv_freq[None, :]
    return np.cos(ang).astype(np.float32), np.sin(ang).astype(np.float32)


def _tok_major(a, ntile):
    return np.ascontiguousarray(a.reshape(ntile, 128, -1).transpose(1, 0, 2))


def _pmajor(w, nchunk):
    return np.ascontiguousarray(w.reshape(nchunk, 128, -1).transpose(1, 0, 2))


def _toeplitz(L, delta, fn):
    k = np.arange(128)[:, None]
    j = np.arange(L)[None, :]
    d = delta + j - 896 - k
    return fn(d)


def host_prep(inputs):
    bf = ml_dtypes.bfloat16
    x = np.asarray(inputs["x"], np.float32)
    w_in = np.asarray(inputs["w_in"], np.float32)[0]
    sizes = (768, 192, 192, 192, 192, 192, 192, 36, 768, 768, 768, 1024, 1024)
    offs = np.concatenate([[0], np.cumsum(sizes)])
    seg = {n: w_in[:, offs[i]:offs[i + 1]] for i, n in enumerate(
        ["q_a", "k_c", "v_c", "k_s", "v_s", "k_w", "v_w", "g_nsa", "q_b", "k_b", "v_b", "g_ma", "g_mb"])}
    wkv = np.concatenate([seg["k_s"], seg["k_w"], seg["k_b"], seg["v_s"], seg["v_w"], seg["v_b"], seg["k_c"], seg["v_c"]], axis=1)
    wq = np.concatenate([seg["q_a"], seg["q_b"], seg["g_nsa"], seg["g_ma"], seg["g_mb"]], axis=1)
    g = lambda n: np.asarray(inputs[n], np.float32)[0]
    common = {
        "wkv": _pmajor(wkv, 8), "wq": _pmajor(wq, 8),
        "g1": np.ascontiguousarray(g("norm1_g").reshape(8, 128).T), "g2": np.ascontiguousarray(g("norm2_g").reshape(8, 128).T),
        "gcolK": np.concatenate([np.tile(g("k_norm_slc"), 3), np.tile(g("k_norm_win"), 3), np.tile(g("k_norm_b"), 12)])[None, :].astype(np.float32),
        "gcolQ": np.concatenate([np.tile(g("q_norm_a"), 12), np.tile(g("q_norm_b"), 12)])[None, :].astype(np.float32),
        "gcmp": g("k_norm_cmp")[None, :].astype(np.float32),
        "ident": np.eye(128, dtype=np.float32),
        "expand": (np.arange(S)[None, :] // 64 == np.arange(64)[:, None]).astype(bf),
        "woa": _pmajor(g("w_o_a"), 6), "wob": _pmajor(g("w_o_b"), 2), "wout": _pmajor(g("w_out"), 8),
        "wup": _pmajor(g("w_up"), 8), "wdown": _pmajor(g("w_down"), 32),
        "w1k": _pmajor(g("cmp_k_w1"), 16), "w2k": _pmajor(g("cmp_k_w2"), 2),
        "w1v": _pmajor(g("cmp_v_w1"), 16), "w2v": _pmajor(g("cmp_v_w2"), 2),
        "posk": np.ascontiguousarray(g("cmp_k_pos").reshape(16, 128).T), "posv": np.ascontiguousarray(g("cmp_v_pos").reshape(16, 128).T),
    }
    cos, sin = _rope_tables(np.arange(S))
    common["cs_all"] = _tok_major(cos, NT)
    common["sn_all"] = _tok_major(sin, NT)
    c_end = np.arange(256) * 16 + 31
    cc, sc = _rope_tables(c_end)
    common["cs_cmp"] = _tok_major(cc, 2)
    common["sn_cmp"] = _tok_major(sc, 2)
    cs_ = np.arange(256)[:, None] * 16
    ss_ = np.arange(64)[None, :] * 64
    ov = np.clip(np.minimum(cs_ + 32, ss_ + 64) - np.maximum(cs_, ss_), 0, None).astype(np.float32) / 32.0
    ov1 = np.concatenate([ov, np.ones((256, 1), np.float32)], axis=1)
    ov1[255] = 0.0
    common["ov1"] = _tok_major(ov1, 2).astype(bf)

    in_maps = []
    for c in range(8):
        b, half = c // 2, c % 2
        qts = QT[half]
        own_idx = np.concatenate([np.arange(q * 512, (q + 1) * 512) for q in qts])
        m = dict(common)
        m["x_all"] = np.ascontiguousarray(x[b])
        m["x_own"] = np.ascontiguousarray(x[b][own_idx])
        m["cs_own"] = _tok_major(cos[own_idx], NOWN)
        m["sn_own"] = _tok_major(sin[own_idx], NOWN)
        cm = (c_end[:, None] <= own_idx[None, :]) & (np.arange(256)[:, None] < 255)
        cm = cm.reshape(2, 128, 4, 512).transpose(1, 2, 0, 3)
        m["cmask"] = np.ascontiguousarray(cm).astype(bf)
        t = own_idx
        cur = t // 64
        blk = np.arange(64)[None, :]
        forced = (blk == 0) | (blk == cur[:, None]) | (blk == cur[:, None] - 1)
        elig = blk <= cur[:, None]
        sA = (elig & ~forced).astype(np.float32)
        sB = np.where(forced, np.float32(1e30), np.where(elig, np.float32(0.0), np.float32(-1e30))).astype(np.float32)
        m["selA"] = np.ascontiguousarray(sA.reshape(4, 4, 128, 64).transpose(2, 0, 1, 3))
        m["selB"] = np.ascontiguousarray(sB.reshape(4, 4, 128, 64).transpose(2, 0, 1, 3))
        def tabs(L, fn):
            out = np.zeros((128, 2, L), np.float32)
            for p in range(2):
                delta = 512 * (qts[p] - 2 * p)
                out[:, p, :] = _toeplitz(L, delta, fn)
            return out.astype(bf)
        m["tbc"] = tabs(L_C, lambda d: d >= 0)
        m["tbw"] = tabs(L_W, lambda d: (d >= 0) & (d < 512))
        m["tbd0"] = tabs(L_D0, lambda d: (d >= 0) & (d <= 128))
        m["tbd1"] = tabs(L_D1, lambda d: (d >= 0) & (d <= 512) & (d % 4 == 0))
        m["tbd2"] = tabs(L_D2, lambda d: (d >= 0) & (d <= 2048) & (d % 16 == 0))
        in_maps.append(m)
    return in_maps


def kernel(**inputs):
    in_maps = host_prep(inputs)
    nc = build()
    res = run_bass_kernel_spmd(nc, in_maps, core_ids=list(range(8)))
    out = np.zeros((4, S, D), np.float32)
    for c in range(8):
        b, half = c // 2, c % 2
        o = np.asarray(res.results[c]["out"], np.float32)
        for s_, q in enumerate(QT[half]):
            out[b, q * 512:(q + 1) * 512] = o[s_ * 512:(s_ + 1) * 512]
    return out
```

```python
import contextlib
import math
import numpy as np
import ml_dtypes
import concourse.bass as bass
import concourse.mybir as mybir
from concourse.bass_utils import run_bass_kernel_spmd

F32 = mybir.dt.float32
BF16 = mybir.dt.bfloat16
AF = mybir.ActivationFunctionType
ALU = mybir.AluOpType
AX = mybir.AxisListType

S = 4096
D = 1024
NT = 32
NOWN = 16
EPS = 1e-6
SCALE = 0.125
QT = ([0, 3, 4, 7], [1, 2, 5, 6])
KV_COLS = 2688
Q_COLS = 3620
NEGSEL = -2048.0
L_C, L_W, L_D0, L_D1, L_D2 = 1408, 1920, 1536, 1920, 3456
SBUF_BYTES = 206 * 1024


class Res:
    __slots__ = ("name", "lw", "rd")

    def __init__(self, name=""):
        self.name = name
        self.lw = None
        self.rd = []


class Tile:
    def __init__(self, ap, name):
        self.ap = ap
        self.res = Res(name)

    def __getitem__(self, k):
        return self.ap[k]


class FW:
    ENG = ["pe", "act", "dve", "pool", "sp"]

    def __init__(self, nc, es):
        self.nc = nc
        self.es = es
        self.q = {e: [] for e in self.ENG}
        self.cnt = {e: 0 for e in self.ENG}
        self.sem = {e: es.enter_context(nc.semaphore("s_" + e)) for e in ["pe", "act", "dve", "pool"]}
        self.waited = {e: {} for e in self.ENG}
        self.dsem = {}

    @staticmethod
    def _r(x):
        return x.res if isinstance(x, Tile) else x

    def _deps(self, reads, writes):
        deps = []
        for r in reads:
            if r.lw is not None:
                deps.append(r.lw)
        for w in writes:
            deps.extend(w.rd)
            if w.lw is not None:
                deps.append(w.lw)
        return deps

    def _emit_waits(self, eng, deps):
        need = {}
        for (k, v) in deps:
            if k == eng and eng == "pe":
                continue
            if v > need.get(k, 0):
                need[k] = v
        for k, v in need.items():
            if self.waited[eng].get(k, 0) >= v:
                continue
            self.waited[eng][k] = v
            semh = self.sem[k] if k in self.sem else self.dsem[k][0]
            self.q[eng].append(("wait", semh, v))

    def _mark(self, key, seq, reads, writes):
        for r in reads:
            r.rd.append((key, seq))
            if len(r.rd) > 48:
                mx = {}
                for (k, v) in r.rd:
                    if v > mx.get(k, 0):
                        mx[k] = v
                r.rd = list(mx.items())
        for w in writes:
            w.lw = (key, seq)
            w.rd = []

    def op(self, eng, fn, reads=(), writes=()):
        return self.group(eng, [fn], reads, writes)

    def group(self, eng, fns, reads=(), writes=()):
        reads = [self._r(x) for x in reads]
        writes = [self._r(x) for x in writes]
        self._emit_waits(eng, self._deps(reads, writes))
        self.cnt[eng] += 1
        seq = self.cnt[eng]
        for f in fns[:-1]:
            self.q[eng].append(("op", f, None))
        self.q[eng].append(("op", fns[-1], self.sem[eng]))
        self._mark(eng, seq, reads, writes)
        return seq

    def dma(self, queue, fn, key, reads=(), writes=()):
        reads = [self._r(x) for x in reads]
        writes = [self._r(x) for x in writes]
        if key not in self.dsem:
            self.dsem[key] = [self.es.enter_context(self.nc.semaphore("d_" + key)), 0]
        self._emit_waits(queue, self._deps(reads, writes))
        ent = self.dsem[key]
        ent[1] += 16
        self.q[queue].append(("dma", fn, ent[0]))
        self._mark(key, ent[1], reads, writes)

    def barrier(self):
        allv = [(e, self.cnt[e]) for e in self.sem if self.cnt[e] > 0]
        allv += [(k, v[1]) for k, v in self.dsem.items() if v[1] > 0]
        for e in self.ENG:
            self._emit_waits(e, [(k, v) for (k, v) in allv if k != e])
            if e in self.sem and e != "pe" and self.cnt[e] > 0:
                self._emit_waits(e, [(e, self.cnt[e])])

    def final_wait(self, eng, ress):
        deps = []
        for r in ress:
            r = self._r(r)
            if r.lw is not None:
                deps.append(r.lw)
        self._emit_waits(eng, deps)

    def replay(self):
        nc = self.nc
        with nc.Block() as block:
            def run(engname):
                def f(e):
                    for it in self.q[engname]:
                        if it[0] == "wait":
                            e.wait_ge(it[1], it[2])
                        elif it[0] == "op":
                            ins = it[1](e)
                            if it[2] is not None:
                                ins.then_inc(it[2], 1)
                        else:
                            it[1](e).then_inc(it[2], 16)
                return f
            block.tensor(run("pe"))
            block.scalar(run("act"))
            block.vector(run("dve"))
            block.gpsimd(run("pool"))
            block.sync(run("sp"))


class Arena:
    def __init__(self, big, nbytes):
        self.big = big
        self.cap = nbytes
        self.top = 0
        self.hi = 0

    def t(self, name, shape, dt):
        esz = 2 if dt == BF16 else 4
        n = int(np.prod(shape))
        nb = (n * esz + 31) // 32 * 32
        off = self.top
        assert off + nb <= self.cap, (name, off, nb, self.cap)
        self.top = off + nb
        self.hi = max(self.hi, self.top)
        ap = self.big[:, off // 4:(off + nb) // 4]
        if dt == BF16:
            ap = ap.bitcast(BF16)
        ap = ap[:, 0:n]
        if len(shape) == 2:
            ap = ap.rearrange("p (a b) -> p a b", a=shape[0])
        elif len(shape) == 3:
            ap = ap.rearrange("p (a b c) -> p a b c", a=shape[0], b=shape[1])
        elif len(shape) == 4:
            ap = ap.rearrange("p (a b c d) -> p a b c d", a=shape[0], b=shape[1], c=shape[2])
        return Tile(ap, name)


def bc(ap, shape):
    return ap.to_broadcast(list(shape))


def MM(out, lhsT, rhs, start=True, stop=True):
    return lambda e: e.matmul(out, lhsT=lhsT, rhs=rhs, start=start, stop=stop)


def TT(out, in0, in1, op):
    return lambda e: e.tensor_tensor(out=out, in0=in0, in1=in1, op=op)


def TS(out, in0, s1, op0, s2=None, op1=None):
    if op1 is None:
        return lambda e: e.tensor_scalar(out=out, in0=in0, scalar1=s1, scalar2=None, op0=op0)
    return lambda e: e.tensor_scalar(out=out, in0=in0, scalar1=s1, scalar2=s2, op0=op0, op1=op1)


def ACTF(out, in_, func, scale=1.0, bias=None, accum=None):
    kw = {}
    if bias is not None:
        kw["bias"] = bias
    if accum is not None:
        kw["accum_out"] = accum
    return lambda e: e.activation(out=out, in_=in_, func=func, scale=scale, **kw)


def ACP(out, in_):
    return lambda e: e.copy(out=out, in_=in_)


def TC(out, in_):
    return lambda e: e.tensor_copy(out=out, in_=in_)


def TRN(out, in_, ident):
    return lambda e: e.transpose(out=out, in_=in_, identity=ident)


def DMA(out, in_):
    return lambda e: e.dma_start(out=out, in_=in_)


def MSET(ap, v):
    return lambda e: e.memset(ap, v)


def RECIP(out, in_):
    return lambda e: e.reciprocal(out=out, in_=in_)


def build(upto="all", debug=False):
    nc = bass.Bass("TRN2", target_bir_lowering=False)
    dbg_kind = "ExternalOutput" if debug else "Internal"

    def din(name, shape, dt=F32):
        return nc.dram_tensor(name, list(shape), dt, kind="ExternalInput").ap()

    def dscr(name, shape, dt=BF16):
        return nc.dram_tensor(name, list(shape), dt, kind=dbg_kind).ap()

    x_all = din("x_all", [S, D])
    x_own = din("x_own", [2048, D])
    wkv = din("wkv", [128, 8, KV_COLS])
    wq = din("wq", [128, 8, Q_COLS])
    g1 = din("g1", [128, 8])
    g2 = din("g2", [128, 8])
    gcolK = din("gcolK", [1, 1152])
    gcolQ = din("gcolQ", [1, 1536])
    gcmp = din("gcmp", [1, 64])
    cs_all = din("cs_all", [128, NT, 32])
    sn_all = din("sn_all", [128, NT, 32])
    cs_own = din("cs_own", [128, NOWN, 32])
    sn_own = din("sn_own", [128, NOWN, 32])
    cs_cmp = din("cs_cmp", [128, 2, 32])
    sn_cmp = din("sn_cmp", [128, 2, 32])
    ident = din("ident", [128, 128])
    expand = din("expand", [64, S], BF16)
    ov1 = din("ov1", [128, 2, 65], BF16)
    cmask = din("cmask", [128, 4, 2, 512], BF16)
    selA = din("selA", [128, 4, 4, 64])
    selB = din("selB", [128, 4, 4, 64])
    tbc = din("tbc", [128, 2, L_C], BF16)
    tbw = din("tbw", [128, 2, L_W], BF16)
    tbd0 = din("tbd0", [128, 2, L_D0], BF16)
    tbd1 = din("tbd1", [128, 2, L_D1], BF16)
    tbd2 = din("tbd2", [128, 2, L_D2], BF16)
    w1k = din("w1k", [128, 16, 256])
    w2k = din("w2k", [128, 2, 64])
    posk = din("posk", [128, 16])
    w1v = din("w1v", [128, 16, 256])
    w2v = din("w2v", [128, 2, 64])
    posv = din("posv", [128, 16])
    woa = din("woa", [128, 6, 1024])
    wob = din("wob", [128, 2, 1024])
    wout = din("wout", [128, 8, 1024])
    wup = din("wup", [128, 8, 4096])
    wdown = din("wdown", [128, 32, 1024])
    out = nc.dram_tensor("out", [2048, D], F32, kind="ExternalOutput").ap()

    KsT_scr = dscr("KsT_scr", [64, 3, S])
    KwT_scr = dscr("KwT_scr", [64, 3, S])
    KbT_scr = dscr("KbT_scr", [128, 6, S])
    V_scr = dscr("V_scr", [129, NT, 1554])
    QaT_scr = dscr("QaT_scr", [64, 12, 2048])
    QbT_scr = dscr("QbT_scr", [128, 6, 2048])
    gT_scr = dscr("gT_scr", [128, 16, 2048])
    yT_scr = dscr("yT_scr", [128, 8, 2048])
    h2T_scr = dscr("h2T_scr", [128, 8, 2048])
    dbg_gn = dscr("dbg_gn", [128, NOWN, 36], F32) if debug else None
    dbg_kc = dscr("dbg_kc", [128, 3, 256], BF16) if debug else None
    dbg_vc = dscr("dbg_vc", [128, 2, 3, 65], BF16) if debug else None

    with contextlib.ExitStack() as es:
        fw = FW(nc, es)
        big = es.enter_context(nc.sbuf_tensor("big", [128, SBUF_BYTES // 4], F32))
        psum = es.enter_context(nc.psum_tensor("psum", [128, 4096], F32))
        A = Arena(big, SBUF_BYTES)

        def bank(b, n=512, dt=F32, name="ps"):
            ap = psum[:, 512 * b:512 * b + 512]
            if dt == BF16:
                ap = ap.bitcast(BF16)
            return Tile(ap[:, 0:n], f"{name}{b}")

        idf = A.t("idf", [128], F32)
        idb = A.t("idb", [128], BF16)
        gn = A.t("gn", [NOWN, 36], F32)
        fw.dma("sp", lambda e: e.dma_start(out=idf[:], in_=ident), "idf", writes=[idf])
        fw.op("dve", lambda e: e.tensor_copy(out=idb[:], in_=idf[:]), reads=[idf], writes=[idb])
        stage_off = A.top
        stage = [A.t(f"wstage{i}", [2048], F32) for i in range(4)]
        stage_ctr = [0]

        def load_weight(dst_tile, dst_ap_fn, src_ap_fn, nchunks, width, gain=None, eng_cycle=("dve", "pool"), as3=None):
            for c in range(nchunks):
                i = stage_ctr[0]
                stage_ctr[0] += 1
                st = stage[i % 4]
                assert width <= 2048
                sview = st[:, 0:width] if as3 is None else st[:, 0:width].rearrange("p (a b) -> p a b", a=as3)
                fw.dma("sp", DMA(sview, src_ap_fn(c)), f"wstage{i % 4}", writes=[st])
                eng = ("dve", "act")[i % 2]
                if gain is None:
                    fw.op(eng, (TC if eng == "dve" else ACP)(dst_ap_fn(c), sview), reads=[st], writes=[dst_tile])
                elif eng == "dve":
                    fw.op(eng, TS(dst_ap_fn(c), sview, gain[:, c:c + 1], ALU.mult), reads=[st, gain], writes=[dst_tile])
                else:
                    fw.op(eng, ACTF(dst_ap_fn(c), sview, AF.Copy, scale=gain[:, c:c + 1]), reads=[st, gain], writes=[dst_tile])

        base_mark = A.top

        def proj_phase(which):
            A.top = base_mark
            isA = which == "A"
            ntile = NT if isA else NOWN
            ncols = KV_COLS if isA else Q_COLS
            nrope = 18 if isA else 24
            xsrc = x_all if isA else x_own
            W = A.t("W", [8, ncols], BF16)
            g1t = A.t("g1t", [8], F32)
            fw.dma("sp", lambda e: e.dma_start(out=g1t[:], in_=g1), "g1t", writes=[g1t])
            gcol = A.t("gcol", [nrope, 64], F32)
            gsrc = gcolK if isA else gcolQ
            fw.dma("sp", lambda e: e.dma_start(out=gcol[:].rearrange("p a b -> p (a b)"),
                                               in_=gsrc.partition_broadcast(128).rearrange("p a b -> p (a b)")),
                   "gcol", writes=[gcol])
            cs = A.t("cs", [ntile, 32], F32)
            sn = A.t("sn", [ntile, 32], F32)
            fw.dma("sp", lambda e: e.dma_start(out=cs[:], in_=cs_all if isA else cs_own), "cs", writes=[cs])
            fw.dma("sp", lambda e: e.dma_start(out=sn[:], in_=sn_all if isA else sn_own), "sn", writes=[sn])
            wsrc = wkv if isA else wq
            csplit = 1152 if isA else 1572
            W_lo, W_hi = Res("W_lo"), Res("W_hi")

            xin = [A.t(f"xin{i}", [D], F32) for i in range(2)]
            junk = A.t("junk", [D], BF16)
            hb = [A.t(f"hb{i}", [D], BF16) for i in range(2)]
            stat = [A.t(f"stat{i}", [4], F32) for i in range(2)]
            sq = [A.t(f"sq{i}", [512], F32) for i in range(2)]
            ssh = [A.t(f"ssh{i}", [nrope], F32) for i in range(2)]
            kn = [A.t(f"kn{i}", [nrope, 64], F32) for i in range(2)]
            tr = [A.t(f"tr{i}", [4, nrope, 32], F32) for i in range(2)]
            kr = [A.t(f"kr{i}", [nrope, 64], BF16) for i in range(2)]
            pTh = bank(7, 1024, BF16, "pTh")
            pTk = [bank(5, 1024, BF16, "pTkA"), bank(6, 1024, BF16, "pTkB")]
            nmm = 5
            pmm = [bank(b, 512, F32, "pmm") for b in range(nmm)]
            mmctr = [0]
            if isA:
                hT = [A.t(f"hT{i}", [8, 128], BF16) for i in range(2)]
                KTst = [A.t(f"KTst{i}", [12, 512], BF16) for i in range(2)]
                Vst = [A.t(f"Vst{i}", [4, 1554], BF16) for i in range(2)]
                for v_ in Vst:
                    fw.op("pool", MSET(v_[:], 1.0), writes=[v_])
            else:
                hT = [A.t(f"hTs{i}", [8, 512], BF16) for i in range(2)]
                QTst = [A.t(f"QTst{i}", [12, 512], BF16) for i in range(2)]
                gst = [A.t(f"gst{i}", [512], BF16) for i in range(4)]

            def s_load(i):
                t = xin[i % 2]
                fw.dma("sp", lambda e: e.dma_start(out=t[:], in_=xsrc[i * 128:(i + 1) * 128, :]), f"xin{i % 2}", writes=[t])

            def s_norm(i):
                p = i % 2
                x, st, h = xin[p], stat[p], hb[p]
                fw.op("act", lambda e: e.activation(out=junk[:], in_=x[:], func=AF.Square, accum_out=st[:, 0:1]),
                      reads=[x], writes=[junk, st])
                fw.op("act", lambda e: e.activation(out=st[:, 1:2], in_=st[:, 0:1], func=AF.Sqrt, scale=1.0 / D, bias=epsb[:, 0:1]),
                      reads=[st, epsb], writes=[st])
                fw.op("dve", lambda e: e.reciprocal(out=st[:, 2:3], in_=st[:, 1:2]), reads=[st], writes=[st])
                fw.op("dve", lambda e: e.tensor_scalar(out=h[:], in0=x[:], scalar1=st[:, 2:3], scalar2=None, op0=ALU.mult),
                      reads=[x, st], writes=[h])

            def s_transp_h(i):
                p = i % 2
                h = hb[p]
                fw.group("pe", [lambda e, kc=kc: e.transpose(out=pTh[:, kc * 128:(kc + 1) * 128], in_=h[:, kc * 128:(kc + 1) * 128],
                                                            identity=idb[:]) for kc in range(8)],
                         reads=[h, idb], writes=[pTh])
                if isA:
                    dst = hT[p]
                    fw.op("act", lambda e: e.copy(out=dst[:].rearrange("p a b -> p (a b)"), in_=pTh[:]), reads=[pTh], writes=[dst])
                else:
                    dst = hT[(i // 4) % 2]
                    sub = i % 4
                    fw.op("act", lambda e: e.copy(out=dst[:, :, sub * 128:(sub + 1) * 128],
                                                  in_=pTh[:].rearrange("p (a b) -> p a b", a=8)), reads=[pTh], writes=[dst])

            def lhs_of(i, kc):
                if isA:
                    return hT[i % 2][:, kc, :]
                return hT[(i // 4) % 2][:, kc, (i % 4) * 128:(i % 4 + 1) * 128]

            def hT_of(i):
                return hT[i % 2] if isA else hT[(i // 4) % 2]

            def mm_block(i, c0, w):
                pm = pmm[mmctr[0] % nmm]
                mmctr[0] += 1
                fw.group("pe", [lambda e, kc=kc: e.matmul(pm[:, 0:w], lhsT=lhs_of(i, kc), rhs=W[:, kc, c0:c0 + w],
                                                          start=(kc == 0), stop=(kc == 7)) for kc in range(8)],
                         reads=[hT_of(i), W_lo if c0 + w <= csplit else W_hi], writes=[pm])
                return pm

            def rope_epilogue(i, blocks):
                p = i % 2
                k_n, s_h, t_r, k_r, s_q = kn[p], ssh[p], tr[p], kr[p], sq
                for bi, (pm, h0, nh) in enumerate(blocks):
                    sqt = s_q[bi % 2]
                    w = nh * 64
                    fw.op("act", lambda e, pm=pm, sqt=sqt, w=w: e.activation(out=sqt[:, 0:w], in_=pm[:, 0:w], func=AF.Square),
                          reads=[pm], writes=[sqt])
                    fw.op("dve", lambda e, sqt=sqt, h0=h0, nh=nh, w=w: e.tensor_reduce(
                        out=s_h[:, h0:h0 + nh], in_=sqt[:, 0:w].rearrange("p (a b) -> p a b", a=nh), axis=AX.X, op=ALU.add),
                        reads=[sqt], writes=[s_h])
                fw.op("act", lambda e: e.activation(out=s_h[:], in_=s_h[:], func=AF.Sqrt, scale=1.0 / 64, bias=epsb[:, 0:1]),
                      reads=[s_h, epsb], writes=[s_h])
                fw.op("dve", lambda e: e.reciprocal(out=s_h[:], in_=s_h[:]), reads=[s_h], writes=[s_h])
                for (pm, h0, nh) in blocks:
                    w = nh * 64
                    fw.op("dve", lambda e, pm=pm, h0=h0, nh=nh, w=w: e.tensor_tensor(
                        out=k_n[:, h0:h0 + nh, :], in0=pm[:, 0:w].rearrange("p (a b) -> p a b", a=nh),
                        in1=bc(s_h[:, h0:h0 + nh].unsqueeze(2), [128, nh, 64]), op=ALU.mult),
                        reads=[pm, s_h], writes=[k_n])
                fw.op("dve", lambda e: e.tensor_tensor(out=k_n[:], in0=k_n[:], in1=gcol[:], op=ALU.mult),
                      reads=[k_n, gcol], writes=[k_n])
                cosb = bc(cs[:, i, :].unsqueeze(1), [128, nrope, 32])
                sinb = bc(sn[:, i, :].unsqueeze(1), [128, nrope, 32])
                x1 = k_n[:, :, 0:32]
                x2 = k_n[:, :, 32:64]
                fw.op("pool", lambda e: e.tensor_tensor(out=t_r[:, 0], in0=x1, in1=cosb, op=ALU.mult), reads=[k_n, cs], writes=[t_r])
                fw.op("pool", lambda e: e.tensor_tensor(out=t_r[:, 1], in0=x2, in1=sinb, op=ALU.mult), reads=[k_n, sn], writes=[t_r])
                fw.op("pool", lambda e: e.tensor_tensor(out=t_r[:, 2], in0=x2, in1=cosb, op=ALU.mult), reads=[k_n, cs], writes=[t_r])
                fw.op("pool", lambda e: e.tensor_tensor(out=t_r[:, 3], in0=x1, in1=sinb, op=ALU.mult), reads=[k_n, sn], writes=[t_r])
                fw.op("pool", lambda e: e.tensor_tensor(out=k_r[:, :, 0:32], in0=t_r[:, 0], in1=t_r[:, 1], op=ALU.subtract),
                      reads=[t_r], writes=[k_r])
                fw.op("pool", lambda e: e.tensor_tensor(out=k_r[:, :, 32:64], in0=t_r[:, 2], in1=t_r[:, 3], op=ALU.add),
                      reads=[t_r], writes=[k_r])

            def s_mm_A(i):
                blocks = []
                for (c0, w, h0, nh) in [(0, 512, 0, 8), (512, 512, 8, 8), (1024, 128, 16, 2)]:
                    blocks.append((mm_block(i, c0, w), h0, nh))
                rope_epilogue(i, blocks)
                vs = Vst[(i // 4) % 2]
                t4 = i % 4
                pm = mm_block(i, 1152, 512)
                fw.op("act", ACP(vs[:, t4, 0:520].rearrange("p (h c) -> p h c", c=65)[:, :, 0:64], pm[:].rearrange("p (h c) -> p h c", c=64)),
                      reads=[pm], writes=[vs])
                pm = mm_block(i, 1664, 512)
                fw.op("dve", TC(vs[:, t4, 520:1040].rearrange("p (h c) -> p h c", c=65)[:, :, 0:64], pm[:].rearrange("p (h c) -> p h c", c=64)),
                      reads=[pm], writes=[vs])
                pm = mm_block(i, 2176, 512)
                fw.op("act", ACP(vs[:, t4, 1040:1170].rearrange("p (h c) -> p h c", c=65)[:, :, 0:64], pm[:, 0:128].rearrange("p (h c) -> p h c", c=64)),
                      reads=[pm], writes=[vs])
                fw.op("act", ACP(vs[:, t4, 1170:1554], pm[:, 128:512]), reads=[pm], writes=[vs])
                if i % 4 == 3:
                    tb = i // 4
                    fw.dma("sp", DMA(V_scr[0:128, 4 * tb:4 * tb + 4, :], vs[:]), f"Vst{(i // 4) % 2}", reads=[vs], writes=[r_Vscr])

            def s_transp_k_A(i):
                p = i % 2
                k_r = kr[p]
                st = KTst[(i // 4) % 2]
                sub = i % 4
                fns = []
                for j in range(6):
                    fns.append(lambda e, j=j: e.transpose(out=pTk[0][0:64, j * 128:(j + 1) * 128], in_=k_r[:, j, :], identity=idb[:]))
                for j in range(2):
                    fns.append(lambda e, j=j: e.transpose(out=pTk[0][:, (6 + j) * 128:(7 + j) * 128],
                                                          in_=k_r[:, 6 + 2 * j:8 + 2 * j, :].rearrange("p a b -> p (a b)"), identity=idb[:]))
                fw.group("pe", fns, reads=[k_r, idb], writes=[pTk[0]])
                fns = []
                for j in range(2, 6):
                    fns.append(lambda e, j=j: e.transpose(out=pTk[1][:, (j - 2) * 128:(j - 1) * 128],
                                                          in_=k_r[:, 6 + 2 * j:8 + 2 * j, :].rearrange("p a b -> p (a b)"), identity=idb[:]))
                fw.group("pe", fns, reads=[k_r, idb], writes=[pTk[1]])
                cols = slice(sub * 128, (sub + 1) * 128)
                fw.op("act", lambda e: e.copy(out=st[0:64, 0:6, cols], in_=pTk[0][0:64, 0:768].rearrange("p (a b) -> p a b", a=6)),
                      reads=[pTk[0]], writes=[st])
                fw.op("dve", lambda e: e.tensor_copy(out=st[:, 6:8, cols], in_=pTk[0][:, 768:1024].rearrange("p (a b) -> p a b", a=2)),
                      reads=[pTk[0]], writes=[st])
                fw.op("act", lambda e: e.copy(out=st[:, 8:12, cols], in_=pTk[1][:, 0:512].rearrange("p (a b) -> p a b", a=4)),
                      reads=[pTk[1]], writes=[st])
                if sub == 3:
                    t0 = (i // 4) * 512
                    key = f"KTst{(i // 4) % 2}"
                    fw.dma("sp", lambda e: e.dma_start(out=KsT_scr[:, :, t0:t0 + 512], in_=st[0:64, 0:3, :]), key, reads=[st], writes=[r_Kscr])
                    fw.dma("sp", lambda e: e.dma_start(out=KwT_scr[:, :, t0:t0 + 512], in_=st[0:64, 3:6, :]), key, reads=[st], writes=[r_Kscr])
                    fw.dma("sp", lambda e: e.dma_start(out=KbT_scr[:, :, t0:t0 + 512], in_=st[:, 6:12, :]), key, reads=[st], writes=[r_Kscr])

            def s_mm_B(i):
                blocks = []
                for j in range(3):
                    blocks.append((mm_block(i, 512 * j, 512), 8 * j, 8))
                rope_epilogue(i, blocks)
                pmg = mm_block(i, 1536, 36)
                fw.op("act", lambda e: e.activation(out=gn[:, i, :], in_=pmg[:, 0:36], func=AF.Sigmoid), reads=[pmg], writes=[gn])
                if i % 4 == 3:
                    slot = i // 4
                    hTs = hT[slot % 2]
                    for cc in range(16):
                        pm = pmm[mmctr[0] % nmm]
                        mmctr[0] += 1
                        c0 = 1572 + cc * 128
                        fw.group("pe", [lambda e, kc=kc, pm=pm, c0=c0: e.matmul(pm[:], lhsT=W[:, kc, c0:c0 + 128], rhs=hTs[:, kc, :],
                                                                              start=(kc == 0), stop=(kc == 7)) for kc in range(8)],
                                 reads=[hTs, W_hi], writes=[pm])
                        g = gst[cc % 4]
                        fw.op("act", lambda e, pm=pm, g=g: e.activation(out=g[:], in_=pm[:], func=AF.Sigmoid), reads=[pm], writes=[g])
                        fw.dma("sp", lambda e, g=g, cc=cc: e.dma_start(out=gT_scr[:, cc, slot * 512:(slot + 1) * 512], in_=g[:]),
                               f"gst{cc % 4}", reads=[g], writes=[r_gscr])

            def s_transp_q_B(i):
                p = i % 2
                k_r = kr[p]
                st = QTst[(i // 4) % 2]
                sub = i % 4
                for half in range(2):
                    fns = [lambda e, j=j, half=half: e.transpose(out=pTk[half][:, j * 128:(j + 1) * 128],
                                                                 in_=k_r[:, 12 * half + 2 * j:12 * half + 2 * j + 2, :].rearrange("p a b -> p (a b)"),
                                                                 identity=idb[:]) for j in range(6)]
                    fw.group("pe", fns, reads=[k_r, idb], writes=[pTk[half]])
                    eng = "act" if half == 0 else "dve"
                    dst = st[:, 6 * half:6 * half + 6, sub * 128:(sub + 1) * 128]
                    src = pTk[half][:, 0:768].rearrange("p (a b) -> p a b", a=6)
                    if eng == "act":
                        fw.op("act", lambda e, dst=dst, src=src: e.copy(out=dst, in_=src), reads=[pTk[half]], writes=[st])
                    else:
                        fw.op("dve", lambda e, dst=dst, src=src: e.tensor_copy(out=dst, in_=src), reads=[pTk[half]], writes=[st])
                if sub == 3:
                    t0 = (i // 4) * 512
                    key = f"QTst{(i // 4) % 2}"
                    qa_v = QaT_scr.rearrange("p (j two) t -> p j two t", two=2)
                    fw.dma("sp", lambda e: e.dma_start(out=qa_v[:, :, 0, t0:t0 + 512], in_=st[0:64, 0:6, :]), key, reads=[st], writes=[r_Qscr])
                    fw.dma("sp", lambda e: e.dma_start(out=qa_v[:, :, 1, t0:t0 + 512], in_=st[64:128, 0:6, :]), key, reads=[st], writes=[r_Qscr])
                    fw.dma("sp", lambda e: e.dma_start(out=QbT_scr[:, :, t0:t0 + 512], in_=st[:, 6:12, :]), key, reads=[st], writes=[r_Qscr])

            s_mm = s_mm_A if isA else s_mm_B
            s_tk = s_transp_k_A if isA else s_transp_q_B
            s_load(0)
            s_load(1)
            load_weight(W_lo, lambda c: W[:, c, 0:csplit], lambda c: wsrc[:, c, 0:csplit], 8, csplit, gain=g1t)
            s_norm(0)
            s_load(2)
            s_transp_h(0)
            s_norm(1)
            load_weight(W_hi, lambda c: W[:, c, csplit:ncols], lambda c: wsrc[:, c, csplit:ncols], 8, ncols - csplit, gain=g1t)
            for i in range(ntile):
                if i + 1 < ntile:
                    s_transp_h(i + 1)
                if i + 2 < ntile:
                    s_norm(i + 2)
                if i + 3 < ntile:
                    s_load(i + 3)
                s_mm(i)
                if i >= 1:
                    s_tk(i - 1)
            s_tk(ntile - 1)
            fw.barrier()

        def cmp_phase(kcT, vc1):
            C2 = 2.0 * math.sqrt(2.0 / math.pi)
            w1b = A.t("w1b", [16, 256], BF16)
            w2b = A.t("w2b", [2, 64], BF16)
            posb = A.t("posb", [16], BF16)
            w2f = A.t("w2f", [2, 64], F32)
            posf = A.t("posf", [16], F32)
            gc = A.t("gc", [64], F32)
            csc = A.t("csc", [2, 32], F32)
            snc = A.t("snc", [2, 32], F32)
            fw.dma("sp", DMA(gc[:], gcmp.partition_broadcast(128).rearrange("p a b -> p (a b)")), "gc", writes=[gc])
            fw.dma("sp", DMA(csc[:], cs_cmp), "csc", writes=[csc])
            fw.dma("sp", DMA(snc[:], sn_cmp), "snc", writes=[snc])
            kc2 = A.t("kc2", [NT, 3, 2, 64], BF16)
            kT2 = A.t("kT2", [3, S], BF16)
            kcv = A.t("kcv", [NT, 384], BF16)
            kcv_sh = A.t("kcv_sh", [NT, 384], BF16)
            fw.dma("sp", DMA(kcv[:], V_scr[0:128, :, 1170:1554]), "kcv", reads=[r_Vscr], writes=[kcv])
            fw.dma("sp", DMA(kcv_sh[:, :, :], V_scr[1:129, :, 1170:1554]), "kcvsh", reads=[r_Vscr], writes=[kcv_sh])
            fw.dma("sp", DMA(kcv_sh[127:128, 0:NT - 1, :], V_scr[0:1, 1:NT, 1170:1554]), "kcvsh", reads=[r_Vscr], writes=[kcv_sh])
            fw.dma("sp", DMA(kcv_sh[127:128, NT - 1, :], expand[0:1, 64:448]), "kcvsh", writes=[kcv_sh])
            biasT = A.t("biasT", [2], F32)
            xs = A.t("xs", [256], F32)
            x2 = A.t("x2", [256], F32)
            uu = A.t("uu", [256], F32)
            sg = A.t("sg", [256], F32)
            gTs = [A.t(f"gTc{i}", [2, 256], BF16) for i in range(2)]
            kcrs = [A.t(f"kcr{i}", [2, 128], BF16) for i in range(2)]
            kcn = A.t("kcn", [2, 64], F32)
            ktr = A.t("ktr", [4, 2, 32], F32)
            sqc = A.t("sqc", [2, 64], F32)
            stc = A.t("stc", [4], F32)
            for t_ in gTs + kcrs:
                fw.op("pool", MSET(t_[:], 0.0), writes=[t_])
            fw.op("pool", MSET(vc1[:], 1.0), writes=[vc1])
            pT = [bank(0, 1024, BF16, "cpT"), bank(1, 1024, BF16, "cpT")]
            pH = [bank(2, 512, F32, "cpH"), bank(3, 512, F32, "cpH")]
            pB = bank(4, 512, F32, "cpB")
            pK = bank(5, 512, F32, "cpK")
            pKT = bank(6, 1024, BF16, "cpKT")
            for kv in range(2):
                colbase = 1152 + 192 * kv
                w1src, w2src, possrc = (w1k, w2k, posk) if kv == 0 else (w1v, w2v, posv)
                load_weight(w1b, lambda c: w1b[:, 8 * c:8 * c + 8, :], lambda c, w1src=w1src: w1src[:, 8 * c:8 * c + 8, :], 2, 2048, as3=8)
                fw.dma("sp", DMA(w2f[:], w2src), "w2f", writes=[w2f])
                fw.dma("sp", DMA(posf[:], possrc), "posf", writes=[posf])
                fw.op("dve", TC(w2b[:], w2f[:]), reads=[w2f], writes=[w2b])
                fw.op("dve", TC(posb[:], posf[:]), reads=[posf], writes=[posb])
                fw.op("dve", TC(kc2[:, :, :, 0, :], kcv[:, :, 192 * kv:192 * kv + 192].rearrange("p t (g d) -> p t g d", g=3)), reads=[kcv], writes=[kc2])
                fw.op("act", ACP(kc2[:, :, :, 1, :], kcv_sh[:, :, 192 * kv:192 * kv + 192].rearrange("p t (g d) -> p t g d", g=3)), reads=[kcv_sh], writes=[kc2])
                n = 0
                for g in range(3):
                    for tb in range(4):
                        p = pT[n % 2]
                        n += 1
                        fw.group("pe", [TRN(p[:, j * 128:(j + 1) * 128], kc2[:, tb * 8 + j, g, :, :].rearrange("p a b -> p (a b)"), idb[:])
                                        for j in range(8)], reads=[kc2, idb], writes=[p])
                        if n % 2 == 0:
                            fw.op("act", ACP(kT2[:, g, tb * 1024:(tb + 1) * 1024], p[:]), reads=[p], writes=[kT2])
                        else:
                            fw.op("dve", TC(kT2[:, g, tb * 1024:(tb + 1) * 1024], p[:]), reads=[p], writes=[kT2])
                for hc in range(2):
                    fw.group("pe", [MM(pB[:, hc:hc + 1], w1b[:, j, hc * 128:(hc + 1) * 128], posb[:, j:j + 1], start=(j == 0), stop=(j == 15))
                                    for j in range(16)], reads=[w1b, posb], writes=[pB])
                fw.op("dve", TC(biasT[:], pB[:, 0:2]), reads=[pB], writes=[biasT])
                def hidden(g):
                    gTg = gTs[g % 2]
                    for hc in range(2):
                        ph = pH[hc]
                        fw.group("pe", [MM(ph[:, 0:255], w1b[:, j, hc * 128:(hc + 1) * 128], kT2[:, g, 2 * j:2 * j + 16 * 254 + 1:16],
                                           start=(j == 0), stop=(j == 15)) for j in range(16)], reads=[w1b, kT2], writes=[ph])
                        fw.op("dve", TS(xs[:, 0:255], ph[:, 0:255], biasT[:, hc:hc + 1], ALU.add), reads=[ph, biasT], writes=[xs])
                        fw.op("pool", TT(x2[:, 0:255], xs[:, 0:255], xs[:, 0:255], ALU.mult), reads=[xs], writes=[x2])
                        fw.op("dve", TS(x2[:, 0:255], x2[:, 0:255], 0.044715, ALU.mult, 1.0, ALU.add), reads=[x2], writes=[x2])
                        fw.op("pool", TT(uu[:, 0:255], x2[:, 0:255], xs[:, 0:255], ALU.mult), reads=[x2, xs], writes=[uu])
                        fw.op("act", ACTF(sg[:, 0:255], uu[:, 0:255], AF.Sigmoid, scale=C2), reads=[uu], writes=[sg])
                        fw.op("dve", TT(gTg[:, hc, 0:255], xs[:, 0:255], sg[:, 0:255], ALU.mult), reads=[xs, sg], writes=[gTg])

                def second(g, kv=kv):
                    gTg = gTs[g % 2]
                    kcr_g = kcrs[g % 2]
                    for ct in range(2):
                        fw.group("pe", [MM(pK[:, ct * 64:(ct + 1) * 64], gTg[:, hc, ct * 128:(ct + 1) * 128], w2b[:, hc, :],
                                           start=(hc == 0), stop=(hc == 1)) for hc in range(2)], reads=[gTg, w2b], writes=[pK])
                    if kv == 1:
                        fw.op("act", ACP(vc1[:, :, g, 0:64], pK[:, 0:128].rearrange("p (a b) -> p a b", a=2)), reads=[pK], writes=[vc1])
                        return None
                    pk3 = pK[:, 0:128].rearrange("p (a b) -> p a b", a=2)
                    fw.op("act", ACTF(sqc[:], pk3, AF.Square), reads=[pK], writes=[sqc])
                    fw.op("dve", lambda e: e.tensor_reduce(out=stc[:, 0:2], in_=sqc[:], axis=AX.X, op=ALU.add), reads=[sqc], writes=[stc])
                    fw.op("act", ACTF(stc[:, 0:2], stc[:, 0:2], AF.Sqrt, scale=1.0 / 64, bias=epsb[:, 0:1]), reads=[stc, epsb], writes=[stc])
                    fw.op("dve", RECIP(stc[:, 0:2], stc[:, 0:2]), reads=[stc], writes=[stc])
                    fw.op("dve", TT(kcn[:], pk3, bc(stc[:, 0:2].unsqueeze(2), [128, 2, 64]), ALU.mult), reads=[pK, stc], writes=[kcn])
                    fw.op("pool", TT(kcn[:], kcn[:], bc(gc[:].unsqueeze(1), [128, 2, 64]), ALU.mult), reads=[kcn, gc], writes=[kcn])
                    x1, xx2 = kcn[:, :, 0:32], kcn[:, :, 32:64]
                    fw.op("pool", TT(ktr[:, 0], x1, csc[:], ALU.mult), reads=[kcn, csc], writes=[ktr])
                    fw.op("pool", TT(ktr[:, 1], xx2, snc[:], ALU.mult), reads=[kcn, snc], writes=[ktr])
                    fw.op("pool", TT(ktr[:, 2], xx2, csc[:], ALU.mult), reads=[kcn, csc], writes=[ktr])
                    fw.op("pool", TT(ktr[:, 3], x1, snc[:], ALU.mult), reads=[kcn, snc], writes=[ktr])
                    fw.op("pool", TT(kcr_g[:, :, 0:32], ktr[:, 0], ktr[:, 1], ALU.subtract), reads=[ktr], writes=[kcr_g])
                    fw.op("pool", TT(kcr_g[:, :, 32:64], ktr[:, 2], ktr[:, 3], ALU.add), reads=[ktr], writes=[kcr_g])

                    def trp(g=g, kcr_g=kcr_g):
                        fw.group("pe", [TRN(pKT[:, ct * 128:(ct + 1) * 128], kcr_g[:, ct, :], idb[:]) for ct in range(2)],
                                 reads=[kcr_g, idb], writes=[pKT])
                        fw.op("act", ACP(kcT[:, g, :], pKT[:, 0:256]), reads=[pKT], writes=[kcT])
                    return trp

                trq = []
                for g in range(3):
                    hidden(g)
                    if trq:
                        trq.pop(0)()
                    if g >= 1:
                        t_ = second(g - 1)
                        if t_ is not None:
                            trq.append(t_)
                t_ = second(2)
                if t_ is not None:
                    trq.append(t_)
                while trq:
                    trq.pop(0)()
            fw.barrier()

        def share(tile, ap, name):
            t = Tile(ap, name)
            t.res = tile.res
            return t

        def attn_units(units, q_of, k_of, v_of, mask_of, pO_of, first_of, last_of, pS, Pt, ctr, kqv, hooks=None):
            LAG = 4
            pend = []
            Kres, Qres, Vres = kqv
            hooks = hooks or {}

            def emit_pv(u, pt):
                po = pO_of(u)
                fw.group("pe", [MM(po[0:65, :], v_of(u), pt[:], start=first_of(u), stop=last_of(u))],
                         reads=[pt, Vres(u) if callable(Vres) else Vres], writes=[po])

            for ui, u in enumerate(units):
                if ui in hooks:
                    hooks[ui]()
                ps = pS[ctr[0] % len(pS)]
                pt = Pt[ctr[0] % len(Pt)]
                ctr[0] += 1
                fw.group("pe", [MM(ps[:], k_of(u), q_of(u))], reads=[Kres(u) if callable(Kres) else Kres, Qres], writes=[ps])
                fw.op("act", ACTF(pt[:], ps[:], AF.Exp, scale=SCALE), reads=[ps], writes=[pt])
                m = mask_of(u)
                if m is not None:
                    fw.op("dve", TT(pt[:], pt[:], m[0], ALU.mult), reads=[pt, m[1]], writes=[pt])
                pend.append((u, pt))
                if len(pend) > LAG:
                    emit_pv(*pend.pop(0))
            while pend:
                emit_pv(*pend.pop(0))

        def finalize(pO_list, heads_cols, coef_fn, yacc, first_branch, OTsb, pTf, s, rd4, ytmp, defer=False):
            for i, po in enumerate(pO_list):
                ot = OTsb[i % len(OTsb)]
                fw.op("dve", TC(ot[0:65, :], po[0:65, :]), reads=[po], writes=[ot])
            parts = []
            for i, po in enumerate(pO_list):
                parts.append(lambda i=i: fin_head(i, pO_list, heads_cols, coef_fn, yacc, first_branch, OTsb, pTf, rd4, ytmp))
            if defer:
                return parts
            for p in parts:
                p()
            return []

        def fin_head(i, pO_list, heads_cols, coef_fn, yacc, first_branch, OTsb, pTf, rd4, ytmp):
            if True:
                ot = OTsb[i % len(OTsb)]
                fw.group("pe", [TRN(pTf[:, sub * 65:(sub + 1) * 65], ot[0:65, sub * 128:(sub + 1) * 128], idf[0:65, 0:65]) for sub in range(4)],
                         reads=[ot, idf], writes=[pTf])
                p3 = pTf[:, 0:260].rearrange("p (a b) -> p a b", a=4)
                fw.op("dve", TS(rd4[:], p3[:, :, 64], 1e-30, ALU.max), reads=[pTf], writes=[rd4])
                fw.op("dve", RECIP(rd4[:], rd4[:]), reads=[rd4], writes=[rd4])
                gate = coef_fn(i)
                if gate is not None:
                    fw.op("dve", TT(rd4[:], rd4[:], gate, ALU.mult), reads=[rd4, gn], writes=[rd4])
                hc = heads_cols[i]
                cb = bc(rd4[:].unsqueeze(2), [128, 4, 64])
                if first_branch:
                    fw.op("dve", TT(yacc[:, :, hc, :], p3[:, :, 0:64], cb, ALU.mult), reads=[pTf, rd4], writes=[yacc])
                else:
                    fw.op("dve", TT(ytmp[:], p3[:, :, 0:64], cb, ALU.mult), reads=[pTf, rd4], writes=[ytmp])
                    fw.op("pool", TT(yacc[:, :, hc, :], yacc[:, :, hc, :], ytmp[:], ALU.add), reads=[ytmp, yacc], writes=[yacc])

        def y_to_scratch(yacc, nchunk, chunk0, s, ybf, yst, pY):
            fw.op("dve", TC(ybf[:], yacc[:].rearrange("p a h d -> p a (h d)")), reads=[yacc], writes=[ybf])
            for c in range(nchunk):
                pb = pY[c // 2]
                fw.group("pe", [TRN(pb[:, (c % 2) * 512 + sub * 128:(c % 2) * 512 + (sub + 1) * 128], ybf[:, sub, c * 128:(c + 1) * 128], idb[:])
                                for sub in range(4)], reads=[ybf, idb], writes=[pb])
            for c2 in range((nchunk + 1) // 2):
                n2 = min(2, nchunk - 2 * c2)
                src = pY[c2][:, 0:n2 * 512].rearrange("p (a b) -> p a b", a=n2)
                if c2 % 2 == 0:
                    fw.op("act", ACP(yst[:, 2 * c2:2 * c2 + n2, :], src), reads=[pY[c2]], writes=[yst])
                else:
                    fw.op("dve", TC(yst[:, 2 * c2:2 * c2 + n2, :], src), reads=[pY[c2]], writes=[yst])
            fw.dma("sp", DMA(yT_scr[:, chunk0:chunk0 + nchunk, s * 512:(s + 1) * 512], yst[:, 0:nchunk, :]), "yst", reads=[yst], writes=[r_yscr])

        def nsa_phase(kcT, vc1):
            KsT = A.t("KsT", [3, S], BF16)
            KwT = A.t("KwT", [3, S], BF16)
            Vs1 = A.t("Vs1", [NT, 3, 65], BF16)
            Vw1 = A.t("Vw1", [NT, 3, 65], BF16)
            tbc_t = A.t("tbc_t", [2, L_C], BF16)
            tbw_t = A.t("tbw_t", [2, L_W], BF16)
            selA_t = A.t("selA_t", [4, 4, 64], F32)
            selB_t = A.t("selB_t", [4, 4, 64], F32)
            ov1b = A.t("ov1b", [2, 65], BF16)
            Qa_bufs = [A.t("Qa", [12, 512], BF16),
                       Tile(big[:, stage_off // 4:stage_off // 4 + 3072].bitcast(BF16).rearrange("p (a b) -> p a b", a=12), "Qa2")]
            cm = A.t("cm", [2, 512], BF16)
            E8 = [A.t(f"E8_{i}", [512], BF16) for i in range(8)]
            cctr = [0]
            Pt = [A.t(f"Pt{i}", [512], BF16) for i in range(8)]
            OTsb = [A.t(f"OTsb{i}", [512], F32) for i in range(4)]
            yacc = A.t("yacc", [4, 12, 64], F32)
            ybf = A.t("ybf", [4, 768], BF16)
            yst = A.t("yst", [6, 512], BF16)
            imp = A.t("imp", [4, 64], F32)
            itmp = A.t("itmp", [4, 64], F32)
            score = A.t("score", [4, 64], F32)
            sc2 = A.t("sc2", [4, 64], F32)
            m8 = A.t("m8", [2, 8], F32)
            negsel_g = [A.t(f"negsel{g}", [4, 128], BF16) for g in range(3)]
            rd4 = A.t("rd4", [4], F32)
            ytmp = A.t("ytmp", [4, 64], F32)
            fw.op("pool", MSET(KwT[64:128, :, :], 0.0), writes=[KwT])
            Qg2 = [[Res(f"Qg{b}_{g}") for g in range(3)] for b in range(2)]
            for b_ in range(2):
                fw.op("pool", MSET(Qa_bufs[b_][64:128, :, :], 0.0), writes=Qg2[b_])

            def load_qa(s_):
                fw.dma("sp", DMA(Qa_bufs[s_ % 2][0:64, :, :], QaT_scr[:, :, s_ * 512:(s_ + 1) * 512]), f"Qal{s_ % 2}", reads=[r_Qscr], writes=Qg2[s_ % 2])
            for ng in negsel_g:
                fw.op("pool", MSET(ng[:], 0.0), writes=[ng])
            fw.dma("sp", DMA(tbc_t[:], tbc), "tbc", writes=[tbc_t])
            fw.dma("sp", DMA(tbw_t[:], tbw), "tbw", writes=[tbw_t])
            fw.dma("sp", DMA(selA_t[:], selA), "selA", writes=[selA_t])
            fw.dma("sp", DMA(selB_t[:], selB), "selB", writes=[selB_t])
            fw.dma("sp", DMA(ov1b[:], ov1), "ov1b", writes=[ov1b])
            pO = [bank(b, 512, F32, "pO") for b in range(4)]
            pS = [bank(b, 512, F32, "pS") for b in (4, 5, 6)]
            b7 = bank(7, 512, F32, "b7")
            b7b = share(b7, psum[:, 512 * 7:512 * 8].bitcast(BF16), "b7b")
            pY = [share(pS[i], psum[:, 512 * (4 + i):512 * (5 + i)].bitcast(BF16), "pY") for i in range(3)]
            ctr = [0]

            for s in range(4):
                par = s % 2
                Qa, Qg = Qa_bufs[s % 2], Qg2[s % 2]
                if s == 0:
                    load_qa(0)
                fw.dma("sp", DMA(cm[:], cmask[:, s]), "cml", writes=[cm])
                if s == 0:
                    for g in range(3):
                        fw.dma("sp", DMA(KsT[0:64, g, :], KsT_scr[:, g, :]), "KsTl", reads=[r_Kscr], writes=[Res()])
                        fw.dma("sp", DMA(KsT[64:128, g, :], expand), "KsTl", writes=[KsT] if g == 2 else [Res()])
                        fw.dma("sp", DMA(KwT[0:64, g, :], KwT_scr[:, g, :]), "KwTl", reads=[r_Kscr], writes=[KwT] if g == 2 else [Res()])
                    for hq in range(2):
                        fw.dma("sp", DMA(Vs1[:, 16 * hq:16 * hq + 16].rearrange("p t g c -> p t (g c)"), V_scr[0:128, 16 * hq:16 * hq + 16, 0:195]), "Vs1l",
                               reads=[r_Vscr], writes=[Vs1] if hq == 1 else [Res()])
                        fw.dma("sp", DMA(Vw1[:, 16 * hq:16 * hq + 16].rearrange("p t g c -> p t (g c)"), V_scr[0:128, 16 * hq:16 * hq + 16, 195:390]), "Vw1l",
                               reads=[r_Vscr], writes=[Vw1] if hq == 1 else [Res()])
                if s + 1 < 4:
                    load_qa(s + 1)
                def select(g):
                    negsel = negsel_g[g]
                    fw.op("dve", TT(score[:], imp[:], selA_t[:, s], ALU.mult), reads=[imp, selA_t], writes=[score])
                    fw.op("dve", TT(score[:], score[:], selB_t[:, s], ALU.add), reads=[score, selB_t], writes=[score])
                    for sub in range(4):
                        fw.op("dve", lambda e, sub=sub: e.max(out=m8[:, 0, :], in_=score[:, sub, :]), reads=[score], writes=[m8])
                        fw.op("dve", lambda e, sub=sub: e.match_replace(out=sc2[:, sub, :], in_to_replace=m8[:, 0, :], in_values=score[:, sub, :],
                                                                        imm_value=-3.0e38), reads=[score, m8], writes=[sc2])
                        fw.op("dve", lambda e, sub=sub: e.max(out=m8[:, 1, :], in_=sc2[:, sub, :]), reads=[sc2], writes=[m8])
                        fw.op("dve", TS(negsel[:, sub, 64:128], score[:, sub, :], m8[:, 1, 7:8], ALU.is_lt, NEGSEL, ALU.mult),
                              reads=[score, m8], writes=[negsel])
                    def tail(g=g):
                        fw.group("pe", [TRN(b7b[:, sub * 128:(sub + 1) * 128], negsel[:, sub, :], idb[:]) for sub in range(4)],
                                 reads=[negsel, idb], writes=[b7])
                        fw.op("act", ACP(Qa[64:128, 4 * g:4 * g + 4, :], bc(b7b[64:128, 0:512].unsqueeze(1), [64, 4, 512])), reads=[b7], writes=[Qg[g]])
                    return tail

                fin_q = []
                impb = [b7, pS[2]]
                sbk = [pS[0], pS[1]]
                for g in range(3):
                    pend = []

                    def emit_back(hh, e0, e1, g=g):
                        ets = (e0, e1)
                        fw.group("pe", [MM(pO[hh][0:65, :], vc1[:, ct, g, :], ets[ct][:], start=(ct == 0), stop=(ct == 1)) for ct in range(2)],
                                 reads=[e0, e1, vc1], writes=[pO[hh]])
                        ib = impb[hh % 2]
                        fw.group("pe", [MM(ib[:, sub * 65:(sub + 1) * 65], ets[ct][:, sub * 128:(sub + 1) * 128], ov1b[:, ct, :],
                                           start=(ct == 0), stop=(ct == 1)) for sub in range(4) for ct in range(2)],
                                 reads=[e0, e1, ov1b], writes=[ib])
                        p3 = ib[:, 0:260].rearrange("p (a b) -> p a b", a=4)
                        fw.op("dve", TS(rd4[:], p3[:, :, 64], 1e-30, ALU.max), reads=[ib], writes=[rd4])
                        fw.op("dve", RECIP(rd4[:], rd4[:]), reads=[rd4], writes=[rd4])
                        cb = bc(rd4[:].unsqueeze(2), [128, 4, 64])
                        if hh == 0:
                            fw.op("dve", TT(imp[:], p3[:, :, 0:64], cb, ALU.mult), reads=[ib, rd4], writes=[imp])
                        else:
                            fw.op("dve", TT(itmp[:], p3[:, :, 0:64], cb, ALU.mult), reads=[ib, rd4], writes=[itmp])
                            fw.op("pool", TT(imp[:], imp[:], itmp[:], ALU.add), reads=[itmp, imp], writes=[imp])
                        for _ in range(2):
                            if fin_q:
                                fin_q.pop(0)()

                    for hh in range(4):
                        ets = []
                        for ct in range(2):
                            ps = sbk[cctr[0] % 2]
                            et = E8[cctr[0] % 8]
                            cctr[0] += 1
                            fw.group("pe", [MM(ps[:], kcT[:, g, ct * 128:(ct + 1) * 128], Qa[:, 4 * g + hh, :])], reads=[kcT, Qg[g]], writes=[ps])
                            fw.op("act", ACTF(et[:], ps[:], AF.Exp, scale=SCALE), reads=[ps], writes=[et])
                            fw.op("dve", TT(et[:], et[:], cm[:, ct, :], ALU.mult), reads=[et, cm], writes=[et])
                            ets.append(et)
                        pend.append((hh, ets[0], ets[1]))
                        if len(pend) > 2:
                            emit_back(*pend.pop(0))
                    while pend:
                        emit_back(*pend.pop(0))
                    fin_q.extend(finalize([pO[hh] for hh in range(4)], [4 * g + hh for hh in range(4)],
                                          lambda i, g=g, s=s: gn[:, 4 * s:4 * s + 4, 3 * (4 * g + i) + 0], yacc, True, OTsb, b7, s, rd4, ytmp,
                                          defer=True))
                    fin_q.append(select(g))
                pending = fin_q
                for br in (1, 2):
                    for g in range(3):
                        if br == 1:
                            KT, V1, tab, kts = KsT, Vs1, tbc_t, list(range(0, 8 * s + 8))
                            need_mask = lambda kt, s=s: kt >= 8 * s
                        else:
                            KT, V1, tab, kts = KwT, Vw1, tbw_t, list(range(max(0, 8 * s - 4), 8 * s + 8))
                            need_mask = lambda kt: True
                        units = [(kt, hh) for kt in kts for hh in range(4)]

                        def mask_of(u, tab=tab, need_mask=need_mask, s=s, par=par):
                            kt = u[0]
                            if not need_mask(kt):
                                return None
                            off = 1024 * s - 128 * kt + 896
                            return (tab[:, par, off:off + 512], tab)
                        attn_units(units,
                                   q_of=lambda u, g=g: Qa[:, 4 * g + u[1], :],
                                   k_of=lambda u, KT=KT, g=g: KT[:, g, u[0] * 128:(u[0] + 1) * 128],
                                   v_of=lambda u, V1=V1, g=g: V1[:, u[0], g, :],
                                   mask_of=mask_of,
                                   pO_of=lambda u: pO[u[1]],
                                   first_of=lambda u, kts=kts: u[0] == kts[0],
                                   last_of=lambda u, kts=kts: u[0] == kts[-1],
                                   pS=pS, Pt=Pt, ctr=ctr, kqv=(KT.res, Qg[g], V1.res),
                                   hooks={6 + 4 * k: p for k, p in enumerate(pending)})
                        pending = finalize([pO[hh] for hh in range(4)], [4 * g + hh for hh in range(4)],
                                           lambda i, g=g, s=s, br=br: gn[:, 4 * s:4 * s + 4, 3 * (4 * g + i) + br], yacc, False, OTsb, b7, s, rd4, ytmp,
                                           defer=True)
                for p in pending:
                    p()
                y_to_scratch(yacc, 6, 0, s, ybf, yst, pY)
            fw.barrier()

        def dil_phase():
            KbT = A.t("KbT", [6, S], BF16)
            Vb1 = A.t("Vb1", [NT, 12, 65], BF16)
            tabs = [A.t("tbd0_t", [2, L_D0], BF16), A.t("tbd1_t", [2, L_D1], BF16), A.t("tbd2_t", [2, L_D2], BF16)]
            Qb2 = [A.t(f"Qb{i}", [12, 512], BF16) for i in range(2)]
            Pt = [A.t(f"Pt{i}", [512], BF16) for i in range(8)]
            OTsb = [A.t(f"OTsb{i}", [512], F32) for i in range(2)]
            yacc = A.t("yaccb", [4, 4, 64], F32)
            ybf = A.t("ybfb", [4, 256], BF16)
            yst = A.t("ystb", [2, 512], BF16)
            rd4 = A.t("rd4", [4], F32)
            ytmp = A.t("ytmp", [4, 64], F32)
            Kc = [Res(f"KbTc{c}") for c in range(4)]
            Vc = [Res(f"Vb1c{c}") for c in range(4)]
            def load_chunk(c):
                fw.dma("sp", DMA(KbT[:, :, 1024 * c:1024 * c + 1024], KbT_scr[:, :, 1024 * c:1024 * c + 1024]), f"KbTl{c}", reads=[r_Kscr], writes=[Kc[c]])
                fw.dma("sp", DMA(Vb1[:, 8 * c:8 * c + 8, :, :].rearrange("p t h c -> p t (h c)"), V_scr[0:128, 8 * c:8 * c + 8, 390:1170]), f"Vb1l{c}",
                       reads=[r_Vscr], writes=[Vc[c]])
            for q_ in Qb2:
                fw.op("pool", MSET(q_[:], 0.0), writes=[q_])
            for g, (src, t) in enumerate(zip((tbd0, tbd1, tbd2), tabs)):
                fw.dma("sp", DMA(t[:], src), f"tbd{g}", writes=[t])
            load_chunk(0)
            pO = [bank(b, 512, F32, "pO") for b in range(2)]
            pS = [bank(b, 512, F32, "pS") for b in (4, 5, 6)]
            b7 = bank(7, 512, F32, "b7")
            pY = [share(pS[i], psum[:, 512 * (4 + i):512 * (5 + i)].bitcast(BF16), "pY") for i in range(3)]
            ctr = [0]
            def load_qb(s):
                Qb = Qb2[s % 2]
                Qv = Qb[:].rearrange("p (j two) t -> p j two t", two=2)
                fw.dma("sp", DMA(Qv[0:64, :, 0, :], QbT_scr[0:64, :, s * 512:(s + 1) * 512]), f"Qbl{s % 2}", reads=[r_Qscr], writes=[Qb])
                fw.dma("sp", DMA(Qv[64:128, :, 1, :], QbT_scr[64:128, :, s * 512:(s + 1) * 512]), f"Qbl{s % 2}", reads=[r_Qscr], writes=[Qb])
            load_qb(0)
            for s in range(4):
                par = s % 2
                Qb = Qb2[s % 2]
                if s + 1 < 4:
                    load_qb(s + 1)
                if s == 0:
                    for c in range(1, 4):
                        load_chunk(c)
                pending = []
                for hp in range(2):
                    units = []
                    for g, back in enumerate((1, 4, 16)):
                        for kt in range(max(0, 8 * s - back), 8 * s + 8):
                            for e_ in range(2):
                                units.append((g, kt, e_))
                    first, last = units[0], units[-1]

                    def mask_of(u, s=s, par=par):
                        off = 1024 * s - 128 * u[1] + 896
                        t = tabs[u[0]]
                        return (t[:, par, off:off + 512], t)
                    attn_units(units,
                               q_of=lambda u, hp=hp: Qb[:, 4 * u[0] + 2 * hp + u[2], :],
                               k_of=lambda u, hp=hp: KbT[:, 2 * u[0] + hp, u[1] * 128:(u[1] + 1) * 128],
                               v_of=lambda u, hp=hp: Vb1[:, u[1], 4 * u[0] + 2 * hp + u[2], :],
                               mask_of=mask_of,
                               pO_of=lambda u: pO[u[2]],
                               first_of=lambda u, first=first: (u[0], u[1]) == (first[0], first[1]),
                               last_of=lambda u, last=last: (u[0], u[1]) == (last[0], last[1]),
                               pS=pS, Pt=Pt, ctr=ctr, kqv=(lambda u: Kc[u[1] // 8], Qb.res, lambda u: Vc[u[1] // 8]),
                               hooks={6 + 4 * k: p for k, p in enumerate(pending)})
                    pending = finalize([pO[0], pO[1]], [2 * hp, 2 * hp + 1], lambda i: None, yacc, True, OTsb, b7, s, rd4, ytmp, defer=True)
                for p in pending:
                    p()
                y_to_scratch(yacc, 2, 6, s, ybf, yst, pY)
            fw.barrier()

        def e_phase():
            xm = A.t("xm", [NOWN, D], F32)
            xr = [Res(f"xm{i}") for i in range(NOWN)]
            for s4 in range(4):
                fw.dma("pool", DMA(xm[:, 4 * s4:4 * s4 + 4, :], x_own[512 * s4:512 * s4 + 512, :].rearrange("(t p) c -> p t c", p=128)), f"xml{s4}",
                       writes=xr[4 * s4:4 * s4 + 4])
            e1_mark = A.top
            woa_b = A.t("woa_b", [6, 1024], BF16)
            wob_b = A.t("wob_b", [2, 1024], BF16)
            wout_b = A.t("wout_b", [8, 1024], BF16)
            load_weight(woa_b, lambda c: woa_b[:, 2 * c:2 * c + 2, :], lambda c: woa[:, 2 * c:2 * c + 2, :], 3, 2048, as3=2)
            load_weight(wob_b, lambda c: wob_b[:, 0:2, :], lambda c: wob[:, 0:2, :], 1, 2048, as3=2)
            load_weight(wout_b, lambda c: wout_b[:, 2 * c:2 * c + 2, :], lambda c: wout[:, 2 * c:2 * c + 2, :], 4, 2048, as3=2)
            yT = [A.t(f"yT{i}", [8, 512], BF16) for i in range(2)]
            gT = [A.t(f"gT{i}", [16, 512], BF16) for i in range(1)]
            mixT = A.t("mixT", [8, 512], BF16)
            t1 = [A.t(f"t1_{i}", [512], F32) for i in range(2)]
            t2 = [A.t(f"t2_{i}", [512], F32) for i in range(2)]
            junk = A.t("junkE", [D], BF16)
            st2 = A.t("st2", [4], F32)
            h2 = [A.t(f"h2_{i}", [D], BF16) for i in range(2)]
            h2st = [A.t(f"h2st{i}", [8, 512], BF16) for i in range(1)]
            pU = [bank(b, 512, F32, "pU") for b in range(4)]
            pD = [bank(b, 512, F32, "pD") for b in (4, 5)]
            pT2 = [bank(b, 1024, BF16, "pT2") for b in (6, 7)]
            for s in range(4):
                y_t, g_t, hst = yT[s % 2], gT[0], h2st[0]
                if s == 0:
                    fw.dma("sp", DMA(y_t[:], yT_scr[:, :, 0:512]), "yTl0", reads=[r_yscr], writes=[y_t])
                fw.dma("sp", DMA(g_t[:, 0:8, :], gT_scr[:, 0:8, s * 512:(s + 1) * 512]), "gTl", reads=[r_gscr], writes=[g_t])
                fw.dma("pool", DMA(g_t[:, 8:16, :], gT_scr[:, 8:16, s * 512:(s + 1) * 512]), "gTl2", reads=[r_gscr], writes=[g_t])
                if s + 1 < 4:
                    fw.dma("sp", DMA(yT[(s + 1) % 2][:], yT_scr[:, :, (s + 1) * 512:(s + 2) * 512]), f"yTl{(s + 1) % 2}", reads=[r_yscr], writes=[yT[(s + 1) % 2]])
                for cc in range(8):
                    pa, pb = pU[(2 * cc) % 4], pU[(2 * cc + 1) % 4]
                    fw.group("pe", [MM(pa[:], woa_b[:, fc, cc * 128:(cc + 1) * 128], y_t[:, fc, :], start=(fc == 0), stop=(fc == 5)) for fc in range(6)],
                             reads=[woa_b, y_t], writes=[pa])
                    fw.group("pe", [MM(pb[:], wob_b[:, fc, cc * 128:(cc + 1) * 128], y_t[:, 6 + fc, :], start=(fc == 0), stop=(fc == 1)) for fc in range(2)],
                             reads=[wob_b, y_t], writes=[pb])
                    ta, tb = t1[cc % 2], t2[cc % 2]
                    fw.op("dve", TT(ta[:], pa[:], g_t[:, cc, :], ALU.mult), reads=[pa, g_t], writes=[ta])
                    fw.op("dve", TT(tb[:], pb[:], g_t[:, 8 + cc, :], ALU.mult), reads=[pb, g_t], writes=[tb])
                    fw.op("pool", TT(mixT[:, cc, :], ta[:], tb[:], ALU.add), reads=[ta, tb], writes=[mixT])
                tr_pending = []
                for sub in range(4):
                    ti = 4 * s + sub
                    for cb in range(2):
                        pd = pD[cb]
                        fw.group("pe", [MM(pd[:], mixT[:, fc, sub * 128:(sub + 1) * 128], wout_b[:, fc, cb * 512:(cb + 1) * 512],
                                           start=(fc == 0), stop=(fc == 7)) for fc in range(8)], reads=[mixT, wout_b], writes=[pd])
                        fw.op("dve", TT(xm[:, ti, cb * 512:(cb + 1) * 512], pd[:], xm[:, ti, cb * 512:(cb + 1) * 512], ALU.add),
                              reads=[pd, xr[ti]], writes=[xr[ti]])
                    fw.op("act", ACTF(junk[:], xm[:, ti, :], AF.Square, accum=st2[:, 0:1]), reads=[xr[ti]], writes=[junk, st2])
                    fw.op("act", ACTF(st2[:, 1:2], st2[:, 0:1], AF.Sqrt, scale=1.0 / D, bias=epsb[:, 0:1]), reads=[st2, epsb], writes=[st2])
                    fw.op("dve", RECIP(st2[:, 2:3], st2[:, 1:2]), reads=[st2], writes=[st2])
                    hh = h2[sub % 2]
                    fw.op("dve", TS(hh[:], xm[:, ti, :], st2[:, 2:3], ALU.mult), reads=[xr[ti], st2], writes=[hh])
                    def tr_part(sub=sub, hh=hh):
                        pt = pT2[sub % 2]
                        fw.group("pe", [TRN(pt[:, kc * 128:(kc + 1) * 128], hh[:, kc * 128:(kc + 1) * 128], idb[:]) for kc in range(8)],
                                 reads=[hh, idb], writes=[pt])
                        fw.op("act", ACP(hst[:, :, sub * 128:(sub + 1) * 128], pt[:].rearrange("p (a b) -> p a b", a=8)), reads=[pt], writes=[hst])
                    if tr_pending:
                        tr_pending.pop(0)()
                    tr_pending.append(tr_part)
                while tr_pending:
                    tr_pending.pop(0)()
                fw.dma("sp", DMA(h2T_scr[:, :, s * 512:(s + 1) * 512], hst[:]), "h2stl", reads=[hst], writes=[r_h2scr])
            fw.barrier()
            A.top = e1_mark
            g2t = A.t("g2t", [8], F32)
            fw.dma("sp", DMA(g2t[:], g2), "g2t", writes=[g2t])
            wupq = [A.t(f"wupq{i}", [8, 1024], BF16) for i in range(2)]
            wdnq = [A.t(f"wdnq{i}", [8, 1024], BF16) for i in range(2)]
            h2T = [A.t(f"h2T{i}", [8, 512], BF16) for i in range(2)]
            aT = [A.t(f"aT{i}", [8, 512], BF16) for i in range(2)]
            rl = [A.t(f"rl{i}", [512], F32) for i in range(2)]
            pUp = [bank(b, 512, F32, "pUp") for b in range(4)]
            pDn = [bank(b, 512, F32, "pDn") for b in (4, 5, 6, 7)]
            n = 0
            def load_quarter(fq):
                wu, wd = wupq[fq % 2], wdnq[fq % 2]
                for kc in range(8):
                    i_ = stage_ctr[0]
                    st = stage[i_ % 4]
                    key = f"wstage{i_ % 4}" if fq == 0 else f"pws{i_ % 4}"
                    stage_ctr[0] += 1
                    qn = "sp" if fq == 0 else "pool"
                    fw.dma(qn, DMA(st[:, 0:1024], wup[:, kc, fq * 1024:(fq + 1) * 1024]), key, writes=[st])
                    if fq > 0:
                        fw.op("pool", TT(wu[:, kc, :], st[:, 0:1024], bc(g2t[:, kc:kc + 1], [128, 1024]), ALU.mult), reads=[st, g2t], writes=[wu])
                    elif kc % 2:
                        fw.op("act", ACTF(wu[:, kc, :], st[:, 0:1024], AF.Copy, scale=g2t[:, kc:kc + 1]), reads=[st, g2t], writes=[wu])
                    else:
                        fw.op("dve", TS(wu[:, kc, :], st[:, 0:1024], g2t[:, kc:kc + 1], ALU.mult), reads=[st, g2t], writes=[wu])
                for c in range(4):
                    i_ = stage_ctr[0]
                    st = stage[i_ % 4]
                    key = f"wstage{i_ % 4}" if fq == 0 else f"pws{i_ % 4}"
                    stage_ctr[0] += 1
                    qn = "sp" if fq == 0 else "pool"
                    sv = st[:, 0:2048].rearrange("p (a b) -> p a b", a=2)
                    fw.dma(qn, DMA(sv, wdown[:, 8 * fq + 2 * c:8 * fq + 2 * c + 2, :]), key, writes=[st])
                    if fq > 0:
                        fw.op("pool", TC(wd[:, 2 * c:2 * c + 2, :], sv), reads=[st], writes=[wd])
                    else:
                        fw.op("act" if c % 2 else "dve", (ACP if c % 2 else TC)(wd[:, 2 * c:2 * c + 2, :], sv), reads=[st], writes=[wd])
            load_quarter(0)
            pend_down = []
            for fq in range(4):
                wu, wd = wupq[fq % 2], wdnq[fq % 2]
                for s in range(4):
                    h_t = h2T[n % 2]
                    a_t = aT[n % 2]
                    fw.dma("sp", DMA(h_t[:], h2T_scr[:, :, s * 512:(s + 1) * 512]), f"h2Tl{n % 2}", reads=[r_h2scr], writes=[h_t])
                    n += 1
                    for fcb in range(8):
                        pu = pUp[fcb % 4]
                        fw.group("pe", [MM(pu[:], wu[:, kc, fcb * 128:(fcb + 1) * 128], h_t[:, kc, :], start=(kc == 0), stop=(kc == 7)) for kc in range(8)],
                                 reads=[wu, h_t], writes=[pu])
                        r = rl[fcb % 2]
                        fw.op("act", ACTF(r[:], pu[:], AF.Relu), reads=[pu], writes=[r])
                        fw.op("dve", TT(a_t[:, fcb, :], r[:], r[:], ALU.mult), reads=[r], writes=[a_t])

                    def down(fq=fq, s=s, a_t=a_t, wd=wd):
                        for sub in range(4):
                            ti = 4 * s + sub
                            for cb in range(2):
                                pd = pDn[(2 * sub + cb) % 4]
                                fw.group("pe", [MM(pd[:], a_t[:, fc, sub * 128:(sub + 1) * 128], wd[:, fc, cb * 512:(cb + 1) * 512],
                                                   start=(fc == 0), stop=(fc == 7)) for fc in range(8)], reads=[a_t, wd], writes=[pd])
                                fw.op("dve", TT(xm[:, ti, cb * 512:(cb + 1) * 512], pd[:], xm[:, ti, cb * 512:(cb + 1) * 512], ALU.add),
                                      reads=[pd, xr[ti]], writes=[xr[ti]])
                            if fq == 3:
                                fw.dma("sp", DMA(out[ti * 128:(ti + 1) * 128, :], xm[:, ti, :]), "outst", reads=[xr[ti]], writes=[r_out])
                    if pend_down:
                        pend_down.pop(0)()
                    if s == 0 and fq < 3:
                        load_quarter(fq + 1)
                    pend_down.append(down)
            while pend_down:
                pend_down.pop(0)()
            fw.barrier()

        epsb = A.t("epsb", [1], F32)
        fw.op("pool", lambda e: e.memset(epsb[:], EPS), writes=[epsb])
        base_mark = A.top
        r_Vscr, r_Kscr, r_Qscr, r_gscr = Res("Vscr"), Res("Kscr"), Res("Qscr"), Res("gscr")
        r_yscr, r_h2scr, r_out = Res("yscr"), Res("h2scr"), Res("out")

        order = ["A", "B", "C", "D1", "D2", "all"]
        lvl = order.index(upto)
        fw.dma("sp", DMA(V_scr[128:129, :, 1170:1554], bass.AP(expand.tensor, 64, [[0, 1], [0, NT], [1, 384]])), "vpad", writes=[r_Vscr])
        proj_phase("A")
        if lvl >= 1:
            proj_phase("B")
        if debug and lvl >= 1:
            fw.dma("sp", DMA(dbg_gn, gn[:]), "dbggn", reads=[gn], writes=[r_out])
        if lvl >= 2:
            A.top = base_mark
            kcT = A.t("kcT", [3, 256], BF16)
            vc1 = A.t("vc1", [2, 3, 65], BF16)
            c_mark = A.top
            cmp_phase(kcT, vc1)
            if debug:
                fw.dma("sp", DMA(dbg_kc, kcT[:]), "dbgkc", reads=[kcT], writes=[r_out])
                fw.dma("sp", DMA(dbg_vc, vc1[:]), "dbgvc", reads=[vc1], writes=[r_out])
                fw.barrier()
        if lvl >= 3:
            A.top = c_mark
            nsa_phase(kcT, vc1)
        if lvl >= 4:
            A.top = base_mark
            dil_phase()
        if lvl >= 5:
            A.top = base_mark
            e_phase()

        fw.barrier()
        fw.replay()
    return nc


def _rope_tables(pos):
    half = 32
    inv_freq = np.power(np.float32(10000.0), -np.arange(half, dtype=np.float32) / np.float32(half)).astype(np.float32)
    ang = pos.astype(np.float32)[:, None] * inv_freq[None, :]
    return np.cos(ang).astype(np.float32), np.sin(ang).astype(np.float32)


def _tok_major(a, ntile):
    return np.ascontiguousarray(a.reshape(ntile, 128, -1).transpose(1, 0, 2))


def _pmajor(w, nchunk):
    return np.ascontiguousarray(w.reshape(nchunk, 128, -1).transpose(1, 0, 2))


def _toeplitz(L, delta, fn):
    k = np.arange(128)[:, None]
    j = np.arange(L)[None, :]
    d = delta + j - 896 - k
    return fn(d)


def host_prep(inputs):
    bf = ml_dtypes.bfloat16
    x = np.asarray(inputs["x"], np.float32)
    w_in = np.asarray(inputs["w_in"], np.float32)[0]
    sizes = (768, 192, 192, 192, 192, 192, 192, 36, 768, 768, 768, 1024, 1024)
    offs = np.concatenate([[0], np.cumsum(sizes)])
    seg = {n: w_in[:, offs[i]:offs[i + 1]] for i, n in enumerate(
        ["q_a", "k_c", "v_c", "k_s", "v_s", "k_w", "v_w", "g_nsa", "q_b", "k_b", "v_b", "g_ma", "g_mb"])}
    wkv = np.concatenate([seg["k_s"], seg["k_w"], seg["k_b"], seg["v_s"], seg["v_w"], seg["v_b"], seg["k_c"], seg["v_c"]], axis=1)
    wq = np.concatenate([seg["q_a"], seg["q_b"], seg["g_nsa"], seg["g_ma"], seg["g_mb"]], axis=1)
    g = lambda n: np.asarray(inputs[n], np.float32)[0]
    common = {
        "wkv": _pmajor(wkv, 8), "wq": _pmajor(wq, 8),
        "g1": np.ascontiguousarray(g("norm1_g").reshape(8, 128).T), "g2": np.ascontiguousarray(g("norm2_g").reshape(8, 128).T),
        "gcolK": np.concatenate([np.tile(g("k_norm_slc"), 3), np.tile(g("k_norm_win"), 3), np.tile(g("k_norm_b"), 12)])[None, :].astype(np.float32),
        "gcolQ": np.concatenate([np.tile(g("q_norm_a"), 12), np.tile(g("q_norm_b"), 12)])[None, :].astype(np.float32),
        "gcmp": g("k_norm_cmp")[None, :].astype(np.float32),
        "ident": np.eye(128, dtype=np.float32),
        "expand": (np.arange(S)[None, :] // 64 == np.arange(64)[:, None]).astype(bf),
        "woa": _pmajor(g("w_o_a"), 6), "wob": _pmajor(g("w_o_b"), 2), "wout": _pmajor(g("w_out"), 8),
        "wup": _pmajor(g("w_up"), 8), "wdown": _pmajor(g("w_down"), 32),
        "w1k": _pmajor(g("cmp_k_w1"), 16), "w2k": _pmajor(g("cmp_k_w2"), 2),
        "w1v": _pmajor(g("cmp_v_w1"), 16), "w2v": _pmajor(g("cmp_v_w2"), 2),
        "posk": np.ascontiguousarray(g("cmp_k_pos").reshape(16, 128).T), "posv": np.ascontiguousarray(g("cmp_v_pos").reshape(16, 128).T),
    }
    cos, sin = _rope_tables(np.arange(S))
    common["cs_all"] = _tok_major(cos, NT)
    common["sn_all"] = _tok_major(sin, NT)
    c_end = np.arange(256) * 16 + 31
    cc, sc = _rope_tables(c_end)
    common["cs_cmp"] = _tok_major(cc, 2)
    common["sn_cmp"] = _tok_major(sc, 2)
    cs_ = np.arange(256)[:, None] * 16
    ss_ = np.arange(64)[None, :] * 64
    ov = np.clip(np.minimum(cs_ + 32, ss_ + 64) - np.maximum(cs_, ss_), 0, None).astype(np.float32) / 32.0
    ov1 = np.concatenate([ov, np.ones((256, 1), np.float32)], axis=1)
    ov1[255] = 0.0
    common["ov1"] = _tok_major(ov1, 2).astype(bf)

    in_maps = []
    for c in range(8):
        b, half = c // 2, c % 2
        qts = QT[half]
        own_idx = np.concatenate([np.arange(q * 512, (q + 1) * 512) for q in qts])
        m = dict(common)
        m["x_all"] = np.ascontiguousarray(x[b])
        m["x_own"] = np.ascontiguousarray(x[b][own_idx])
        m["cs_own"] = _tok_major(cos[own_idx], NOWN)
        m["sn_own"] = _tok_major(sin[own_idx], NOWN)
        cm = (c_end[:, None] <= own_idx[None, :]) & (np.arange(256)[:, None] < 255)
        cm = cm.reshape(2, 128, 4, 512).transpose(1, 2, 0, 3)
        m["cmask"] = np.ascontiguousarray(cm).astype(bf)
        t = own_idx
        cur = t // 64
        blk = np.arange(64)[None, :]
        forced = (blk == 0) | (blk == cur[:, None]) | (blk == cur[:, None] - 1)
        elig = blk <= cur[:, None]
        sA = (elig & ~forced).astype(np.float32)
        sB = np.where(forced, np.float32(1e30), np.where(elig, np.float32(0.0), np.float32(-1e30))).astype(np.float32)
        m["selA"] = np.ascontiguousarray(sA.reshape(4, 4, 128, 64).transpose(2, 0, 1, 3))
        m["selB"] = np.ascontiguousarray(sB.reshape(4, 4, 128, 64).transpose(2, 0, 1, 3))
        def tabs(L, fn):
            out = np.zeros((128, 2, L), np.float32)
            for p in range(2):
                delta = 512 * (qts[p] - 2 * p)
                out[:, p, :] = _toeplitz(L, delta, fn)
            return out.astype(bf)
        m["tbc"] = tabs(L_C, lambda d: d >= 0)
        m["tbw"] = tabs(L_W, lambda d: (d >= 0) & (d < 512))
        m["tbd0"] = tabs(L_D0, lambda d: (d >= 0) & (d <= 128))
        m["tbd1"] = tabs(L_D1, lambda d: (d >= 0) & (d <= 512) & (d % 4 == 0))
        m["tbd2"] = tabs(L_D2, lambda d: (d >= 0) & (d <= 2048) & (d % 16 == 0))
        in_maps.append(m)
    return in_maps


def kernel(**inputs):
    in_maps = host_prep(inputs)
    nc = build()
    res = run_bass_kernel_spmd(nc, in_maps, core_ids=list(range(8)))
    out = np.zeros((4, S, D), np.float32)
    for c in range(8):
        b, half = c // 2, c % 2
        o = np.asarray(res.results[c]["out"], np.float32)
        for s_, q in enumerate(QT[half]):
            out[b, q * 512:(q + 1) * 512] = o[s_ * 512:(s_ + 1) * 512]
    return out
```
